# Optimizing a Trainium2 kernel written in Bass

```python
import jax, jax.numpy as jnp
from jax import lax
import numpy as np

D_MODEL = 2048
BATCH = 4
SEQ = 4096
DEPTH = 4

GRID_W = 64
MIX_WIDTH = D_MODEL
CONV_WIDTH = MIX_WIDTH // 2
NA_HEADS = 16
NA_HEAD_DIM = (MIX_WIDTH - CONV_WIDTH) // NA_HEADS
NA_WIDTH = NA_HEADS * NA_HEAD_DIM
CONV_KERNEL = 31
WIN_ROWS_MAX = 8
WIN_COLS = 16
D_FF = 4 * D_MODEL
IN_COLS = 2 * CONV_WIDTH + 3 * NA_WIDTH
RMS_EPS = 1e-6
LN_EPS = 1e-5
NEG_INF = -1e30

kernel_name = "hybrid_conv_natten_encoder"


def rms_norm(x, g):
    xf = x.astype(jnp.float32)
    y = xf * lax.rsqrt(jnp.mean(xf * xf, axis=-1, keepdims=True) + RMS_EPS)
    return (y * g.astype(jnp.float32)).astype(x.dtype)


def layer_norm(x, g, b):
    xf = x.astype(jnp.float32)
    mu = jnp.mean(xf, axis=-1, keepdims=True)
    xc = xf - mu
    var = jnp.mean(xc * xc, axis=-1, keepdims=True)
    y = xc * lax.rsqrt(var + LN_EPS) * g.astype(jnp.float32) + b.astype(jnp.float32)
    return y.astype(x.dtype)


def conformer_conv_group(a, gate, w_dw, b_dw, ln_g, ln_b):
    u = a * jax.nn.sigmoid(gate)
    u = lax.conv_general_dilated(
        u, w_dw[:, None, :].astype(u.dtype),
        window_strides=(1,),
        padding=[(CONV_KERNEL // 2, CONV_KERNEL // 2)],
        dimension_numbers=("NWC", "WIO", "NWC"),
        feature_group_count=CONV_WIDTH,
    ) + b_dw.astype(u.dtype)
    return jax.nn.silu(layer_norm(u, ln_g, ln_b))


def neighbourhood_attention_group(q, k, v, rpb):
    B, T, _ = q.shape
    rows = T // GRID_W
    kr = min(WIN_ROWS_MAX, rows)
    r = np.arange(rows)
    c = np.arange(GRID_W)
    row_start = np.clip(r - kr // 2, 0, rows - kr)
    row_idx = row_start[:, None] + np.arange(kr)[None, :]
    col_start = np.clip(c - WIN_COLS // 2, 0, GRID_W - WIN_COLS)
    col_mask = (c[None, :] >= col_start[:, None]) & (c[None, :] < col_start[:, None] + WIN_COLS)
    dr = row_idx - r[:, None] + (WIN_ROWS_MAX - 1)
    dc = np.clip(c[None, :] - c[:, None], -(WIN_COLS - 1), WIN_COLS - 1) + (WIN_COLS - 1)
    bias = rpb[:, dr[:, None, :, None], dc[None, :, None, :]].astype(jnp.float32)

    def to_grid(t):
        return t.reshape(B, rows, GRID_W, NA_HEADS, NA_HEAD_DIM)

    qg, kg, vg = to_grid(q), to_grid(k), to_grid(v)
    k_rows = kg[:, row_idx]
    v_rows = vg[:, row_idx]
    s = jnp.einsum("brwhd,brkvhd->bhrwkv", qg, k_rows,
                   preferred_element_type=jnp.float32) * (NA_HEAD_DIM ** -0.5)
    s = jnp.where(col_mask[:, None, :], s + bias, NEG_INF)
    p = jax.nn.softmax(s.reshape(B, NA_HEADS, rows, GRID_W, kr * GRID_W), axis=-1)
    p = p.reshape(s.shape).astype(v.dtype)
    o = jnp.einsum("bhrwkv,brkvhd->brwhd", p, v_rows)
    return o.reshape(B, T, NA_WIDTH)


def setup_inputs(seed: int = 0) -> dict:
    key = jax.random.key(seed)
    ks = jax.random.split(key, 16)
    f32 = jnp.float32
    x = jax.random.normal(ks[0], (BATCH, SEQ, D_MODEL), f32)
    w_in = jax.random.normal(ks[1], (DEPTH, D_MODEL, IN_COLS), f32) * D_MODEL ** -0.5
    w_dw = jax.random.normal(ks[2], (DEPTH, CONV_KERNEL, CONV_WIDTH), f32) * CONV_KERNEL ** -0.5
    b_dw = jax.random.normal(ks[3], (DEPTH, CONV_WIDTH), f32) * 0.01
    conv_ln_g = 1.0 + 0.05 * jax.random.normal(ks[4], (DEPTH, CONV_WIDTH), f32)
    conv_ln_b = 0.01 * jax.random.normal(ks[5], (DEPTH, CONV_WIDTH), f32)
    rpb = 0.02 * jax.random.normal(ks[6], (DEPTH, NA_HEADS, 2 * WIN_ROWS_MAX - 1, 2 * WIN_COLS - 1), f32)
    w_out = jax.random.normal(ks[7], (DEPTH, MIX_WIDTH, D_MODEL), f32) * MIX_WIDTH ** -0.5
    w_up = jax.random.normal(ks[8], (DEPTH, D_MODEL, D_FF), f32) * D_MODEL ** -0.5
    w_down = jax.random.normal(ks[9], (DEPTH, D_FF, D_MODEL), f32) * D_FF ** -0.5
    pre_mix_g = 1.0 + 0.05 * jax.random.normal(ks[10], (DEPTH, D_MODEL), f32)
    post_mix_g = 1.0 + 0.05 * jax.random.normal(ks[11], (DEPTH, D_MODEL), f32)
    pre_mlp_g = 1.0 + 0.05 * jax.random.normal(ks[12], (DEPTH, D_MODEL), f32)
    post_mlp_g = 1.0 + 0.05 * jax.random.normal(ks[13], (DEPTH, D_MODEL), f32)
    return {"x": x, "w_in": w_in, "w_dw": w_dw, "b_dw": b_dw, "conv_ln_g": conv_ln_g,
            "conv_ln_b": conv_ln_b, "rpb": rpb, "w_out": w_out, "w_up": w_up, "w_down": w_down,
            "pre_mix_g": pre_mix_g, "post_mix_g": post_mix_g, "pre_mlp_g": pre_mlp_g,
            "post_mlp_g": post_mlp_g}


def reference(x, w_in, w_dw, b_dw, conv_ln_g, conv_ln_b, rpb, w_out, w_up, w_down,
              pre_mix_g, post_mix_g, pre_mlp_g, post_mlp_g):
    splits = [CONV_WIDTH, 2 * CONV_WIDTH, 2 * CONV_WIDTH + NA_WIDTH, 2 * CONV_WIDTH + 2 * NA_WIDTH]
    for l in range(DEPTH):
        h = rms_norm(x, pre_mix_g[l])
        proj = h @ w_in[l]
        a, gate, q, k, v = jnp.split(proj, splits, axis=-1)
        yc = conformer_conv_group(a, gate, w_dw[l], b_dw[l], conv_ln_g[l], conv_ln_b[l])
        ya = neighbourhood_attention_group(q, k, v, rpb[l])
        mix = jnp.concatenate([yc, ya], axis=-1) @ w_out[l]
        x = x + rms_norm(mix, post_mix_g[l])
        h = rms_norm(x, pre_mlp_g[l])
        f = jnp.square(jax.nn.relu(h @ w_up[l])) @ w_down[l]
        x = x + rms_norm(f, post_mlp_g[l])
    return x
```

```python
import numpy as np
from contextlib import ExitStack
import concourse.bass as bass
import concourse.mybir as mybir
from concourse.bass_utils import run_bass_kernel_spmd

F32 = mybir.dt.float32
BF16 = mybir.dt.bfloat16
I32 = mybir.dt.int32
AF = mybir.ActivationFunctionType
ALU = mybir.AluOpType

DEPTH = 4
DM = 2048
NCH = 16
SEQ = 4096
GW = 64
CW = 1024
NHEAD = 16
HD = 64
KCONV = 31
DFF = 8192
NBT = 13
RMS_EPS = 1e-6
LN_EPS = 1e-5
NEG = -30000.0
NS = 3
NTILE = 46


class Sem:
    def __init__(self, h):
        self.h = h
        self.count = 0


class Buf:
    def __init__(self, name):
        self.name = name
        self.writers = {}
        self.readers = {}
        self.war = {}
        self.aliases = []


def alias(*bufs):
    for a in bufs:
        for b in bufs:
            if a is not b and b not in a.aliases:
                a.aliases.append(b)


class Prog:
    def __init__(self, nc, sems):
        self.nc = nc
        self.eng = {}
        for name in ("pe", "act", "dve", "pool", "sync"):
            self.eng[name] = dict(ops=[], waited={}, sem=sems[name])

    def op(self, eng, fn, reads=(), writes=(), deps=(), dsem=None):
        E = self.eng[eng]
        rset = set(id(b) for b in reads)
        wset = set(id(b) for b in writes)
        hs = []
        for b in reads:
            hs.extend(b.writers.items())
        for b in writes:
            if b.readers or id(b) in rset:
                b.war = dict(b.readers)
                for k, v in b.writers.items():
                    b.war[k] = max(b.war.get(k, 0), v)
                b.readers = {}
                b.writers = {}
            for bb in b.aliases:
                for d in (bb.readers, bb.writers):
                    for k, v in d.items():
                        b.war[k] = max(b.war.get(k, 0), v)
            hs.extend(b.war.items())
        hs.extend(deps)
        need = {}
        for sem, val in hs:
            if sem is E["sem"] and eng == "pe":
                continue
            need[sem] = max(need.get(sem, 0), val)
        waits = []
        em = E.setdefault("emitted", {})
        for sem, val in need.items():
            if em.get(sem, 0) < val:
                em[sem] = val
                waits.append((sem, val))
        if dsem is not None:
            dsem.count += 16
            h = (dsem, dsem.count)
            inc = (dsem, 16)
        else:
            E["sem"].count += 1
            h = (E["sem"], E["sem"].count)
            inc = (E["sem"], 1)
        E["ops"].append((waits, fn, inc))
        for b in reads:
            if id(b) not in wset:
                b.readers[h[0]] = max(b.readers.get(h[0], 0), h[1])
        for b in writes:
            b.writers[h[0]] = max(b.writers.get(h[0], 0), h[1])
        return h

    def wait_only(self, eng, deps):
        E = self.eng[eng]
        waits = []
        for sem, val in deps:
            if E.setdefault("emitted", {}).get(sem, 0) < val:
                E["emitted"][sem] = val
                waits.append((sem, val))
        E["ops"].append((waits, None, None))

    def replay(self, name, e):
        for waits, fn, inc in self.eng[name]["ops"]:
            for sem, val in waits:
                e.wait_ge(sem.h, val)
            if fn is not None:
                ins = fn(e)
                ins.then_inc(inc[0].h, inc[1])


def f_dma(out, in_):
    return lambda e: e.dma_start(out=out, in_=in_)


def f_act(out, in_, func, **kw):
    return lambda e: e.activation(out=out, in_=in_, func=func, **kw)


def f_ts(out, in0, s1, s2, op0, op1=None):
    if op1 is None:
        return lambda e: e.tensor_scalar(out=out, in0=in0, scalar1=s1, scalar2=None, op0=op0)
    return lambda e: e.tensor_scalar(out=out, in0=in0, scalar1=s1, scalar2=s2, op0=op0, op1=op1)


def f_tt(out, in0, in1, op):
    return lambda e: e.tensor_tensor(out=out, in0=in0, in1=in1, op=op)


def f_stt(out, in0, scalar, in1, op0, op1):
    return lambda e: e.scalar_tensor_tensor(out=out, in0=in0, scalar=scalar, in1=in1, op0=op0, op1=op1)


def f_copy(out, in_):
    return lambda e: e.tensor_copy(out=out, in_=in_)


def f_recip(out, in_):
    return lambda e: e.reciprocal(out=out, in_=in_)


def f_memset(ap, val):
    return lambda e: e.memset(ap, val)


def f_mm(items):
    def fn(e):
        ins = None
        for (out, lhsT, rhs, start, stop) in items:
            ins = e.matmul(out, lhsT=lhsT, rhs=rhs, start=start, stop=stop)
        return ins
    return fn


def f_tr(out, in_, ident):
    return lambda e: e.transpose(out, in_, ident)


def blocks_of(T):
    res = []
    t = 0
    while t < T:
        n = 512 if T - t >= 512 else T - t
        res.append((t, n))
        t += n
    return res


def build_program(NL=DEPTH, dump=(), stop=None):
    nc = bass.Bass("TRN2", target_bir_lowering=False)
    RIN = [32 + 4 * (NL - l) for l in range(NL)]
    ROUT = [32 + 4 * (NL - 1 - l) for l in range(NL)]
    TIN = [r * GW for r in RIN]
    TOUT = [r * GW for r in ROUT]
    T0 = TIN[0]
    T1 = TOUT[0]

    def dram(name, shape, dt, kind="Internal"):
        if name in dump:
            kind = "ExternalOutput"
        return nc.dram_tensor(name, shape, dt, kind=kind).ap()

    xin = dram("xin", [NCH, 128, T0], F32, "ExternalInput")
    w_in = dram("w_in", [DEPTH, DM, 5120], F32, "ExternalInput")
    w_out = dram("w_out", [DEPTH, DM, DM], F32, "ExternalInput")
    w_up = dram("w_up", [DEPTH, DM, DFF], F32, "ExternalInput")
    w_down = dram("w_down", [DEPTH, DFF, DM], F32, "ExternalInput")
    gpack_d = dram("gpack", [128, DEPTH * 4 * NCH], F32, "ExternalInput")
    cpack_d = dram("cpack", [128, DEPTH * 8 * 34], F32, "ExternalInput")
    bias_d = dram("biasg", [DEPTH, NHEAD, 128, NBT, 128], F32, "ExternalInput")
    yout = dram("yout", [NCH, 128, TOUT[NL - 1]], F32, "ExternalOutput")
    xs = [dram("xs0", [NCH, 128, T1], F32), dram("xs1", [NCH, 128, T1], F32)]
    wsc = dram("wsc", [2, NTILE, 128, 8192], BF16)
    uT = dram("uT", [8, 128, T0 + 32], BF16)
    qT = dram("qT", [8, 128, T0], BF16)
    kT = dram("kT", [8, 128, T0], BF16)
    vS = dram("vS", [T0 // 128, 128, NHEAD * 65], BF16)
    mixT = dram("mixT", [NCH, 128, T1], BF16)

    es = ExitStack()
    with es:
        def sb(name, shape, dt):
            return es.enter_context(nc.sbuf_tensor(name, shape, dt))

        def newsem(name):
            return Sem(es.enter_context(nc.semaphore(name)))

        RW = sb("RW", [128, NS * 4096], F32)
        RX = sb("RX", [128, 8192], F32)
        RH = sb("RH", [128, 8192], F32)
        RS = sb("RS", [128, 8320], F32)
        RD = sb("RD", [128, 4096], F32)
        sqb = sb("sqb", [128, 2, 4, 512], BF16)
        rs_t = sb("rs_t", [128, 4, 512], F32)
        sg_t = sb("sg_t", [128, 2, 512], F32)
        ident = sb("ident", [128, 128], BF16)
        ones = sb("ones", [128, 128], BF16)
        iot = sb("iot", [128, 128], I32)
        zer = sb("zer", [128, 128], BF16)
        gpack = sb("gpack_s", [128, DEPTH, 4, NCH], F32)
        cpack = sb("cpack_s", [128, DEPTH, 8, 34], F32)
        psF = es.enter_context(nc.psum_tensor("psF", [128, 7 * 512], F32))
        psB = es.enter_context(nc.psum_tensor("psB", [128, 1024], BF16))

        def bfview(reg, off_w, n_w):
            return reg[:, off_w:off_w + n_w].bitcast(BF16)

        wslot = [bfview(RW, s * 4096, 4096) for s in range(NS)]
        B_w = [Buf(f"w{s}") for s in range(NS)]
        S_w = [newsem(f"sw{s}") for s in range(NS)]
        xblk = RX[:, :].rearrange("p (c n) -> p c n", c=NCH)
        B_x = Buf("x")
        acc = RX[:, 0:4096].rearrange("p (c n) -> p c n", c=8)
        B_acc = [Buf(f"acc{c}") for c in range(8)]
        ublk = bfview(RX, 4096, 2176)[:, 0:8 * 542].rearrange("p (c n) -> p c n", c=8)
        B_u = Buf("ublk")
        S_u = newsem("su")
        tmpc = RX[:, 6272:6272 + 1024].rearrange("p (c n) -> p c n", c=2)
        B_tmpc = [Buf("tmpc0"), Buf("tmpc1")]
        for a in B_acc + B_tmpc + [B_u]:
            a.aliases = [B_x]
        B_x.aliases = B_acc + B_tmpc + [B_u]
        S_x = newsem("sx")
        hT = [bfview(RH, i * 4096, 4096).rearrange("p (c n) -> p c n", c=NCH) for i in range(2)]
        B_hT = [Buf("hT0"), Buf("hT1")]
        mixb = hT[0]
        B_mix = B_hT[0]
        S_mix = newsem("smix")
        h2 = hT[1]
        B_h2 = B_hT[1]
        RHb = RH[:, :].bitcast(BF16)
        o = 0
        kblk = []
        for i in range(2):
            kblk.append(RHb[:, o:o + 1024]); o += 1024
        vblk = []
        for i in range(2):
            vblk.append(RHb[:, o:o + 1040].rearrange("p (j h d) -> p j h d", j=8, h=2)); o += 1056
        qA = []
        qB = []
        for i in range(2):
            qA.append(RHb[:, o:o + 512]); o += 512
            qB.append(RHb[:, o:o + 512]); o += 512
        biasb = []
        for i in range(2):
            biasb.append(RHb[:, o:o + 2 * NBT * 128].rearrange("p (h t k) -> p h t k", h=2, t=NBT)); o += 2 * NBT * 128
        pT = []
        for i in range(2):
            pT.append(RHb[:, o:o + 640]); o += 640
        onb = []
        for i in range(2):
            onb.append(RHb[:, o:o + 128]); o += 128
        assert o <= 16384, o
        B_k = [Buf("k0"), Buf("k1")]
        B_v = [Buf("v0"), Buf("v1")]
        B_qA = [Buf("qA0"), Buf("qA1")]
        B_qB = [Buf("qB0"), Buf("qB1")]
        B_bias = [Buf("bias0"), Buf("bias1")]
        B_pT = [Buf("pT0"), Buf("pT1")]
        B_on = [Buf("on0"), Buf("on1")]
        S_k = [newsem("sk0"), newsem("sk1")]
        S_v = [newsem("sv0"), newsem("sv1")]
        S_qA = [newsem("sqa0"), newsem("sqa1")]
        S_qB = [newsem("sqb0"), newsem("sqb1")]
        S_bias = [newsem("sbi0"), newsem("sbi1")]
        attn_bufs = B_k + B_v + B_qA + B_qB + B_bias + B_pT + B_on
        for a in attn_bufs:
            a.aliases = list(B_hT)
        for a in B_hT:
            a.aliases = list(attn_bufs)
        RSb = RS[:, :].bitcast(BF16)
        ustage = RSb[:, 0:4096].rearrange("p (c n) -> p c n", c=8)
        qstage = RSb[:, 4096:8192].rearrange("p (c n) -> p c n", c=8)
        kstage = RSb[:, 8192:12288].rearrange("p (c n) -> p c n", c=8)
        vstage = RSb[:, 12288:12288 + 4160].rearrange("p (t h d) -> p t h d", t=4, h=NHEAD)
        mf = RS[:, 0:8192].rearrange("p (c n) -> p c n", c=NCH)
        B_us, B_qs, B_ks, B_vs = Buf("ustage"), Buf("qstage"), Buf("kstage"), Buf("vstage")
        S_us, S_qs, S_ks, S_vs = newsem("sus"), newsem("sqs"), newsem("sks"), newsem("svs")
        B_mf = Buf("mf")
        for a in (B_us, B_qs, B_ks, B_vs):
            a.aliases = [B_mf]
        B_mf.aliases = [B_us, B_qs, B_ks, B_vs]
        S_xst = newsem("sxst")
        hid = RD[:, :].bitcast(BF16).rearrange("p (c n) -> p c n", c=16)
        B_hid = Buf("hid")
        B_sq = [Buf("sq0"), Buf("sq1")]
        B_rs = [Buf(f"rs{i}") for i in range(4)]
        B_sg = [Buf("sg0"), Buf("sg1")]
        B_const = Buf("const")
        B_par = Buf("params")
        S_par = newsem("spar")
        S_misc = newsem("smisc")
        def bank(b):
            return psF[:, b * 512:(b + 1) * 512]
        B_bank = [Buf(f"bank{b}") for b in range(7)]
        B_bT = Buf("bankT")
        rot = [0]

        def next_bank():
            b = rot[0]
            rot[0] = (b + 1) % 7
            return b

        sems = {n: newsem("e_" + n) for n in ("pe", "act", "dve", "pool", "sync")}
        P = Prog(nc, sems)
        conv_sems = [newsem(f"cv{i}") for i in range(8)]
        conv_hist = []

        def gran(name, n):
            return [Buf(f"{name}_g{i}") for i in range(n)]
        NG = T0 // 256
        G_x = {"xin": gran("xin", NG), 0: gran("xs0", NG), 1: gran("xs1", NG), "yout": gran("yout", NG)}
        G_u = gran("uT", NG + 1)
        G_q = gran("qT", NG)
        G_k = gran("kT", NG)
        G_v = gran("vS", NG)
        G_myc = gran("myc", NG)
        G_mya = gran("mya", NG)
        B_wsc = [[[Buf(f"wsc{p}_{t}_{h}") for h in range(2)] for t in range(NTILE)] for p in range(2)]

        def gr(G, lo, hi):
            return G[lo // 256:(hi + 255) // 256]

        P.op("pool", lambda e: e.iota(iot[:], pattern=[[1, 128]], base=0, channel_multiplier=-1), writes=[B_const])
        P.op("dve", f_ts(ident[:], iot[:], 0.0, None, ALU.is_equal), reads=[B_const], writes=[B_const])
        P.op("dve", f_memset(ones[:], 1.0), writes=[B_const])
        P.op("dve", f_memset(zer[:], 0.0), writes=[B_const])
        P.op("sync", f_dma(gpack[:].rearrange("p a b c -> p (a b c)"), gpack_d), writes=[B_par], dsem=S_par)
        P.op("sync", f_dma(cpack[:].rearrange("p a b c -> p (a b c)"), cpack_d), writes=[B_par], dsem=S_par)
        for c in range(8):
            P.op("pool", f_dma(uT[c, :, 0:15], zer[:, 0:15]), reads=[B_const], writes=[G_u[0]], dsem=S_misc)

        def tile_src(l, tid, half):
            par = l % 2
            dst = wsc[par, tid].rearrange("p (k n) -> p k n", k=16)
            k0, k1 = half * 8, half * 8 + 8
            res = []
            if tid < 4:
                for (cb, d0) in ((2 * tid * 128, 0), (1024 + 2 * tid * 128, 256)):
                    src = w_in[l, k0 * 128:k1 * 128, cb:cb + 256].rearrange("(k p) n -> p k n", p=128)
                    res.append((dst[:, k0:k1, d0:d0 + 256], src))
            elif tid < 10:
                cb = 2048 + (tid - 4) * 512
                src = w_in[l, k0 * 128:k1 * 128, cb:cb + 512].rearrange("(k p) n -> p k n", p=128)
                res.append((dst[:, k0:k1, :], src))
            elif tid < 14:
                cb = (tid - 10) * 512
                src = w_out[l, k0 * 128:k1 * 128, cb:cb + 512].rearrange("(k p) n -> p k n", p=128)
                res.append((dst[:, k0:k1, :], src))
            else:
                qd, r = divmod(tid - 14, 8)
                if r < 4:
                    cb = qd * 2048 + r * 512
                    src = w_up[l, k0 * 128:k1 * 128, cb:cb + 512].rearrange("(k p) n -> p k n", p=128)
                else:
                    cb = (r - 4) * 512
                    rb = qd * 2048
                    src = w_down[l, rb + k0 * 128:rb + k1 * 128, cb:cb + 512].rearrange("(k p) n -> p k n", p=128)
                res.append((dst[:, k0:k1, :], src))
            return res

        pending_conv = []

        def queue_conversions(l):
            for tid in range(NTILE):
                for half in range(2):
                    pending_conv.append((l, tid, half))

        def pump_conv(k):
            for _ in range(k):
                if not pending_conv:
                    return
                l, tid, half = pending_conv.pop(0)
                for (dst, src) in tile_src(l, tid, half):
                    i = len(conv_hist)
                    cs = conv_sems[i % 8]
                    deps = [conv_hist[i - 8]] if i >= 8 else []
                    h = P.op("pool", f_dma(dst, src), writes=[B_wsc[l % 2][tid][half]], deps=deps, dsem=cs)
                    conv_hist.append(h)

        wtiles = []
        for l in range(NL):
            for _ in blocks_of(TIN[l]):
                wtiles.extend((l, t) for t in range(10))
            for _ in blocks_of(TOUT[l]):
                wtiles.extend((l, t) for t in range(10, NTILE))
        wstate = dict(next_load=0, next_use=0)

        def w_load_one():
            i = wstate["next_load"]
            if i >= len(wtiles):
                return
            l, tid = wtiles[i]
            while any(pc[0] == l and pc[1] == tid for pc in pending_conv):
                pump_conv(1)
            s = i % NS
            P.op("sync", f_dma(wslot[s], wsc[l % 2, tid]), reads=B_wsc[l % 2][tid], writes=[B_w[s]], dsem=S_w[s])
            wstate["next_load"] = i + 1

        def w_next(expect):
            i = wstate["next_use"]
            assert wtiles[i] == expect, (wtiles[i], expect)
            while wstate["next_load"] < min(i + NS, len(wtiles)):
                if wstate["next_load"] >= i + NS - 1 and i >= 1:
                    pass
                w_load_one()
            wstate["next_use"] = i + 1
            pump_conv(1)
            s = i % NS
            return wslot[s].rearrange("p (k n) -> p k n", k=16), B_w[s]

        def rstd_from(bank_b, n, ridx, scale, eps):
            r = rs_t[:, ridx, 0:n]
            P.op("dve", f_ts(r, bank(bank_b)[:, 0:n], scale, eps, ALU.mult, ALU.add), reads=[B_bank[bank_b]], writes=[B_rs[ridx]])
            P.op("act", f_act(r, r, AF.Sqrt), reads=[B_rs[ridx]], writes=[B_rs[ridx]])
            P.op("dve", f_recip(r, r), reads=[B_rs[ridx]], writes=[B_rs[ridx]])

        def sumsq_of(src3, B_src, n, nchunks):
            b = next_bank()
            ng = nchunks // 4
            for g in range(ng):
                sq = sqb[:, g % 2, :, 0:n]
                P.op("act", f_act(sq, src3[:, 4 * g:4 * g + 4, 0:n], AF.Square), reads=[B_src], writes=[B_sq[g % 2]])
                items = [(bank(b)[:, 0:n], ones[:], sqb[:, g % 2, j, 0:n], (g == 0 and j == 0), (g == ng - 1 and j == 3)) for j in range(4)]
                P.op("pe", f_mm(items), reads=[B_sq[g % 2], B_const], writes=[B_bank[b]])
            return b

        def gcol(l, k, c):
            return gpack[:, l, k, c:c + 1]

        def phase1(l):
            xsrc, Gsrc = (xin, G_x["xin"]) if l == 0 else (xs[(l - 1) % 2], G_x[(l - 1) % 2])
            P.op("dve", f_memset(vstage[:, :, :, 64:65], 1.0), writes=[B_vs])
            blks = blocks_of(TIN[l])

            def norm_block(bi):
                t0, n = blks[bi]
                par = bi % 2
                P.op("sync", f_dma(xblk[:, :, 0:n], xsrc[:, :, t0:t0 + n].rearrange("c p t -> p c t")),
                     reads=gr(Gsrc, t0, t0 + n), writes=[B_x], dsem=S_x)
                b = sumsq_of(xblk, B_x, n, NCH)
                rstd_from(b, n, 0, 1.0 / DM, RMS_EPS)
                for c in range(NCH):
                    P.op("dve", f_stt(hT[par][:, c, 0:n], xblk[:, c, 0:n], gcol(l, 0, c), rs_t[:, 0, 0:n], ALU.mult, ALU.mult),
                         reads=[B_x, B_rs[0], B_par], writes=[B_hT[par]])

            norm_block(0)
            for bi, (t0, n) in enumerate(blks):
                par = bi % 2
                nt = n // 128
                for w in range(10):
                    wt, Bw = w_next((l, w))
                    if w == 5 and bi + 1 < len(blks):
                        norm_block(bi + 1)
                    if w < 8:
                        bs = [next_bank() for _ in range(4)]
                        for cc in range(4):
                            items = [(bank(bs[cc])[:, 0:n], wt[:, kc, cc * 128:(cc + 1) * 128], hT[par][:, kc, 0:n], kc == 0, kc == 15) for kc in range(16)]
                            P.op("pe", f_mm(items), reads=[Bw, B_hT[par]], writes=[B_bank[bs[cc]]])
                        if w < 4:
                            for j in range(2):
                                P.op("act", f_act(sg_t[:, j, 0:n], bank(bs[2 + j])[:, 0:n], AF.Sigmoid), reads=[B_bank[bs[2 + j]]], writes=[B_sg[j]])
                                P.op("dve", f_tt(ustage[:, 2 * w + j, 0:n], bank(bs[j])[:, 0:n], sg_t[:, j, 0:n], ALU.mult),
                                     reads=[B_bank[bs[j]], B_sg[j]], writes=[B_us])
                        else:
                            stg, Bs = (qstage, B_qs) if w < 6 else (kstage, B_ks)
                            sc = 0.125 if w < 6 else 1.0
                            for cc in range(4):
                                ch = (w % 2) * 4 + cc
                                if cc % 2 == 0:
                                    P.op("act", f_act(stg[:, ch, 0:n], bank(bs[cc])[:, 0:n], AF.Copy, scale=sc), reads=[B_bank[bs[cc]]], writes=[Bs])
                                else:
                                    P.op("dve", f_ts(stg[:, ch, 0:n], bank(bs[cc])[:, 0:n], sc, None, ALU.mult), reads=[B_bank[bs[cc]]], writes=[Bs])
                    else:
                        hb = (w - 8) * 8
                        for tt in range(nt):
                            b2 = next_bank()
                            items = [(bank(b2)[:, :], hT[par][:, kc, tt * 128:(tt + 1) * 128], wt[:, kc, :], kc == 0, kc == 15) for kc in range(16)]
                            P.op("pe", f_mm(items), reads=[Bw, B_hT[par]], writes=[B_bank[b2]])
                            src = bank(b2)[:, :].rearrange("p (h d) -> p h d", d=64)
                            dst = vstage[:, tt, hb:hb + 8, 0:64]
                            if tt % 2 == 0:
                                P.op("act", f_act(dst, src, AF.Copy), reads=[B_bank[b2]], writes=[B_vs])
                            else:
                                P.op("dve", f_copy(dst, src), reads=[B_bank[b2]], writes=[B_vs])
                    if w == 3:
                        P.op("pool", f_dma(uT[:, :, 15 + t0:15 + t0 + n].rearrange("c p t -> p c t"), ustage[:, :, 0:n]),
                             reads=[B_us], writes=gr(G_u, t0, t0 + n), dsem=S_us)
                    if w == 5:
                        P.op("pool", f_dma(qT[:, :, t0:t0 + n].rearrange("c p t -> p c t"), qstage[:, :, 0:n]),
                             reads=[B_qs], writes=gr(G_q, t0, t0 + n), dsem=S_qs)
                    if w == 7:
                        P.op("pool", f_dma(kT[:, :, t0:t0 + n].rearrange("c p t -> p c t"), kstage[:, :, 0:n]),
                             reads=[B_ks], writes=gr(G_k, t0, t0 + n), dsem=S_ks)
                    if w == 9:
                        P.op("pool", f_dma(vS[t0 // 128:t0 // 128 + nt].rearrange("t p f -> p t f"),
                                           vstage[:, 0:nt].rearrange("p t h d -> p t (h d)")),
                             reads=[B_vs], writes=gr(G_v, t0, t0 + n), dsem=S_vs)

        def conv_gen(l, t0, n):
            wv = lambda c, j: cpack[:, l, c, j:j + 1]
            P.op("sync", f_dma(ublk[:, :, 0:n + 30], uT[:, :, t0:t0 + n + 30].rearrange("c p t -> p c t")),
                 reads=gr(G_u, t0, t0 + n + 30), writes=[B_u], dsem=S_u)
            yield
            for c0 in range(0, 8, 2):
                for j in range(KCONV):
                    for c in (c0, c0 + 1):
                        if j == 0:
                            P.op("dve", f_ts(acc[:, c, 0:n], ublk[:, c, 0:n], wv(c, 0), cpack[:, l, c, 31:32], ALU.mult, ALU.add),
                                 reads=[B_u, B_par], writes=[B_acc[c]])
                        else:
                            P.op("dve", f_stt(acc[:, c, 0:n], ublk[:, c, j:j + n], wv(c, j), acc[:, c, 0:n], ALU.mult, ALU.add),
                                 reads=[B_u, B_par, B_acc[c]], writes=[B_acc[c]])
                        yield
            for g in range(2):
                cb = sqb[:, 0, :, 0:n]
                P.op("act", f_act(cb, acc[:, 4 * g:4 * g + 4, 0:n], AF.Copy), reads=B_acc[4 * g:4 * g + 4], writes=[B_sq[0]])
                items = [(bank(5)[:, 0:n], ones[:], sqb[:, 0, j, 0:n], (g == 0 and j == 0), (g == 1 and j == 3)) for j in range(4)]
                P.op("pe", f_mm(items), reads=[B_sq[0], B_const], writes=[B_bank[5]])
                sq = sqb[:, 1, :, 0:n]
                P.op("act", f_act(sq, acc[:, 4 * g:4 * g + 4, 0:n], AF.Square), reads=B_acc[4 * g:4 * g + 4], writes=[B_sq[1]])
                items = [(bank(6)[:, 0:n], ones[:], sqb[:, 1, j, 0:n], (g == 0 and j == 0), (g == 1 and j == 3)) for j in range(4)]
                P.op("pe", f_mm(items), reads=[B_sq[1], B_const], writes=[B_bank[6]])
                yield
            mean = rs_t[:, 1, 0:n]
            msq = rs_t[:, 2, 0:n]
            rst = rs_t[:, 3, 0:n]
            P.op("dve", f_ts(mean, bank(5)[:, 0:n], 1.0 / CW, None, ALU.mult), reads=[B_bank[5]], writes=[B_rs[1]])
            P.op("dve", f_tt(msq, mean, mean, ALU.mult), reads=[B_rs[1]], writes=[B_rs[2]])
            P.op("dve", f_stt(rst, bank(6)[:, 0:n], 1.0 / CW, msq, ALU.mult, ALU.subtract), reads=[B_bank[6], B_rs[2]], writes=[B_rs[3]])
            P.op("dve", f_ts(rst, rst, LN_EPS, None, ALU.add), reads=[B_rs[3]], writes=[B_rs[3]])
            P.op("act", f_act(rst, rst, AF.Sqrt), reads=[B_rs[3]], writes=[B_rs[3]])
            P.op("dve", f_recip(rst, rst), reads=[B_rs[3]], writes=[B_rs[3]])
            yield
            for c in range(8):
                tb = c % 2
                P.op("dve", f_tt(tmpc[:, tb, 0:n], acc[:, c, 0:n], mean, ALU.subtract), reads=[B_acc[c], B_rs[1]], writes=[B_tmpc[tb]])
                if c >= 1:
                    cp = c - 1
                    P.op("dve", f_tt(acc[:, cp, 0:n], tmpc[:, cp % 2, 0:n], rst, ALU.mult), reads=[B_tmpc[cp % 2], B_rs[3]], writes=[B_acc[cp]])
                    P.op("act", f_act(ustage[:, cp, 0:n], acc[:, cp, 0:n], AF.Silu, scale=cpack[:, l, cp, 32:33], bias=cpack[:, l, cp, 33:34]),
                         reads=[B_acc[cp], B_par], writes=[B_us])
                yield
            cp = 7
            P.op("dve", f_tt(acc[:, cp, 0:n], tmpc[:, cp % 2, 0:n], rst, ALU.mult), reads=[B_tmpc[cp % 2], B_rs[3]], writes=[B_acc[cp]])
            P.op("act", f_act(ustage[:, cp, 0:n], acc[:, cp, 0:n], AF.Silu, scale=cpack[:, l, cp, 32:33], bias=cpack[:, l, cp, 33:34]),
                 reads=[B_acc[cp], B_par], writes=[B_us])
            P.op("pool", f_dma(mixT[0:8, :, t0:t0 + n].rearrange("c p t -> p c t"), ustage[:, :, 0:n]),
                 reads=[B_us], writes=gr(G_myc, t0, t0 + n), dsem=S_us)
            yield

        def phase2(l):
            for i in range(2):
                P.op("dve", f_memset(qA[i][64:128, :], 0.0), writes=[B_qA[i]])
                P.op("dve", f_memset(qB[i][0:64, :], 0.0), writes=[B_qB[i]])
            it = 0
            for (t0, n) in blocks_of(TOUT[l]):
                cg = conv_gen(l, t0, n)
                nq = n // 128
                i0 = t0 // 128
                jlo = max(0, i0 - 2)
                jhi = max(i0 + nq - 1 + 2, 3)
                nk = jhi - jlo + 1
                edge = (i0 == 0)
                tlo, ntl = (0, NBT) if edge else (8, 5)
                total_conv_steps = 8 * KCONV + 16
                per_step = (total_conv_steps + 8 * nq - 1) // (8 * nq) + 1
                for hp in range(8):
                    par = hp % 2
                    P.op("sync", f_dma(kblk[par][:, 0:nk * 128], kT[hp, :, jlo * 128:(jlo + nk) * 128]),
                         reads=gr(G_k, jlo * 128, (jlo + nk) * 128), writes=[B_k[par]], dsem=S_k[par])
                    P.op("sync", f_dma(vblk[par][:, 0:nk].rearrange("p j h d -> p j (h d)"),
                                       vS[jlo:jlo + nk, :, 2 * hp * 65:(2 * hp + 2) * 65].rearrange("j p f -> p j f")),
                         reads=gr(G_v, jlo * 128, (jlo + nk) * 128), writes=[B_v[par]], dsem=S_v[par])
                    P.op("sync", f_dma(qA[par][0:64, 0:n], qT[hp, 0:64, t0:t0 + n]), reads=gr(G_q, t0, t0 + n), writes=[B_qA[par]], dsem=S_qA[par])
                    P.op("sync", f_dma(qB[par][64:128, 0:n], qT[hp, 64:128, t0:t0 + n]), reads=gr(G_q, t0, t0 + n), writes=[B_qB[par]], dsem=S_qB[par])
                    P.op("pool", f_dma(biasb[par][:, :, 0:ntl, :], bias_d[l, 2 * hp:2 * hp + 2, :, tlo:tlo + ntl, :].rearrange("h q t k -> q h t k")),
                         writes=[B_bias[par]], dsem=S_bias[par])
                    pump_conv(1)
                    for qi in range(nq):
                        i = i0 + qi
                        if i < 2:
                            js = [0, 1, 2, 3]
                            tb = 0 if i == 0 else 4
                        else:
                            js = list(range(i - 2, i + 3))
                            tb = 8
                        tb -= tlo
                        onp = it % 2
                        for head in range(2):
                            sp = head
                            sb0 = 2 * sp
                            qsel, Bq = (qA[par], B_qA[par]) if head == 0 else (qB[par], B_qB[par])
                            items = []
                            for jj, j in enumerate(js):
                                o_ap = psF[:, sb0 * 512 + jj * 128: sb0 * 512 + (jj + 1) * 128]
                                items.append((o_ap, kblk[par][:, (j - jlo) * 128:(j - jlo + 1) * 128], qsel[:, qi * 128:(qi + 1) * 128], True, False))
                                items.append((o_ap, biasb[par][:, head, tb + jj, :], ident[:], False, True))
                            P.op("pe", f_mm(items), reads=[B_k[par], Bq, B_bias[par], B_const], writes=[B_bank[sb0], B_bank[sb0 + 1]])
                            nkk = len(js) * 128
                            P.op("act", f_act(pT[sp][:, 0:min(nkk, 512)], psF[:, sb0 * 512:sb0 * 512 + min(nkk, 512)], AF.Exp),
                                 reads=[B_bank[sb0]], writes=[B_pT[sp]])
                            if nkk > 512:
                                P.op("act", f_act(pT[sp][:, 512:nkk], psF[:, sb0 * 512 + 512:sb0 * 512 + nkk], AF.Exp),
                                     reads=[B_bank[sb0 + 1]], writes=[B_pT[sp]])
                            items = []
                            for jj, j in enumerate(js):
                                items.append((bank(4)[:, head * 65:(head + 1) * 65], pT[sp][:, jj * 128:(jj + 1) * 128], vblk[par][:, j - jlo, head, :],
                                              jj == 0, jj == len(js) - 1))
                            P.op("pe", f_mm(items), reads=[B_pT[sp], B_v[par]], writes=[B_bank[4]])
                        rc = rs_t[:, 0, 0:2]
                        P.op("dve", f_recip(rc, bank(4)[:, 0:130].rearrange("p (h d) -> p h d", h=2)[:, :, 64]), reads=[B_bank[4]], writes=[B_rs[0]])
                        for head in range(2):
                            P.op("dve", f_ts(onb[onp][:, head * 64:(head + 1) * 64], bank(4)[:, head * 65:head * 65 + 64], rs_t[:, 0, head:head + 1], None, ALU.mult),
                                 reads=[B_bank[4], B_rs[0]], writes=[B_on[onp]])
                        P.op("pe", f_tr(psB[:, 0:128], onb[onp][:, :], ident[:]), reads=[B_on[onp], B_const], writes=[B_bT])
                        P.op("act", f_act(qstage[:, hp, qi * 128:(qi + 1) * 128], psB[:, 0:128], AF.Copy), reads=[B_bT], writes=[B_qs])
                        it += 1
                        for _ in range(per_step):
                            next(cg, None)
                for _ in cg:
                    pass
                P.op("pool", f_dma(mixT[8:16, :, t0:t0 + n].rearrange("c p t -> p c t"), qstage[:, :, 0:n]),
                     reads=[B_qs], writes=gr(G_mya, t0, t0 + n), dsem=S_qs)

        def phase3(l):
            xsrc, Gsrc = (xin, G_x["xin"]) if l == 0 else (xs[(l - 1) % 2], G_x[(l - 1) % 2])
            if l == NL - 1:
                xdst, Gdst = yout, G_x["yout"]
            else:
                xdst, Gdst = xs[l % 2], G_x[l % 2]
            blks = blocks_of(TOUT[l])

            def load_mix(bi):
                t0, n = blks[bi]
                P.op("sync", f_dma(mixb[:, :, 0:n], mixT[:, :, t0:t0 + n].rearrange("c p t -> p c t")),
                     reads=gr(G_myc, t0, t0 + n) + gr(G_mya, t0, t0 + n), writes=[B_mix], dsem=S_mix)

            load_mix(0)
            for bi, (t0, n) in enumerate(blks):
                P.op("sync", f_dma(xblk[:, :, 0:n], xsrc[:, :, t0:t0 + n].rearrange("c p t -> p c t")),
                     reads=gr(Gsrc, t0, t0 + n), writes=[B_x], dsem=S_x)
                ssb = next_bank()
                for w in range(4):
                    wt, Bw = w_next((l, 10 + w))
                    for cc in range(4):
                        ch = w * 4 + cc
                        b = next_bank()
                        if b == ssb:
                            b = next_bank()
                        items = [(bank(b)[:, 0:n], wt[:, kc, cc * 128:(cc + 1) * 128], mixb[:, kc, 0:n], kc == 0, kc == 15) for kc in range(16)]
                        P.op("pe", f_mm(items), reads=[Bw, B_mix], writes=[B_bank[b]])
                        P.op("act", f_act(mf[:, ch, 0:n], bank(b)[:, 0:n], AF.Copy), reads=[B_bank[b]], writes=[B_mf])
                        P.op("act", f_act(sqb[:, ch % 2, 0, 0:n], bank(b)[:, 0:n], AF.Square), reads=[B_bank[b]], writes=[B_sq[ch % 2]])
                        P.op("pe", f_mm([(bank(ssb)[:, 0:n], ones[:], sqb[:, ch % 2, 0, 0:n], ch == 0, ch == 15)]),
                             reads=[B_sq[ch % 2], B_const], writes=[B_bank[ssb]])
                if bi + 1 < len(blks):
                    load_mix(bi + 1)
                rstd_from(ssb, n, 0, 1.0 / DM, RMS_EPS)
                for c in range(NCH):
                    P.op("dve", f_stt(mf[:, c, 0:n], mf[:, c, 0:n], gcol(l, 1, c), rs_t[:, 0, 0:n], ALU.mult, ALU.mult),
                         reads=[B_mf, B_rs[0], B_par], writes=[B_mf])
                for c in range(NCH):
                    P.op("dve", f_tt(xblk[:, c, 0:n], xblk[:, c, 0:n], mf[:, c, 0:n], ALU.add), reads=[B_x, B_mf], writes=[B_x])
                b = sumsq_of(xblk, B_x, n, NCH)
                rstd_from(b, n, 1, 1.0 / DM, RMS_EPS)
                for c in range(NCH):
                    P.op("dve", f_stt(h2[:, c, 0:n], xblk[:, c, 0:n], gcol(l, 2, c), rs_t[:, 1, 0:n], ALU.mult, ALU.mult),
                         reads=[B_x, B_rs[1], B_par], writes=[B_h2])
                ssb = None
                for qd in range(4):
                    for r in range(4):
                        wt, Bw = w_next((l, 14 + 8 * qd + r))
                        for cc in range(4):
                            hc = r * 4 + cc
                            b = next_bank()
                            items = [(bank(b)[:, 0:n], wt[:, kc, cc * 128:(cc + 1) * 128], h2[:, kc, 0:n], kc == 0, kc == 15) for kc in range(16)]
                            P.op("pe", f_mm(items), reads=[Bw, B_h2], writes=[B_bank[b]])
                            j = hc % 2
                            if j == 0:
                                P.op("act", f_act(sg_t[:, 0, 0:n], bank(b)[:, 0:n], AF.Relu), reads=[B_bank[b]], writes=[B_sg[0]])
                                P.op("act", f_act(hid[:, hc, 0:n], sg_t[:, 0, 0:n], AF.Square), reads=[B_sg[0]], writes=[B_hid])
                            else:
                                P.op("dve", f_ts(sg_t[:, 1, 0:n], bank(b)[:, 0:n], 0.0, None, ALU.max), reads=[B_bank[b]], writes=[B_sg[1]])
                                P.op("dve", f_tt(hid[:, hc, 0:n], sg_t[:, 1, 0:n], sg_t[:, 1, 0:n], ALU.mult), reads=[B_sg[1]], writes=[B_hid])
                    if qd == 3:
                        ssb = next_bank()
                    for r in range(4):
                        wt, Bw = w_next((l, 14 + 8 * qd + 4 + r))
                        for cc in range(4):
                            ch = r * 4 + cc
                            b = next_bank()
                            if b == ssb:
                                b = next_bank()
                            items = [(bank(b)[:, 0:n], wt[:, kc, cc * 128:(cc + 1) * 128], hid[:, kc, 0:n], kc == 0, kc == 15) for kc in range(16)]
                            P.op("pe", f_mm(items), reads=[Bw, B_hid], writes=[B_bank[b]])
                            if qd == 0:
                                P.op("act", f_act(mf[:, ch, 0:n], bank(b)[:, 0:n], AF.Copy), reads=[B_bank[b]], writes=[B_mf])
                            else:
                                P.op("dve", f_tt(mf[:, ch, 0:n], bank(b)[:, 0:n], mf[:, ch, 0:n], ALU.add), reads=[B_bank[b], B_mf], writes=[B_mf])
                            if qd == 3:
                                P.op("act", f_act(sqb[:, ch % 2, 0, 0:n], mf[:, ch, 0:n], AF.Square), reads=[B_mf], writes=[B_sq[ch % 2]])
                                P.op("pe", f_mm([(bank(ssb)[:, 0:n], ones[:], sqb[:, ch % 2, 0, 0:n], ch == 0, ch == 15)]),
                                     reads=[B_sq[ch % 2], B_const], writes=[B_bank[ssb]])
                rstd_from(ssb, n, 2, 1.0 / DM, RMS_EPS)
                for c in range(NCH):
                    P.op("dve", f_stt(mf[:, c, 0:n], mf[:, c, 0:n], gcol(l, 3, c), rs_t[:, 2, 0:n], ALU.mult, ALU.mult),
                         reads=[B_mf, B_rs[2], B_par], writes=[B_mf])
                for c in range(NCH):
                    P.op("dve", f_tt(mf[:, c, 0:n], xblk[:, c, 0:n], mf[:, c, 0:n], ALU.add), reads=[B_x, B_mf], writes=[B_mf])
                P.op("pool", f_dma(xdst[:, :, t0:t0 + n].rearrange("c p t -> p c t"), mf[:, :, 0:n]),
                     reads=[B_mf], writes=gr(Gdst, t0, t0 + n), dsem=S_xst)

        queue_conversions(0)
        pump_conv(20)
        for l in range(NL):
            if stop == "conv":
                break
            phase1(l)
            if l + 1 < NL:
                queue_conversions(l + 1)
            if stop == "p1":
                break
            phase2(l)
            if stop == "p2":
                break
            phase3(l)
        pump_conv(10 ** 6)
        final = []
        for g in G_x["yout"]:
            final.extend(g.writers.items())
        for S in [S_us, S_qs, S_ks, S_vs, S_xst, S_misc] + conv_sems:
            final.append((S, S.count))
        P.wait_only("pool", final)
        P.wait_only("sync", final)

        with nc.Block() as block:
            @block.tensor
            def _(e):
                P.replay("pe", e)

            @block.scalar
            def _(e):
                P.replay("act", e)

            @block.vector
            def _(e):
                P.replay("dve", e)

            @block.gpsimd
            def _(e):
                P.replay("pool", e)

            @block.sync
            def _(e):
                P.replay("sync", e)
    info = dict(RIN=RIN, ROUT=ROUT, TIN=TIN, TOUT=TOUT, nops={k: len(v["ops"]) for k, v in P.eng.items()})
    return nc, info


def _bias_table(rpb, flipped):
    rows = SEQ // GW

    def true_rc(lr, lc):
        if flipped:
            return rows - 1 - lr, GW - 1 - lc
        return lr, lc
    tiles = [(0, 2 * j) for j in range(4)] + [(2, 2 * j) for j in range(4)] + [(20, 2 * j) for j in range(8, 13)]
    qi = np.arange(128)
    out = np.full((DEPTH, NHEAD, 128, NBT, 128), NEG, np.float32)
    for t, (qr0, kr0) in enumerate(tiles):
        qlr = qr0 + qi // GW
        qlc = qi % GW
        klr = kr0 + qi // GW
        klc = qi % GW
        qr, qc = true_rc(qlr, qlc)
        kr, kc = true_rc(klr, klc)
        rs = np.clip(qr - 4, 0, rows - 8)
        cs = np.clip(qc - 8, 0, GW - 16)
        valid = (kr[None, :] >= rs[:, None]) & (kr[None, :] < rs[:, None] + 8) & \
                (kc[None, :] >= cs[:, None]) & (kc[None, :] < cs[:, None] + 16)
        dr = np.clip(kr[None, :] - qr[:, None] + 7, 0, 14)
        dc = np.clip(kc[None, :] - qc[:, None], -15, 15) + 15
        g = rpb[:, :, dr, dc]
        out[:, :, :, t, :] = np.where(valid[None, None], g, np.float32(NEG))
    return out


_CACHE = {}


def kernel(x, w_in, w_dw, b_dw, conv_ln_g, conv_ln_b, rpb, w_out, w_up, w_down,
           pre_mix_g, post_mix_g, pre_mlp_g, post_mlp_g, _NL=DEPTH, _dump=(), _trace=False, _stop=None, _cores=8):
    NL = _NL
    f32 = np.float32
    x = np.asarray(x, f32)
    key = (NL, tuple(_dump), _stop)
    if key not in _CACHE:
        _CACHE[key] = build_program(NL, _dump, _stop)
    nc, info = _CACHE[key]
    T0 = info["TIN"][0]
    R0 = info["RIN"][0]
    w_in = np.ascontiguousarray(w_in, f32)
    w_out = np.ascontiguousarray(w_out, f32)
    w_up = np.ascontiguousarray(w_up, f32)
    w_down = np.ascontiguousarray(w_down, f32)
    g4 = np.stack([np.asarray(a, f32) for a in (pre_mix_g, post_mix_g, pre_mlp_g, post_mlp_g)], axis=1)
    gpack = np.ascontiguousarray(g4.reshape(DEPTH, 4, NCH, 128).transpose(3, 0, 1, 2)).reshape(128, -1)
    rpb = np.asarray(rpb, f32)
    packs = {}
    for flipped in (False, True):
        wd = np.asarray(w_dw, f32)
        if flipped:
            wd = wd[:, ::-1, :]
        cp = np.concatenate([wd.transpose(0, 2, 1),
                             np.asarray(b_dw, f32)[:, :, None],
                             np.asarray(conv_ln_g, f32)[:, :, None],
                             np.asarray(conv_ln_b, f32)[:, :, None]], axis=2)
        cp = cp.reshape(DEPTH, 8, 128, 34).transpose(2, 0, 1, 3)
        packs[flipped] = (np.ascontiguousarray(cp).reshape(128, -1), _bias_table(rpb, flipped))
    in_maps = []
    for c in range(_cores):
        b, half = divmod(c, 2)
        if half == 0:
            xt = x[b, 0:T0, :]
        else:
            xt = x[b, SEQ - T0:SEQ, :][::-1]
        xT = np.ascontiguousarray(xt.T).reshape(NCH, 128, T0)
        cpk, bt = packs[half == 1]
        in_maps.append({"xin": xT, "w_in": w_in, "w_out": w_out, "w_up": w_up, "w_down": w_down,
                        "gpack": gpack, "cpack": cpk, "biasg": bt})
    if _trace:
        res = run_bass_kernel_spmd(nc, in_maps, core_ids=list(range(_cores)), trace=True)
    else:
        res = run_bass_kernel_spmd(nc, in_maps, core_ids=list(range(_cores)))
    _CACHE["last"] = res
    out = np.zeros((4, SEQ, DM), f32)
    for c in range(_cores):
        b, half = divmod(c, 2)
        y = np.asarray(res.results[c]["yout"], f32).reshape(DM, 2048).T
        if half == 0:
            out[b, 0:2048] = y
        else:
            out[b, 2048:SEQ] = y[::-1]
    return out
```

```python
import numpy as np
from contextlib import ExitStack
import concourse.bass as bass
import concourse.mybir as mybir
from concourse.bass_utils import run_bass_kernel_spmd

F32 = mybir.dt.float32
BF16 = mybir.dt.bfloat16
I32 = mybir.dt.int32
AF = mybir.ActivationFunctionType
ALU = mybir.AluOpType

DEPTH = 4
DM = 2048
NCH = 16
SEQ = 4096
GW = 64
CW = 1024
NHEAD = 16
HD = 64
KCONV = 31
DFF = 8192
NBT = 13
RMS_EPS = 1e-6
LN_EPS = 1e-5
NEG = -30000.0
NS = 4
NTILE = 46


class Sem:
    def __init__(self, h):
        self.h = h
        self.count = 0


class Buf:
    def __init__(self, name):
        self.name = name
        self.writers = {}
        self.readers = {}
        self.war = {}
        self.aliases = []


def alias(*bufs):
    for a in bufs:
        for b in bufs:
            if a is not b and b not in a.aliases:
                a.aliases.append(b)


class Prog:
    def __init__(self, nc, sems):
        self.nc = nc
        self.eng = {}
        for name in ("pe", "act", "dve", "pool", "sync"):
            self.eng[name] = dict(ops=[], waited={}, sem=sems[name])

    def op(self, eng, fn, reads=(), writes=(), deps=(), dsem=None):
        E = self.eng[eng]
        rset = set(id(b) for b in reads)
        wset = set(id(b) for b in writes)
        hs = []
        for b in reads:
            hs.extend(b.writers.items())
        for b in writes:
            if b.readers or id(b) in rset:
                b.war = dict(b.readers)
                for k, v in b.writers.items():
                    b.war[k] = max(b.war.get(k, 0), v)
                b.readers = {}
                b.writers = {}
            for bb in b.aliases:
                for d in (bb.readers, bb.writers):
                    for k, v in d.items():
                        b.war[k] = max(b.war.get(k, 0), v)
            hs.extend(b.war.items())
        hs.extend(deps)
        need = {}
        for sem, val in hs:
            if sem is E["sem"] and eng == "pe":
                continue
            need[sem] = max(need.get(sem, 0), val)
        waits = []
        em = E.setdefault("emitted", {})
        for sem, val in need.items():
            if em.get(sem, 0) < val:
                em[sem] = val
                waits.append((sem, val))
        if dsem is not None:
            dsem.count += 16
            h = (dsem, dsem.count)
            inc = (dsem, 16)
        else:
            E["sem"].count += 1
            h = (E["sem"], E["sem"].count)
            inc = (E["sem"], 1)
        E["ops"].append((waits, fn, inc))
        for b in reads:
            if id(b) not in wset:
                b.readers[h[0]] = max(b.readers.get(h[0], 0), h[1])
        for b in writes:
            b.writers[h[0]] = max(b.writers.get(h[0], 0), h[1])
        return h

    def wait_only(self, eng, deps):
        E = self.eng[eng]
        waits = []
        for sem, val in deps:
            if E.setdefault("emitted", {}).get(sem, 0) < val:
                E["emitted"][sem] = val
                waits.append((sem, val))
        E["ops"].append((waits, None, None))

    def replay(self, name, e):
        for waits, fn, inc in self.eng[name]["ops"]:
            for sem, val in waits:
                e.wait_ge(sem.h, val)
            if fn is not None:
                ins = fn(e)
                ins.then_inc(inc[0].h, inc[1])


def f_dma(out, in_):
    return lambda e: e.dma_start(out=out, in_=in_)


def f_act(out, in_, func, **kw):
    return lambda e: e.activation(out=out, in_=in_, func=func, **kw)


def f_ts(out, in0, s1, s2, op0, op1=None):
    if op1 is None:
        return lambda e: e.tensor_scalar(out=out, in0=in0, scalar1=s1, scalar2=None, op0=op0)
    return lambda e: e.tensor_scalar(out=out, in0=in0, scalar1=s1, scalar2=s2, op0=op0, op1=op1)


def f_tt(out, in0, in1, op):
    return lambda e: e.tensor_tensor(out=out, in0=in0, in1=in1, op=op)


def f_stt(out, in0, scalar, in1, op0, op1):
    return lambda e: e.scalar_tensor_tensor(out=out, in0=in0, scalar=scalar, in1=in1, op0=op0, op1=op1)


def f_copy(out, in_):
    return lambda e: e.tensor_copy(out=out, in_=in_)


def f_recip(out, in_):
    return lambda e: e.reciprocal(out=out, in_=in_)


def f_memset(ap, val):
    return lambda e: e.memset(ap, val)


def f_mm(items):
    def fn(e):
        ins = None
        for (out, lhsT, rhs, start, stop) in items:
            ins = e.matmul(out, lhsT=lhsT, rhs=rhs, start=start, stop=stop)
        return ins
    return fn


def f_tr(out, in_, ident):
    return lambda e: e.transpose(out, in_, ident)


def blocks_of(T):
    res = []
    t = 0
    while t < T:
        n = 512 if T - t >= 512 else T - t
        res.append((t, n))
        t += n
    return res


def build_program(NL=DEPTH, dump=(), stop=None):
    nc = bass.Bass("TRN2", target_bir_lowering=False)
    RIN = [32 + 4 * (NL - l) for l in range(NL)]
    ROUT = [32 + 4 * (NL - 1 - l) for l in range(NL)]
    TIN = [r * GW for r in RIN]
    TOUT = [r * GW for r in ROUT]
    T0 = TIN[0]
    T1 = TOUT[0]

    def dram(name, shape, dt, kind="Internal"):
        if name in dump:
            kind = "ExternalOutput"
        return nc.dram_tensor(name, shape, dt, kind=kind).ap()

    xin = dram("xin", [NCH, 128, T0], F32, "ExternalInput")
    w_in = dram("w_in", [DEPTH, DM, 5120], F32, "ExternalInput")
    w_out = dram("w_out", [DEPTH, DM, DM], F32, "ExternalInput")
    w_up = dram("w_up", [DEPTH, DM, DFF], F32, "ExternalInput")
    w_down = dram("w_down", [DEPTH, DFF, DM], F32, "ExternalInput")
    gpack_d = dram("gpack", [128, DEPTH * 4 * NCH], F32, "ExternalInput")
    cpack_d = dram("cpack", [128, DEPTH * 8 * 34], F32, "ExternalInput")
    bias_d = dram("biasg", [DEPTH, NHEAD, 128, NBT, 128], F32, "ExternalInput")
    yout = dram("yout", [NCH, 128, TOUT[NL - 1]], F32, "ExternalOutput")
    xs = [dram("xs0", [NCH, 128, T1], F32), dram("xs1", [NCH, 128, T1], F32)]
    wsc = dram("wsc", [2, NTILE, 128, 8192], BF16)
    uT = dram("uT", [8, 128, T0 + 32], BF16)
    qT = dram("qT", [8, 128, T0], BF16)
    kT = dram("kT", [8, 128, T0], BF16)
    vS = dram("vS", [T0 // 128, 128, NHEAD * 65], BF16)
    mixT = dram("mixT", [NCH, 128, T1], BF16)

    es = ExitStack()
    with es:
        def sb(name, shape, dt):
            return es.enter_context(nc.sbuf_tensor(name, shape, dt))

        def newsem(name):
            return Sem(es.enter_context(nc.semaphore(name)))

        RW = sb("RW", [128, NS * 4096], F32)
        RX = sb("RX", [128, 8192], F32)
        RH = sb("RH", [128, 8192], F32)
        RS = sb("RS", [128, 8320], F32)
        RD = sb("RD", [128, 4096], F32)
        sqb = sb("sqb", [128, 2, 4, 512], BF16)
        rs_t = sb("rs_t", [128, 4, 512], F32)
        sg_t = sb("sg_t", [128, 2, 512], F32)
        ident = sb("ident", [128, 128], BF16)
        ones = sb("ones", [128, 128], BF16)
        iot = sb("iot", [128, 128], I32)
        zer = sb("zer", [128, 128], BF16)
        gpack = sb("gpack_s", [128, DEPTH, 4, NCH], F32)
        cpack = sb("cpack_s", [128, DEPTH, 8, 34], F32)
        psF = es.enter_context(nc.psum_tensor("psF", [128, 7 * 512], F32))
        psB = es.enter_context(nc.psum_tensor("psB", [128, 1024], BF16))

        def bfview(reg, off_w, n_w):
            return reg[:, off_w:off_w + n_w].bitcast(BF16)

        wslot = [bfview(RW, s * 4096, 4096) for s in range(NS)]
        B_w = [Buf(f"w{s}") for s in range(NS)]
        S_w = [newsem(f"sw{s}") for s in range(NS)]
        xblk = RX[:, :].rearrange("p (c n) -> p c n", c=NCH)
        B_xg = [Buf(f"x{g}") for g in range(4)]
        acc = RX[:, 0:4096].rearrange("p (c n) -> p c n", c=8)
        B_acc = [Buf(f"acc{c}") for c in range(8)]
        ublk = bfview(RX, 4096, 2176)[:, 0:8 * 542].rearrange("p (c n) -> p c n", c=8)
        B_u = Buf("ublk")
        S_u = newsem("su")
        tmpc = RX[:, 6272:6272 + 1024].rearrange("p (c n) -> p c n", c=2)
        B_tmpc = [Buf("tmpc0"), Buf("tmpc1")]
        for a in B_acc + B_tmpc + [B_u]:
            a.aliases = list(B_xg)
        for a in B_xg:
            a.aliases = B_acc + B_tmpc + [B_u]
        S_x = newsem("sx")
        hT = [bfview(RH, i * 4096, 4096).rearrange("p (c n) -> p c n", c=NCH) for i in range(2)]
        B_hT = [Buf("hT0"), Buf("hT1")]
        mixb = hT[0]
        B_mix = B_hT[0]
        S_mix = newsem("smix")
        h2 = hT[1]
        B_h2 = B_hT[1]
        RHb = RH[:, :].bitcast(BF16)
        o = 0
        kblk = []
        for i in range(2):
            kblk.append(RHb[:, o:o + 1024]); o += 1024
        vblk = []
        for i in range(2):
            vblk.append(RHb[:, o:o + 1040].rearrange("p (j h d) -> p j h d", j=8, h=2)); o += 1056
        qA = []
        qB = []
        for i in range(2):
            qA.append(RHb[:, o:o + 512]); o += 512
            qB.append(RHb[:, o:o + 512]); o += 512
        biasb = []
        for i in range(2):
            biasb.append(RHb[:, o:o + 2 * NBT * 128].rearrange("p (h t k) -> p h t k", h=2, t=NBT)); o += 2 * NBT * 128
        pT = []
        for i in range(2):
            pT.append(RHb[:, o:o + 640]); o += 640
        onb = []
        for i in range(2):
            onb.append(RHb[:, o:o + 128]); o += 128
        assert o <= 16384, o
        B_k = [Buf("k0"), Buf("k1")]
        B_v = [Buf("v0"), Buf("v1")]
        B_qA = [Buf("qA0"), Buf("qA1")]
        B_qB = [Buf("qB0"), Buf("qB1")]
        B_bias = [Buf("bias0"), Buf("bias1")]
        B_pT = [Buf("pT0"), Buf("pT1")]
        B_on = [Buf("on0"), Buf("on1")]
        S_k = [newsem("sk0"), newsem("sk1")]
        S_v = [newsem("sv0"), newsem("sv1")]
        S_qA = [newsem("sqa0"), newsem("sqa1")]
        S_qB = [newsem("sqb0"), newsem("sqb1")]
        S_bias = [newsem("sbi0"), newsem("sbi1")]
        attn_bufs = B_k + B_v + B_qA + B_qB + B_bias + B_pT + B_on
        for a in attn_bufs:
            a.aliases = list(B_hT)
        for a in B_hT:
            a.aliases = list(attn_bufs)
        RSb = RS[:, :].bitcast(BF16)
        ustage = RSb[:, 0:4096].rearrange("p (c n) -> p c n", c=8)
        qstage = RSb[:, 4096:8192].rearrange("p (c n) -> p c n", c=8)
        kstage = RSb[:, 8192:12288].rearrange("p (c n) -> p c n", c=8)
        vstage = RSb[:, 12288:12288 + 4160].rearrange("p (t h d) -> p t h d", t=4, h=NHEAD)
        mf = RS[:, 0:8192].rearrange("p (c n) -> p c n", c=NCH)
        B_us, B_qs, B_ks, B_vs = Buf("ustage"), Buf("qstage"), Buf("kstage"), Buf("vstage")
        S_us, S_qs, S_ks, S_vs = newsem("sus"), newsem("sqs"), newsem("sks"), newsem("svs")
        B_mfg = [Buf(f"mf{g}") for g in range(4)]
        for a in (B_us, B_qs, B_ks, B_vs):
            a.aliases = list(B_mfg)
        for a in B_mfg:
            a.aliases = [B_us, B_qs, B_ks, B_vs]
        S_xst = newsem("sxst")
        hid = RD[:, :].bitcast(BF16).rearrange("p (c n) -> p c n", c=16)
        B_hid = Buf("hid")
        B_sq = [Buf("sq0"), Buf("sq1")]
        B_rs = [Buf(f"rs{i}") for i in range(4)]
        B_sg = [Buf("sg0"), Buf("sg1")]
        B_const = Buf("const")
        B_par = Buf("params")
        S_par = newsem("spar")
        S_misc = newsem("smisc")
        def bank(b):
            return psF[:, b * 512:(b + 1) * 512]
        B_bank = [Buf(f"bank{b}") for b in range(7)]
        B_bT = Buf("bankT")
        rot = [0]

        def next_bank():
            b = rot[0]
            rot[0] = (b + 1) % 7
            return b

        sems = {n: newsem("e_" + n) for n in ("pe", "act", "dve", "pool", "sync")}
        P = Prog(nc, sems)
        conv_sems = [newsem(f"cv{i}") for i in range(8)]
        conv_hist = []

        def gran(name, n):
            return [Buf(f"{name}_g{i}") for i in range(n)]
        NG = T0 // 256
        G_x = {"xin": gran("xin", NG), 0: gran("xs0", NG), 1: gran("xs1", NG), "yout": gran("yout", NG)}
        G_u = gran("uT", NG + 1)
        G_q = gran("qT", NG)
        G_k = gran("kT", NG)
        G_v = gran("vS", NG)
        G_myc = gran("myc", NG)
        G_mya = gran("mya", NG)
        B_wsc = [[[Buf(f"wsc{p}_{t}_{h}") for h in range(2)] for t in range(NTILE)] for p in range(2)]

        def gr(G, lo, hi):
            return G[lo // 256:(hi + 255) // 256]

        P.op("pool", lambda e: e.iota(iot[:], pattern=[[1, 128]], base=0, channel_multiplier=-1), writes=[B_const])
        P.op("dve", f_ts(ident[:], iot[:], 0.0, None, ALU.is_equal), reads=[B_const], writes=[B_const])
        P.op("dve", f_memset(ones[:], 1.0), writes=[B_const])
        P.op("dve", f_memset(zer[:], 0.0), writes=[B_const])
        P.op("sync", f_dma(gpack[:].rearrange("p a b c -> p (a b c)"), gpack_d), writes=[B_par], dsem=S_par)
        P.op("sync", f_dma(cpack[:].rearrange("p a b c -> p (a b c)"), cpack_d), writes=[B_par], dsem=S_par)
        for c in range(8):
            P.op("pool", f_dma(uT[c, :, 0:15], zer[:, 0:15]), reads=[B_const], writes=[G_u[0]], dsem=S_misc)

        def tile_src(l, tid, half):
            par = l % 2
            dst = wsc[par, tid].rearrange("p (k n) -> p k n", k=16)
            k0, k1 = half * 8, half * 8 + 8
            res = []
            if tid < 4:
                for (cb, d0) in ((2 * tid * 128, 0), (1024 + 2 * tid * 128, 256)):
                    src = w_in[l, k0 * 128:k1 * 128, cb:cb + 256].rearrange("(k p) n -> p k n", p=128)
                    res.append((dst[:, k0:k1, d0:d0 + 256], src))
            elif tid < 10:
                cb = 2048 + (tid - 4) * 512
                src = w_in[l, k0 * 128:k1 * 128, cb:cb + 512].rearrange("(k p) n -> p k n", p=128)
                res.append((dst[:, k0:k1, :], src))
            elif tid < 14:
                cb = (tid - 10) * 512
                src = w_out[l, k0 * 128:k1 * 128, cb:cb + 512].rearrange("(k p) n -> p k n", p=128)
                res.append((dst[:, k0:k1, :], src))
            else:
                qd, r = divmod(tid - 14, 8)
                if r < 4:
                    cb = qd * 2048 + r * 512
                    src = w_up[l, k0 * 128:k1 * 128, cb:cb + 512].rearrange("(k p) n -> p k n", p=128)
                else:
                    cb = (r - 4) * 512
                    rb = qd * 2048
                    src = w_down[l, rb + k0 * 128:rb + k1 * 128, cb:cb + 512].rearrange("(k p) n -> p k n", p=128)
                res.append((dst[:, k0:k1, :], src))
            return res

        pending_conv = []

        def queue_conversions(l):
            for tid in range(NTILE):
                for half in range(2):
                    pending_conv.append((l, tid, half))

        def pump_conv(k):
            for _ in range(k):
                if not pending_conv:
                    return
                l, tid, half = pending_conv.pop(0)
                for (dst, src) in tile_src(l, tid, half):
                    i = len(conv_hist)
                    cs = conv_sems[i % 8]
                    deps = [conv_hist[i - 8]] if i >= 8 else []
                    h = P.op("pool", f_dma(dst, src), writes=[B_wsc[l % 2][tid][half]], deps=deps, dsem=cs)
                    conv_hist.append(h)

        wtiles = []
        for l in range(NL):
            for _ in blocks_of(TIN[l]):
                wtiles.extend((l, t) for t in range(10))
            for _ in blocks_of(TOUT[l]):
                wtiles.extend((l, t) for t in range(10, NTILE))
        wstate = dict(next_load=0, next_use=0)

        def w_load_one():
            i = wstate["next_load"]
            if i >= len(wtiles):
                return
            l, tid = wtiles[i]
            while any(pc[0] == l and pc[1] == tid for pc in pending_conv):
                pump_conv(1)
            s = i % NS
            P.op("sync", f_dma(wslot[s], wsc[l % 2, tid]), reads=B_wsc[l % 2][tid], writes=[B_w[s]], dsem=S_w[s])
            wstate["next_load"] = i + 1

        def w_next(expect):
            i = wstate["next_use"]
            assert wtiles[i] == expect, (wtiles[i], expect)
            while wstate["next_load"] < min(i + NS, len(wtiles)):
                if wstate["next_load"] >= i + NS - 1 and i >= 1:
                    pass
                w_load_one()
            wstate["next_use"] = i + 1
            pump_conv(1)
            s = i % NS
            return wslot[s].rearrange("p (k n) -> p k n", k=16), B_w[s]

        def rstd_from(bank_b, n, ridx, scale, eps):
            r = rs_t[:, ridx, 0:n]
            P.op("dve", f_ts(r, bank(bank_b)[:, 0:n], scale, eps, ALU.mult, ALU.add), reads=[B_bank[bank_b]], writes=[B_rs[ridx]])
            P.op("act", f_act(r, r, AF.Sqrt), reads=[B_rs[ridx]], writes=[B_rs[ridx]])
            P.op("dve", f_recip(r, r), reads=[B_rs[ridx]], writes=[B_rs[ridx]])

        def sumsq_of(src3, B_src, n, nchunks):
            b = next_bank()
            ng = nchunks // 4
            for g in range(ng):
                sq = sqb[:, g % 2, :, 0:n]
                P.op("act", f_act(sq, src3[:, 4 * g:4 * g + 4, 0:n], AF.Square), reads=[B_src[g]], writes=[B_sq[g % 2]])
                items = [(bank(b)[:, 0:n], ones[:], sqb[:, g % 2, j, 0:n], (g == 0 and j == 0), (g == ng - 1 and j == 3)) for j in range(4)]
                P.op("pe", f_mm(items), reads=[B_sq[g % 2], B_const], writes=[B_bank[b]])
            return b

        def gcol(l, k, c):
            return gpack[:, l, k, c:c + 1]

        def phase1(l):
            xsrc, Gsrc = (xin, G_x["xin"]) if l == 0 else (xs[(l - 1) % 2], G_x[(l - 1) % 2])
            P.op("dve", f_memset(vstage[:, :, :, 64:65], 1.0), writes=[B_vs])
            blks = blocks_of(TIN[l])

            def norm_block(bi):
                t0, n = blks[bi]
                par = bi % 2
                P.op("sync", f_dma(xblk[:, :, 0:n], xsrc[:, :, t0:t0 + n].rearrange("c p t -> p c t")),
                     reads=gr(Gsrc, t0, t0 + n), writes=B_xg, dsem=S_x)
                b = sumsq_of(xblk, B_xg, n, NCH)
                rstd_from(b, n, 0, 1.0 / DM, RMS_EPS)
                for c in range(NCH):
                    P.op("dve", f_stt(hT[par][:, c, 0:n], xblk[:, c, 0:n], gcol(l, 0, c), rs_t[:, 0, 0:n], ALU.mult, ALU.mult),
                         reads=[B_xg[c // 4], B_rs[0], B_par], writes=[B_hT[par]])

            norm_block(0)
            for bi, (t0, n) in enumerate(blks):
                par = bi % 2
                nt = n // 128
                for w in range(10):
                    wt, Bw = w_next((l, w))
                    if w == 5 and bi + 1 < len(blks):
                        norm_block(bi + 1)
                    if w < 8:
                        bs = [next_bank() for _ in range(4)]
                        for cc in range(4):
                            items = [(bank(bs[cc])[:, 0:n], wt[:, kc, cc * 128:(cc + 1) * 128], hT[par][:, kc, 0:n], kc == 0, kc == 15) for kc in range(16)]
                            P.op("pe", f_mm(items), reads=[Bw, B_hT[par]], writes=[B_bank[bs[cc]]])
                        if w < 4:
                            for j in range(2):
                                P.op("act", f_act(sg_t[:, j, 0:n], bank(bs[2 + j])[:, 0:n], AF.Sigmoid), reads=[B_bank[bs[2 + j]]], writes=[B_sg[j]])
                                P.op("dve", f_tt(ustage[:, 2 * w + j, 0:n], bank(bs[j])[:, 0:n], sg_t[:, j, 0:n], ALU.mult),
                                     reads=[B_bank[bs[j]], B_sg[j]], writes=[B_us])
                        else:
                            stg, Bs = (qstage, B_qs) if w < 6 else (kstage, B_ks)
                            sc = 0.125 if w < 6 else 1.0
                            for cc in range(4):
                                ch = (w % 2) * 4 + cc
                                if cc % 2 == 0:
                                    P.op("act", f_act(stg[:, ch, 0:n], bank(bs[cc])[:, 0:n], AF.Copy, scale=sc), reads=[B_bank[bs[cc]]], writes=[Bs])
                                else:
                                    P.op("dve", f_ts(stg[:, ch, 0:n], bank(bs[cc])[:, 0:n], sc, None, ALU.mult), reads=[B_bank[bs[cc]]], writes=[Bs])
                    else:
                        hb = (w - 8) * 8
                        for tt in range(nt):
                            b2 = next_bank()
                            items = [(bank(b2)[:, :], hT[par][:, kc, tt * 128:(tt + 1) * 128], wt[:, kc, :], kc == 0, kc == 15) for kc in range(16)]
                            P.op("pe", f_mm(items), reads=[Bw, B_hT[par]], writes=[B_bank[b2]])
                            src = bank(b2)[:, :].rearrange("p (h d) -> p h d", d=64)
                            dst = vstage[:, tt, hb:hb + 8, 0:64]
                            if tt % 2 == 0:
                                P.op("act", f_act(dst, src, AF.Copy), reads=[B_bank[b2]], writes=[B_vs])
                            else:
                                P.op("dve", f_copy(dst, src), reads=[B_bank[b2]], writes=[B_vs])
                    if w == 3:
                        P.op("pool", f_dma(uT[:, :, 15 + t0:15 + t0 + n].rearrange("c p t -> p c t"), ustage[:, :, 0:n]),
                             reads=[B_us], writes=gr(G_u, t0, t0 + n), dsem=S_us)
                    if w == 5:
                        P.op("pool", f_dma(qT[:, :, t0:t0 + n].rearrange("c p t -> p c t"), qstage[:, :, 0:n]),
                             reads=[B_qs], writes=gr(G_q, t0, t0 + n), dsem=S_qs)
                    if w == 7:
                        P.op("pool", f_dma(kT[:, :, t0:t0 + n].rearrange("c p t -> p c t"), kstage[:, :, 0:n]),
                             reads=[B_ks], writes=gr(G_k, t0, t0 + n), dsem=S_ks)
                    if w == 9:
                        P.op("pool", f_dma(vS[t0 // 128:t0 // 128 + nt].rearrange("t p f -> p t f"),
                                           vstage[:, 0:nt].rearrange("p t h d -> p t (h d)")),
                             reads=[B_vs], writes=gr(G_v, t0, t0 + n), dsem=S_vs)

        def conv_gen(l, t0, n):
            wv = lambda c, j: cpack[:, l, c, j:j + 1]
            P.op("sync", f_dma(ublk[:, :, 0:n + 30], uT[:, :, t0:t0 + n + 30].rearrange("c p t -> p c t")),
                 reads=gr(G_u, t0, t0 + n + 30), writes=[B_u], dsem=S_u)
            yield
            for c0 in range(0, 8, 4):
                for j in range(KCONV):
                    for c in range(c0, c0 + 4):
                        if j == 0:
                            P.op("dve", f_ts(acc[:, c, 0:n], ublk[:, c, 0:n], wv(c, 0), cpack[:, l, c, 31:32], ALU.mult, ALU.add),
                                 reads=[B_u, B_par], writes=[B_acc[c]])
                        else:
                            P.op("dve", f_stt(acc[:, c, 0:n], ublk[:, c, j:j + n], wv(c, j), acc[:, c, 0:n], ALU.mult, ALU.add),
                                 reads=[B_u, B_par, B_acc[c]], writes=[B_acc[c]])
                        yield
            for g in range(2):
                cb = sqb[:, 0, :, 0:n]
                P.op("act", f_act(cb, acc[:, 4 * g:4 * g + 4, 0:n], AF.Copy), reads=B_acc[4 * g:4 * g + 4], writes=[B_sq[0]])
                items = [(bank(5)[:, 0:n], ones[:], sqb[:, 0, j, 0:n], (g == 0 and j == 0), (g == 1 and j == 3)) for j in range(4)]
                P.op("pe", f_mm(items), reads=[B_sq[0], B_const], writes=[B_bank[5]])
                sq = sqb[:, 1, :, 0:n]
                P.op("act", f_act(sq, acc[:, 4 * g:4 * g + 4, 0:n], AF.Square), reads=B_acc[4 * g:4 * g + 4], writes=[B_sq[1]])
                items = [(bank(6)[:, 0:n], ones[:], sqb[:, 1, j, 0:n], (g == 0 and j == 0), (g == 1 and j == 3)) for j in range(4)]
                P.op("pe", f_mm(items), reads=[B_sq[1], B_const], writes=[B_bank[6]])
                yield
            mean = rs_t[:, 1, 0:n]
            msq = rs_t[:, 2, 0:n]
            rst = rs_t[:, 3, 0:n]
            P.op("dve", f_ts(mean, bank(5)[:, 0:n], 1.0 / CW, None, ALU.mult), reads=[B_bank[5]], writes=[B_rs[1]])
            P.op("dve", f_tt(msq, mean, mean, ALU.mult), reads=[B_rs[1]], writes=[B_rs[2]])
            P.op("dve", f_stt(rst, bank(6)[:, 0:n], 1.0 / CW, msq, ALU.mult, ALU.subtract), reads=[B_bank[6], B_rs[2]], writes=[B_rs[3]])
            P.op("dve", f_ts(rst, rst, LN_EPS, None, ALU.add), reads=[B_rs[3]], writes=[B_rs[3]])
            P.op("act", f_act(rst, rst, AF.Sqrt), reads=[B_rs[3]], writes=[B_rs[3]])
            P.op("dve", f_recip(rst, rst), reads=[B_rs[3]], writes=[B_rs[3]])
            yield
            for c in range(8):
                tb = c % 2
                P.op("dve", f_tt(tmpc[:, tb, 0:n], acc[:, c, 0:n], mean, ALU.subtract), reads=[B_acc[c], B_rs[1]], writes=[B_tmpc[tb]])
                if c >= 1:
                    cp = c - 1
                    P.op("dve", f_tt(acc[:, cp, 0:n], tmpc[:, cp % 2, 0:n], rst, ALU.mult), reads=[B_tmpc[cp % 2], B_rs[3]], writes=[B_acc[cp]])
                    P.op("act", f_act(ustage[:, cp, 0:n], acc[:, cp, 0:n], AF.Silu, scale=cpack[:, l, cp, 32:33], bias=cpack[:, l, cp, 33:34]),
                         reads=[B_acc[cp], B_par], writes=[B_us])
                yield
            cp = 7
            P.op("dve", f_tt(acc[:, cp, 0:n], tmpc[:, cp % 2, 0:n], rst, ALU.mult), reads=[B_tmpc[cp % 2], B_rs[3]], writes=[B_acc[cp]])
            P.op("act", f_act(ustage[:, cp, 0:n], acc[:, cp, 0:n], AF.Silu, scale=cpack[:, l, cp, 32:33], bias=cpack[:, l, cp, 33:34]),
                 reads=[B_acc[cp], B_par], writes=[B_us])
            P.op("pool", f_dma(mixT[0:8, :, t0:t0 + n].rearrange("c p t -> p c t"), ustage[:, :, 0:n]),
                 reads=[B_us], writes=gr(G_myc, t0, t0 + n), dsem=S_us)
            yield

        def phase2(l):
            for i in range(2):
                P.op("dve", f_memset(qA[i][64:128, :], 0.0), writes=[B_qA[i]])
                P.op("dve", f_memset(qB[i][0:64, :], 0.0), writes=[B_qB[i]])
            it = 0
            for (t0, n) in blocks_of(TOUT[l]):
                cg = conv_gen(l, t0, n)
                nq = n // 128
                i0 = t0 // 128
                jlo = max(0, i0 - 2)
                jhi = max(i0 + nq - 1 + 2, 3)
                nk = jhi - jlo + 1
                edge = (i0 == 0)
                tlo, ntl = (0, NBT) if edge else (8, 5)
                total_conv_steps = 8 * KCONV + 16
                per_step = (total_conv_steps + 8 * nq - 1) // (8 * nq) + 1
                for hp in range(8):
                    par = hp % 2
                    P.op("sync", f_dma(kblk[par][:, 0:nk * 128], kT[hp, :, jlo * 128:(jlo + nk) * 128]),
                         reads=gr(G_k, jlo * 128, (jlo + nk) * 128), writes=[B_k[par]], dsem=S_k[par])
                    P.op("sync", f_dma(vblk[par][:, 0:nk].rearrange("p j h d -> p j (h d)"),
                                       vS[jlo:jlo + nk, :, 2 * hp * 65:(2 * hp + 2) * 65].rearrange("j p f -> p j f")),
                         reads=gr(G_v, jlo * 128, (jlo + nk) * 128), writes=[B_v[par]], dsem=S_v[par])
                    P.op("sync", f_dma(qA[par][0:64, 0:n], qT[hp, 0:64, t0:t0 + n]), reads=gr(G_q, t0, t0 + n), writes=[B_qA[par]], dsem=S_qA[par])
                    P.op("sync", f_dma(qB[par][64:128, 0:n], qT[hp, 64:128, t0:t0 + n]), reads=gr(G_q, t0, t0 + n), writes=[B_qB[par]], dsem=S_qB[par])
                    P.op("pool", f_dma(biasb[par][:, :, 0:ntl, :], bias_d[l, 2 * hp:2 * hp + 2, :, tlo:tlo + ntl, :].rearrange("h q t k -> q h t k")),
                         writes=[B_bias[par]], dsem=S_bias[par])
                    pump_conv(1)
                    for qi in range(nq):
                        i = i0 + qi
                        if i < 2:
                            js = [0, 1, 2, 3]
                            tb = 0 if i == 0 else 4
                        else:
                            js = list(range(i - 2, i + 3))
                            tb = 8
                        tb -= tlo
                        onp = it % 2
                        for head in range(2):
                            sp = head
                            sb0 = 2 * sp
                            qsel, Bq = (qA[par], B_qA[par]) if head == 0 else (qB[par], B_qB[par])
                            items = []
                            for jj, j in enumerate(js):
                                o_ap = psF[:, sb0 * 512 + jj * 128: sb0 * 512 + (jj + 1) * 128]
                                items.append((o_ap, kblk[par][:, (j - jlo) * 128:(j - jlo + 1) * 128], qsel[:, qi * 128:(qi + 1) * 128], True, False))
                                items.append((o_ap, biasb[par][:, head, tb + jj, :], ident[:], False, True))
                            P.op("pe", f_mm(items), reads=[B_k[par], Bq, B_bias[par], B_const], writes=[B_bank[sb0], B_bank[sb0 + 1]])
                            nkk = len(js) * 128
                            P.op("act", f_act(pT[sp][:, 0:min(nkk, 512)], psF[:, sb0 * 512:sb0 * 512 + min(nkk, 512)], AF.Exp),
                                 reads=[B_bank[sb0]], writes=[B_pT[sp]])
                            if nkk > 512:
                                P.op("act", f_act(pT[sp][:, 512:nkk], psF[:, sb0 * 512 + 512:sb0 * 512 + nkk], AF.Exp),
                                     reads=[B_bank[sb0 + 1]], writes=[B_pT[sp]])
                            items = []
                            for jj, j in enumerate(js):
                                items.append((bank(4)[:, head * 65:(head + 1) * 65], pT[sp][:, jj * 128:(jj + 1) * 128], vblk[par][:, j - jlo, head, :],
                                              jj == 0, jj == len(js) - 1))
                            P.op("pe", f_mm(items), reads=[B_pT[sp], B_v[par]], writes=[B_bank[4]])
                        rc = rs_t[:, 0, 0:2]
                        P.op("dve", f_recip(rc, bank(4)[:, 0:130].rearrange("p (h d) -> p h d", h=2)[:, :, 64]), reads=[B_bank[4]], writes=[B_rs[0]])
                        for head in range(2):
                            P.op("dve", f_ts(onb[onp][:, head * 64:(head + 1) * 64], bank(4)[:, head * 65:head * 65 + 64], rs_t[:, 0, head:head + 1], None, ALU.mult),
                                 reads=[B_bank[4], B_rs[0]], writes=[B_on[onp]])
                        P.op("pe", f_tr(psB[:, 0:128], onb[onp][:, :], ident[:]), reads=[B_on[onp], B_const], writes=[B_bT])
                        P.op("act", f_act(qstage[:, hp, qi * 128:(qi + 1) * 128], psB[:, 0:128], AF.Copy), reads=[B_bT], writes=[B_qs])
                        it += 1
                        for _ in range(per_step):
                            next(cg, None)
                for _ in cg:
                    pass
                P.op("pool", f_dma(mixT[8:16, :, t0:t0 + n].rearrange("c p t -> p c t"), qstage[:, :, 0:n]),
                     reads=[B_qs], writes=gr(G_mya, t0, t0 + n), dsem=S_qs)

        def phase3(l):
            xsrc, Gsrc = (xin, G_x["xin"]) if l == 0 else (xs[(l - 1) % 2], G_x[(l - 1) % 2])
            if l == NL - 1:
                xdst, Gdst = yout, G_x["yout"]
            else:
                xdst, Gdst = xs[l % 2], G_x[l % 2]
            blks = blocks_of(TOUT[l])

            def load_mix(bi):
                t0, n = blks[bi]
                P.op("sync", f_dma(mixb[:, :, 0:n], mixT[:, :, t0:t0 + n].rearrange("c p t -> p c t")),
                     reads=gr(G_myc, t0, t0 + n) + gr(G_mya, t0, t0 + n), writes=[B_mix], dsem=S_mix)

            load_mix(0)
            for bi, (t0, n) in enumerate(blks):
                P.op("sync", f_dma(xblk[:, :, 0:n], xsrc[:, :, t0:t0 + n].rearrange("c p t -> p c t")),
                     reads=gr(Gsrc, t0, t0 + n), writes=B_xg, dsem=S_x)
                ssb = next_bank()
                for w in range(4):
                    wt, Bw = w_next((l, 10 + w))
                    for cc in range(4):
                        ch = w * 4 + cc
                        b = next_bank()
                        if b == ssb:
                            b = next_bank()
                        items = [(bank(b)[:, 0:n], wt[:, kc, cc * 128:(cc + 1) * 128], mixb[:, kc, 0:n], kc == 0, kc == 15) for kc in range(16)]
                        P.op("pe", f_mm(items), reads=[Bw, B_mix], writes=[B_bank[b]])
                        P.op("act", f_act(mf[:, ch, 0:n], bank(b)[:, 0:n], AF.Copy, scale=gcol(l, 1, ch)), reads=[B_bank[b], B_par], writes=[B_mfg[ch // 4]])
                        P.op("act", f_act(sqb[:, ch % 2, 0, 0:n], bank(b)[:, 0:n], AF.Square), reads=[B_bank[b]], writes=[B_sq[ch % 2]])
                        P.op("pe", f_mm([(bank(ssb)[:, 0:n], ones[:], sqb[:, ch % 2, 0, 0:n], ch == 0, ch == 15)]),
                             reads=[B_sq[ch % 2], B_const], writes=[B_bank[ssb]])
                if bi + 1 < len(blks):
                    load_mix(bi + 1)
                rstd_from(ssb, n, 0, 1.0 / DM, RMS_EPS)
                for g in range(4):
                    P.op("dve", f_tt(mf[:, 4 * g:4 * g + 4, 0:n], mf[:, 4 * g:4 * g + 4, 0:n], rs_t[:, 0:1, 0:n].broadcast_to([128, 4, n]), ALU.mult),
                         reads=[B_mfg[g], B_rs[0]], writes=[B_mfg[g]])
                for g in range(4):
                    P.op("dve", f_tt(xblk[:, 4 * g:4 * g + 4, 0:n], xblk[:, 4 * g:4 * g + 4, 0:n], mf[:, 4 * g:4 * g + 4, 0:n], ALU.add),
                         reads=[B_xg[g], B_mfg[g]], writes=[B_xg[g]])
                b = sumsq_of(xblk, B_xg, n, NCH)
                rstd_from(b, n, 1, 1.0 / DM, RMS_EPS)
                for c in range(NCH):
                    P.op("dve", f_stt(h2[:, c, 0:n], xblk[:, c, 0:n], gcol(l, 2, c), rs_t[:, 1, 0:n], ALU.mult, ALU.mult),
                         reads=[B_xg[c // 4], B_rs[1], B_par], writes=[B_h2])
                ssb = None
                for qd in range(4):
                    for r in range(4):
                        wt, Bw = w_next((l, 14 + 8 * qd + r))
                        for cc in range(4):
                            hc = r * 4 + cc
                            b = next_bank()
                            items = [(bank(b)[:, 0:n], wt[:, kc, cc * 128:(cc + 1) * 128], h2[:, kc, 0:n], kc == 0, kc == 15) for kc in range(16)]
                            P.op("pe", f_mm(items), reads=[Bw, B_h2], writes=[B_bank[b]])
                            j = hc % 2
                            if j == 0:
                                P.op("act", f_act(sg_t[:, 0, 0:n], bank(b)[:, 0:n], AF.Relu), reads=[B_bank[b]], writes=[B_sg[0]])
                                P.op("act", f_act(hid[:, hc, 0:n], sg_t[:, 0, 0:n], AF.Square), reads=[B_sg[0]], writes=[B_hid])
                            else:
                                P.op("dve", f_ts(sg_t[:, 1, 0:n], bank(b)[:, 0:n], 0.0, None, ALU.max), reads=[B_bank[b]], writes=[B_sg[1]])
                                P.op("dve", f_tt(hid[:, hc, 0:n], sg_t[:, 1, 0:n], sg_t[:, 1, 0:n], ALU.mult), reads=[B_sg[1]], writes=[B_hid])
                    if qd == 3:
                        ssb = next_bank()
                    for r in range(4):
                        wt, Bw = w_next((l, 14 + 8 * qd + 4 + r))
                        for cc in range(4):
                            ch = r * 4 + cc
                            b = next_bank()
                            if b == ssb:
                                b = next_bank()
                            items = [(bank(b)[:, 0:n], wt[:, kc, cc * 128:(cc + 1) * 128], hid[:, kc, 0:n], kc == 0, kc == 15) for kc in range(16)]
                            P.op("pe", f_mm(items), reads=[Bw, B_hid], writes=[B_bank[b]])
                            Bg = B_mfg[ch // 4]
                            if qd == 0:
                                P.op("act", f_act(mf[:, ch, 0:n], bank(b)[:, 0:n], AF.Copy), reads=[B_bank[b]], writes=[Bg])
                            else:
                                P.op("dve", f_tt(mf[:, ch, 0:n], bank(b)[:, 0:n], mf[:, ch, 0:n], ALU.add), reads=[B_bank[b], Bg], writes=[Bg])
                            if qd == 3:
                                P.op("act", f_act(sqb[:, ch % 2, 0, 0:n], mf[:, ch, 0:n], AF.Square), reads=[Bg], writes=[B_sq[ch % 2]])
                                P.op("pe", f_mm([(bank(ssb)[:, 0:n], ones[:], sqb[:, ch % 2, 0, 0:n], ch == 0, ch == 15)]),
                                     reads=[B_sq[ch % 2], B_const], writes=[B_bank[ssb]])
                rstd_from(ssb, n, 2, 1.0 / DM, RMS_EPS)
                for c in range(NCH):
                    P.op("act", f_act(mf[:, c, 0:n], mf[:, c, 0:n], AF.Copy, scale=gcol(l, 3, c)), reads=[B_mfg[c // 4], B_par], writes=[B_mfg[c // 4]])
                for g in range(4):
                    P.op("dve", f_tt(mf[:, 4 * g:4 * g + 4, 0:n], mf[:, 4 * g:4 * g + 4, 0:n], rs_t[:, 2:3, 0:n].broadcast_to([128, 4, n]), ALU.mult),
                         reads=[B_mfg[g], B_rs[2]], writes=[B_mfg[g]])
                for g in range(4):
                    P.op("dve", f_tt(mf[:, 4 * g:4 * g + 4, 0:n], xblk[:, 4 * g:4 * g + 4, 0:n], mf[:, 4 * g:4 * g + 4, 0:n], ALU.add),
                         reads=[B_xg[g], B_mfg[g]], writes=[B_mfg[g]])
                P.op("pool", f_dma(xdst[:, :, t0:t0 + n].rearrange("c p t -> p c t"), mf[:, :, 0:n]),
                     reads=B_mfg, writes=gr(Gdst, t0, t0 + n), dsem=S_xst)

        queue_conversions(0)
        pump_conv(20)
        for l in range(NL):
            if stop == "conv":
                break
            phase1(l)
            if l + 1 < NL:
                queue_conversions(l + 1)
            if stop == "p1":
                break
            phase2(l)
            if stop == "p2":
                break
            phase3(l)
        pump_conv(10 ** 6)
        final = []
        for g in G_x["yout"]:
            final.extend(g.writers.items())
        for S in [S_us, S_qs, S_ks, S_vs, S_xst, S_misc] + conv_sems:
            final.append((S, S.count))
        P.wait_only("pool", final)
        P.wait_only("sync", final)

        with nc.Block() as block:
            @block.tensor
            def _(e):
                P.replay("pe", e)

            @block.scalar
            def _(e):
                P.replay("act", e)

            @block.vector
            def _(e):
                P.replay("dve", e)

            @block.gpsimd
            def _(e):
                P.replay("pool", e)

            @block.sync
            def _(e):
                P.replay("sync", e)
    info = dict(RIN=RIN, ROUT=ROUT, TIN=TIN, TOUT=TOUT, nops={k: len(v["ops"]) for k, v in P.eng.items()})
    return nc, info


def _bias_table(rpb, flipped):
    rows = SEQ // GW

    def true_rc(lr, lc):
        if flipped:
            return rows - 1 - lr, GW - 1 - lc
        return lr, lc
    tiles = [(0, 2 * j) for j in range(4)] + [(2, 2 * j) for j in range(4)] + [(20, 2 * j) for j in range(8, 13)]
    qi = np.arange(128)
    out = np.full((DEPTH, NHEAD, 128, NBT, 128), NEG, np.float32)
    for t, (qr0, kr0) in enumerate(tiles):
        qlr = qr0 + qi // GW
        qlc = qi % GW
        klr = kr0 + qi // GW
        klc = qi % GW
        qr, qc = true_rc(qlr, qlc)
        kr, kc = true_rc(klr, klc)
        rs = np.clip(qr - 4, 0, rows - 8)
        cs = np.clip(qc - 8, 0, GW - 16)
        valid = (kr[None, :] >= rs[:, None]) & (kr[None, :] < rs[:, None] + 8) & \
                (kc[None, :] >= cs[:, None]) & (kc[None, :] < cs[:, None] + 16)
        dr = np.clip(kr[None, :] - qr[:, None] + 7, 0, 14)
        dc = np.clip(kc[None, :] - qc[:, None], -15, 15) + 15
        g = rpb[:, :, dr, dc]
        out[:, :, :, t, :] = np.where(valid[None, None], g, np.float32(NEG))
    return out


_CACHE = {}


def kernel(x, w_in, w_dw, b_dw, conv_ln_g, conv_ln_b, rpb, w_out, w_up, w_down,
           pre_mix_g, post_mix_g, pre_mlp_g, post_mlp_g, _NL=DEPTH, _dump=(), _trace=False, _stop=None, _cores=8):
    NL = _NL
    f32 = np.float32
    x = np.asarray(x, f32)
    key = (NL, tuple(_dump), _stop)
    if key not in _CACHE:
        _CACHE[key] = build_program(NL, _dump, _stop)
    nc, info = _CACHE[key]
    T0 = info["TIN"][0]
    R0 = info["RIN"][0]
    w_in = np.ascontiguousarray(w_in, f32)
    w_out = np.ascontiguousarray(w_out, f32)
    w_up = np.ascontiguousarray(w_up, f32)
    w_down = np.ascontiguousarray(w_down, f32)
    g4 = np.stack([np.asarray(a, f32) for a in (pre_mix_g, post_mix_g, pre_mlp_g, post_mlp_g)], axis=1)
    gpack = np.ascontiguousarray(g4.reshape(DEPTH, 4, NCH, 128).transpose(3, 0, 1, 2)).reshape(128, -1)
    rpb = np.asarray(rpb, f32)
    packs = {}
    for flipped in (False, True):
        wd = np.asarray(w_dw, f32)
        if flipped:
            wd = wd[:, ::-1, :]
        cp = np.concatenate([wd.transpose(0, 2, 1),
                             np.asarray(b_dw, f32)[:, :, None],
                             np.asarray(conv_ln_g, f32)[:, :, None],
                             np.asarray(conv_ln_b, f32)[:, :, None]], axis=2)
        cp = cp.reshape(DEPTH, 8, 128, 34).transpose(2, 0, 1, 3)
        packs[flipped] = (np.ascontiguousarray(cp).reshape(128, -1), _bias_table(rpb, flipped))
    in_maps = []
    for c in range(_cores):
        b, half = divmod(c, 2)
        if half == 0:
            xt = x[b, 0:T0, :]
        else:
            xt = x[b, SEQ - T0:SEQ, :][::-1]
        xT = np.ascontiguousarray(xt.T).reshape(NCH, 128, T0)
        cpk, bt = packs[half == 1]
        in_maps.append({"xin": xT, "w_in": w_in, "w_out": w_out, "w_up": w_up, "w_down": w_down,
                        "gpack": gpack, "cpack": cpk, "biasg": bt})
    if _trace:
        res = run_bass_kernel_spmd(nc, in_maps, core_ids=list(range(_cores)), trace=True)
    else:
        res = run_bass_kernel_spmd(nc, in_maps, core_ids=list(range(_cores)))
    _CACHE["last"] = res
    out = np.zeros((4, SEQ, DM), f32)
    for c in range(_cores):
        b, half = divmod(c, 2)
        y = np.asarray(res.results[c]["yout"], f32).reshape(DM, 2048).T
        if half == 0:
            out[b, 0:2048] = y
        else:
            out[b, 2048:SEQ] = y[::-1]
    return out
```

```python
import numpy as np
from contextlib import ExitStack
import concourse.bass as bass
import concourse.mybir as mybir
from concourse.bass_utils import run_bass_kernel_spmd

F32 = mybir.dt.float32
BF16 = mybir.dt.bfloat16
I32 = mybir.dt.int32
AF = mybir.ActivationFunctionType
ALU = mybir.AluOpType

DEPTH = 4
DM = 2048
NCH = 16
SEQ = 4096
GW = 64
CW = 1024
NHEAD = 16
HD = 64
KCONV = 31
DFF = 8192
NBT = 13
RMS_EPS = 1e-6
LN_EPS = 1e-5
NEG = -30000.0
NS = 4
NTILE = 46


class Sem:
    def __init__(self, h):
        self.h = h
        self.count = 0


class Buf:
    def __init__(self, name):
        self.name = name
        self.writers = {}
        self.readers = {}
        self.war = {}
        self.aliases = []


def alias(*bufs):
    for a in bufs:
        for b in bufs:
            if a is not b and b not in a.aliases:
                a.aliases.append(b)


class Prog:
    def __init__(self, nc, sems):
        self.nc = nc
        self.eng = {}
        for name in ("pe", "act", "dve", "pool", "sync"):
            self.eng[name] = dict(ops=[], waited={}, sem=sems[name])

    def op(self, eng, fn, reads=(), writes=(), deps=(), dsem=None):
        E = self.eng[eng]
        rset = set(id(b) for b in reads)
        wset = set(id(b) for b in writes)
        hs = []
        for b in reads:
            hs.extend(b.writers.items())
        for b in writes:
            if b.readers or id(b) in rset:
                b.war = dict(b.readers)
                for k, v in b.writers.items():
                    b.war[k] = max(b.war.get(k, 0), v)
                b.readers = {}
                b.writers = {}
            for bb in b.aliases:
                for d in (bb.readers, bb.writers):
                    for k, v in d.items():
                        b.war[k] = max(b.war.get(k, 0), v)
            hs.extend(b.war.items())
        hs.extend(deps)
        need = {}
        for sem, val in hs:
            if sem is E["sem"] and eng == "pe":
                continue
            need[sem] = max(need.get(sem, 0), val)
        waits = []
        em = E.setdefault("emitted", {})
        for sem, val in need.items():
            if em.get(sem, 0) < val:
                em[sem] = val
                waits.append((sem, val))
        if dsem is not None:
            dsem.count += 16
            h = (dsem, dsem.count)
            inc = (dsem, 16)
        else:
            E["sem"].count += 1
            h = (E["sem"], E["sem"].count)
            inc = (E["sem"], 1)
        E["ops"].append((waits, fn, inc))
        for b in reads:
            if id(b) not in wset:
                b.readers[h[0]] = max(b.readers.get(h[0], 0), h[1])
        for b in writes:
            b.writers[h[0]] = max(b.writers.get(h[0], 0), h[1])
        return h

    def wait_only(self, eng, deps):
        E = self.eng[eng]
        waits = []
        for sem, val in deps:
            if E.setdefault("emitted", {}).get(sem, 0) < val:
                E["emitted"][sem] = val
                waits.append((sem, val))
        E["ops"].append((waits, None, None))

    def replay(self, name, e):
        for waits, fn, inc in self.eng[name]["ops"]:
            for sem, val in waits:
                e.wait_ge(sem.h, val)
            if fn is not None:
                ins = fn(e)
                ins.then_inc(inc[0].h, inc[1])


def f_dma(out, in_):
    return lambda e: e.dma_start(out=out, in_=in_)


def f_act(out, in_, func, **kw):
    return lambda e: e.activation(out=out, in_=in_, func=func, **kw)


def f_ts(out, in0, s1, s2, op0, op1=None):
    if op1 is None:
        return lambda e: e.tensor_scalar(out=out, in0=in0, scalar1=s1, scalar2=None, op0=op0)
    return lambda e: e.tensor_scalar(out=out, in0=in0, scalar1=s1, scalar2=s2, op0=op0, op1=op1)


def f_tt(out, in0, in1, op):
    return lambda e: e.tensor_tensor(out=out, in0=in0, in1=in1, op=op)


def f_stt(out, in0, scalar, in1, op0, op1):
    return lambda e: e.scalar_tensor_tensor(out=out, in0=in0, scalar=scalar, in1=in1, op0=op0, op1=op1)


def f_copy(out, in_):
    return lambda e: e.tensor_copy(out=out, in_=in_)


def f_recip(out, in_):
    return lambda e: e.reciprocal(out=out, in_=in_)


def f_memset(ap, val):
    return lambda e: e.memset(ap, val)


def f_mm(items):
    def fn(e):
        ins = None
        for (out, lhsT, rhs, start, stop) in items:
            ins = e.matmul(out, lhsT=lhsT, rhs=rhs, start=start, stop=stop)
        return ins
    return fn


def f_tr(out, in_, ident):
    return lambda e: e.transpose(out, in_, ident)


def blocks_of(T):
    res = []
    t = 0
    while t < T:
        n = 512 if T - t >= 512 else T - t
        res.append((t, n))
        t += n
    return res


def build_program(NL=DEPTH, dump=(), stop=None):
    nc = bass.Bass("TRN2", target_bir_lowering=False)
    RIN = [32 + 4 * (NL - l) for l in range(NL)]
    ROUT = [32 + 4 * (NL - 1 - l) for l in range(NL)]
    TIN = [r * GW for r in RIN]
    TOUT = [r * GW for r in ROUT]
    T0 = TIN[0]
    T1 = TOUT[0]

    def dram(name, shape, dt, kind="Internal"):
        if name in dump:
            kind = "ExternalOutput"
        return nc.dram_tensor(name, shape, dt, kind=kind).ap()

    xin = dram("xin", [NCH, 128, T0], F32, "ExternalInput")
    w_in = dram("w_in", [DEPTH, DM, 5120], F32, "ExternalInput")
    w_out = dram("w_out", [DEPTH, DM, DM], F32, "ExternalInput")
    w_up = dram("w_up", [DEPTH, DM, DFF], F32, "ExternalInput")
    w_down = dram("w_down", [DEPTH, DFF, DM], F32, "ExternalInput")
    gpack_d = dram("gpack", [128, DEPTH * 4 * NCH], F32, "ExternalInput")
    cpack_d = dram("cpack", [128, DEPTH * 8 * 34], F32, "ExternalInput")
    bias_d = dram("biasg", [DEPTH, NHEAD, 128, NBT, 128], F32, "ExternalInput")
    yout = dram("yout", [NCH, 128, TOUT[NL - 1]], F32, "ExternalOutput")
    xs = [dram("xs0", [NCH, 128, T1], F32), dram("xs1", [NCH, 128, T1], F32)]
    wsc = dram("wsc", [2, NTILE, 128, 8192], BF16)
    uT = dram("uT", [8, 128, T0 + 32], BF16)
    qT = dram("qT", [8, 128, T0], BF16)
    kT = dram("kT", [8, 128, T0], BF16)
    vS = dram("vS", [T0 // 128, 128, NHEAD * 65], BF16)
    mixT = dram("mixT", [NCH, 128, T1], BF16)

    es = ExitStack()
    with es:
        def sb(name, shape, dt):
            return es.enter_context(nc.sbuf_tensor(name, shape, dt))

        def newsem(name):
            return Sem(es.enter_context(nc.semaphore(name)))

        RW = sb("RW", [128, NS * 4096], F32)
        RX = sb("RX", [128, 8192], F32)
        RH = sb("RH", [128, 8192], F32)
        RS = sb("RS", [128, 8320], F32)
        RD = sb("RD", [128, 4096], F32)
        sqb = sb("sqb", [128, 2, 4, 512], BF16)
        rs_t = sb("rs_t", [128, 4, 512], F32)
        sg_t = sb("sg_t", [128, 2, 512], F32)
        ident = sb("ident", [128, 128], BF16)
        ones = sb("ones", [128, 128], BF16)
        iot = sb("iot", [128, 128], I32)
        zer = sb("zer", [128, 128], BF16)
        gpack = sb("gpack_s", [128, DEPTH, 4, NCH], F32)
        cpack = sb("cpack_s", [128, DEPTH, 8, 34], F32)
        psF = es.enter_context(nc.psum_tensor("psF", [128, 7 * 512], F32))
        psB = es.enter_context(nc.psum_tensor("psB", [128, 1024], BF16))

        def bfview(reg, off_w, n_w):
            return reg[:, off_w:off_w + n_w].bitcast(BF16)

        wslot = [bfview(RW, s * 4096, 4096) for s in range(NS)]
        B_w = [Buf(f"w{s}") for s in range(NS)]
        S_w = [newsem(f"sw{s}") for s in range(NS)]
        xblk = RX[:, :].rearrange("p (c n) -> p c n", c=NCH)
        B_xg = [Buf(f"x{g}") for g in range(4)]
        acc = RX[:, 0:4096].rearrange("p (c n) -> p c n", c=8)
        B_acc = [Buf(f"acc{c}") for c in range(8)]
        ublk = bfview(RX, 4096, 2176)[:, 0:8 * 542].rearrange("p (c n) -> p c n", c=8)
        B_u = Buf("ublk")
        S_u = newsem("su")
        tmpc = RX[:, 6272:6272 + 1024].rearrange("p (c n) -> p c n", c=2)
        B_tmpc = [Buf("tmpc0"), Buf("tmpc1")]
        for a in B_acc + B_tmpc + [B_u]:
            a.aliases = list(B_xg)
        for a in B_xg:
            a.aliases = B_acc + B_tmpc + [B_u]
        S_x = newsem("sx")
        hT = [bfview(RH, i * 4096, 4096).rearrange("p (c n) -> p c n", c=NCH) for i in range(2)]
        B_hT = [Buf("hT0"), Buf("hT1")]
        mixb = hT[0]
        B_mix = B_hT[0]
        S_mix = newsem("smix")
        h2 = hT[1]
        B_h2 = B_hT[1]
        RHb = RH[:, :].bitcast(BF16)
        o = 0
        kblk = []
        for i in range(2):
            kblk.append(RHb[:, o:o + 1024]); o += 1024
        vblk = []
        for i in range(2):
            vblk.append(RHb[:, o:o + 1040].rearrange("p (j h d) -> p j h d", j=8, h=2)); o += 1056
        qA = []
        qB = []
        for i in range(2):
            qA.append(RHb[:, o:o + 512]); o += 512
            qB.append(RHb[:, o:o + 512]); o += 512
        biasb = []
        for i in range(2):
            biasb.append(RHb[:, o:o + 2 * NBT * 128].rearrange("p (h t k) -> p h t k", h=2, t=NBT)); o += 2 * NBT * 128
        pT = []
        for i in range(2):
            pT.append(RHb[:, o:o + 640]); o += 640
        onb = []
        for i in range(2):
            onb.append(RHb[:, o:o + 128]); o += 128
        assert o <= 16384, o
        B_k = [Buf("k0"), Buf("k1")]
        B_v = [Buf("v0"), Buf("v1")]
        B_qA = [Buf("qA0"), Buf("qA1")]
        B_qB = [Buf("qB0"), Buf("qB1")]
        B_bias = [Buf("bias0"), Buf("bias1")]
        B_pT = [Buf("pT0"), Buf("pT1")]
        B_on = [Buf("on0"), Buf("on1")]
        S_k = [newsem("sk0"), newsem("sk1")]
        S_v = [newsem("sv0"), newsem("sv1")]
        S_qA = [newsem("sqa0"), newsem("sqa1")]
        S_qB = [newsem("sqb0"), newsem("sqb1")]
        S_bias = [newsem("sbi0"), newsem("sbi1")]
        attn_bufs = B_k + B_v + B_qA + B_qB + B_bias + B_pT + B_on
        for a in attn_bufs:
            a.aliases = list(B_hT)
        for a in B_hT:
            a.aliases = list(attn_bufs)
        RSb = RS[:, :].bitcast(BF16)
        ustage = RSb[:, 0:4096].rearrange("p (c n) -> p c n", c=8)
        qstage = RSb[:, 4096:8192].rearrange("p (c n) -> p c n", c=8)
        kstage = RSb[:, 8192:12288].rearrange("p (c n) -> p c n", c=8)
        vstage = RSb[:, 12288:12288 + 4160].rearrange("p (t h d) -> p t h d", t=4, h=NHEAD)
        mf = RS[:, 0:8192].rearrange("p (c n) -> p c n", c=NCH)
        B_us, B_qs, B_ks, B_vs = Buf("ustage"), Buf("qstage"), Buf("kstage"), Buf("vstage")
        S_us, S_qs, S_ks, S_vs = newsem("sus"), newsem("sqs"), newsem("sks"), newsem("svs")
        B_mfg = [Buf(f"mf{g}") for g in range(4)]
        for a in (B_us, B_qs, B_ks, B_vs):
            a.aliases = list(B_mfg)
        for a in B_mfg:
            a.aliases = [B_us, B_qs, B_ks, B_vs]
        S_xst = newsem("sxst")
        hid = RD[:, :].bitcast(BF16).rearrange("p (c n) -> p c n", c=16)
        B_hid = Buf("hid")
        RDb = RD[:, :].bitcast(BF16)
        dg = [RDb[:, i * 3968:(i + 1) * 3968].rearrange("p (j k) -> p j k", j=KCONV) for i in range(2)]
        B_dg = [Buf("dg0"), Buf("dg1")]
        for a in B_dg:
            a.aliases = [B_hid]
        B_hid.aliases = list(B_dg)
        B_sq = [Buf("sq0"), Buf("sq1")]
        B_rs = [Buf(f"rs{i}") for i in range(4)]
        B_sg = [Buf("sg0"), Buf("sg1")]
        B_const = Buf("const")
        B_par = Buf("params")
        S_par = newsem("spar")
        S_misc = newsem("smisc")
        def bank(b):
            return psF[:, b * 512:(b + 1) * 512]
        B_bank = [Buf(f"bank{b}") for b in range(7)]
        B_bT = Buf("bankT")
        Obank = psB[:, 512:1024].bitcast(F32)
        rot = [0]

        def next_bank():
            b = rot[0]
            rot[0] = (b + 1) % 7
            return b

        sems = {n: newsem("e_" + n) for n in ("pe", "act", "dve", "pool", "sync")}
        P = Prog(nc, sems)
        conv_sems = [newsem(f"cv{i}") for i in range(8)]
        conv_hist = []

        def gran(name, n):
            return [Buf(f"{name}_g{i}") for i in range(n)]
        NG = T0 // 256
        G_x = {"xin": gran("xin", NG), 0: gran("xs0", NG), 1: gran("xs1", NG), "yout": gran("yout", NG)}
        G_u = gran("uT", NG + 1)
        G_q = gran("qT", NG)
        G_k = gran("kT", NG)
        G_v = gran("vS", NG)
        G_myc = gran("myc", NG)
        G_mya = gran("mya", NG)
        B_wsc = [[[Buf(f"wsc{p}_{t}_{h}") for h in range(2)] for t in range(NTILE)] for p in range(2)]

        def gr(G, lo, hi):
            return G[lo // 256:(hi + 255) // 256]

        P.op("pool", lambda e: e.iota(iot[:], pattern=[[1, 128]], base=0, channel_multiplier=-1), writes=[B_const])
        P.op("dve", f_ts(ident[:], iot[:], 0.0, None, ALU.is_equal), reads=[B_const], writes=[B_const])
        P.op("dve", f_memset(ones[:], 1.0), writes=[B_const])
        P.op("dve", f_memset(zer[:], 0.0), writes=[B_const])
        P.op("sync", f_dma(gpack[:].rearrange("p a b c -> p (a b c)"), gpack_d), writes=[B_par], dsem=S_par)
        P.op("sync", f_dma(cpack[:].rearrange("p a b c -> p (a b c)"), cpack_d), writes=[B_par], dsem=S_par)
        for c in range(8):
            P.op("pool", f_dma(uT[c, :, 0:15], zer[:, 0:15]), reads=[B_const], writes=[G_u[0]], dsem=S_misc)

        def tile_src(l, tid, half):
            par = l % 2
            dst = wsc[par, tid].rearrange("p (k n) -> p k n", k=16)
            k0, k1 = half * 8, half * 8 + 8
            res = []
            if tid < 4:
                for (cb, d0) in ((2 * tid * 128, 0), (1024 + 2 * tid * 128, 256)):
                    src = w_in[l, k0 * 128:k1 * 128, cb:cb + 256].rearrange("(k p) n -> p k n", p=128)
                    res.append((dst[:, k0:k1, d0:d0 + 256], src))
            elif tid < 10:
                cb = 2048 + (tid - 4) * 512
                src = w_in[l, k0 * 128:k1 * 128, cb:cb + 512].rearrange("(k p) n -> p k n", p=128)
                res.append((dst[:, k0:k1, :], src))
            elif tid < 14:
                cb = (tid - 10) * 512
                src = w_out[l, k0 * 128:k1 * 128, cb:cb + 512].rearrange("(k p) n -> p k n", p=128)
                res.append((dst[:, k0:k1, :], src))
            else:
                qd, r = divmod(tid - 14, 8)
                if r < 4:
                    cb = qd * 2048 + r * 512
                    src = w_up[l, k0 * 128:k1 * 128, cb:cb + 512].rearrange("(k p) n -> p k n", p=128)
                else:
                    cb = (r - 4) * 512
                    rb = qd * 2048
                    src = w_down[l, rb + k0 * 128:rb + k1 * 128, cb:cb + 512].rearrange("(k p) n -> p k n", p=128)
                res.append((dst[:, k0:k1, :], src))
            return res

        pending_conv = []

        def queue_conversions(l):
            for tid in range(NTILE):
                for half in range(2):
                    pending_conv.append((l, tid, half))

        def pump_conv(k):
            for _ in range(k):
                if not pending_conv:
                    return
                l, tid, half = pending_conv.pop(0)
                for (dst, src) in tile_src(l, tid, half):
                    i = len(conv_hist)
                    cs = conv_sems[i % 8]
                    deps = [conv_hist[i - 8]] if i >= 8 else []
                    h = P.op("pool", f_dma(dst, src), writes=[B_wsc[l % 2][tid][half]], deps=deps, dsem=cs)
                    conv_hist.append(h)

        wtiles = []
        for l in range(NL):
            for _ in blocks_of(TIN[l]):
                wtiles.extend((l, t) for t in range(10))
            for _ in blocks_of(TOUT[l]):
                wtiles.extend((l, t) for t in range(10, NTILE))
        wstate = dict(next_load=0, next_use=0)

        def w_load_one():
            i = wstate["next_load"]
            if i >= len(wtiles):
                return
            l, tid = wtiles[i]
            while any(pc[0] == l and pc[1] == tid for pc in pending_conv):
                pump_conv(1)
            s = i % NS
            P.op("sync", f_dma(wslot[s], wsc[l % 2, tid]), reads=B_wsc[l % 2][tid], writes=[B_w[s]], dsem=S_w[s])
            wstate["next_load"] = i + 1

        def w_next(expect):
            i = wstate["next_use"]
            assert wtiles[i] == expect, (wtiles[i], expect)
            while wstate["next_load"] < min(i + NS, len(wtiles)):
                if wstate["next_load"] >= i + NS - 1 and i >= 1:
                    pass
                w_load_one()
            wstate["next_use"] = i + 1
            pump_conv(1)
            s = i % NS
            return wslot[s].rearrange("p (k n) -> p k n", k=16), B_w[s]

        def rstd_from(bank_b, n, ridx, scale, eps):
            r = rs_t[:, ridx, 0:n]
            P.op("dve", f_ts(r, bank(bank_b)[:, 0:n], scale, eps, ALU.mult, ALU.add), reads=[B_bank[bank_b]], writes=[B_rs[ridx]])
            P.op("act", f_act(r, r, AF.Sqrt), reads=[B_rs[ridx]], writes=[B_rs[ridx]])
            P.op("dve", f_recip(r, r), reads=[B_rs[ridx]], writes=[B_rs[ridx]])

        def sumsq_of(src3, B_src, n, nchunks):
            b = next_bank()
            ng = nchunks // 4
            for g in range(ng):
                sq = sqb[:, g % 2, :, 0:n]
                P.op("act", f_act(sq, src3[:, 4 * g:4 * g + 4, 0:n], AF.Square), reads=[B_src[g]], writes=[B_sq[g % 2]])
                items = [(bank(b)[:, 0:n], ones[:], sqb[:, g % 2, j, 0:n], (g == 0 and j == 0), (g == ng - 1 and j == 3)) for j in range(4)]
                P.op("pe", f_mm(items), reads=[B_sq[g % 2], B_const], writes=[B_bank[b]])
            return b

        def gcol(l, k, c):
            return gpack[:, l, k, c:c + 1]

        def phase1(l):
            xsrc, Gsrc = (xin, G_x["xin"]) if l == 0 else (xs[(l - 1) % 2], G_x[(l - 1) % 2])
            P.op("dve", f_memset(vstage[:, :, :, 64:65], 1.0), writes=[B_vs])
            blks = blocks_of(TIN[l])

            def load_block(bi):
                t0, n = blks[bi]
                P.op("sync", f_dma(xblk[:, :, 0:n], xsrc[:, :, t0:t0 + n].rearrange("c p t -> p c t")),
                     reads=gr(Gsrc, t0, t0 + n), writes=B_xg, dsem=S_x)

            def norm_block(bi):
                t0, n = blks[bi]
                par = bi % 2
                b = sumsq_of(xblk, B_xg, n, NCH)
                rstd_from(b, n, 0, 1.0 / DM, RMS_EPS)
                for c in range(NCH):
                    P.op("dve", f_stt(hT[par][:, c, 0:n], xblk[:, c, 0:n], gcol(l, 0, c), rs_t[:, 0, 0:n], ALU.mult, ALU.mult),
                         reads=[B_xg[c // 4], B_rs[0], B_par], writes=[B_hT[par]])
                if bi + 1 < len(blks):
                    load_block(bi + 1)

            load_block(0)
            norm_block(0)
            for bi, (t0, n) in enumerate(blks):
                par = bi % 2
                nt = n // 128
                for w in range(10):
                    wt, Bw = w_next((l, w))
                    if w == 5 and bi + 1 < len(blks):
                        norm_block(bi + 1)
                    if w < 8:
                        bs = [next_bank() for _ in range(4)]
                        for cc in range(4):
                            items = [(bank(bs[cc])[:, 0:n], wt[:, kc, cc * 128:(cc + 1) * 128], hT[par][:, kc, 0:n], kc == 0, kc == 15) for kc in range(16)]
                            P.op("pe", f_mm(items), reads=[Bw, B_hT[par]], writes=[B_bank[bs[cc]]])
                        if w < 4:
                            for j in range(2):
                                P.op("act", f_act(sg_t[:, j, 0:n], bank(bs[2 + j])[:, 0:n], AF.Sigmoid), reads=[B_bank[bs[2 + j]]], writes=[B_sg[j]])
                                P.op("dve", f_tt(ustage[:, 2 * w + j, 0:n], bank(bs[j])[:, 0:n], sg_t[:, j, 0:n], ALU.mult),
                                     reads=[B_bank[bs[j]], B_sg[j]], writes=[B_us])
                        else:
                            stg, Bs = (qstage, B_qs) if w < 6 else (kstage, B_ks)
                            sc = 0.125 if w < 6 else 1.0
                            for cc in range(4):
                                ch = (w % 2) * 4 + cc
                                if cc % 2 == 0:
                                    P.op("act", f_act(stg[:, ch, 0:n], bank(bs[cc])[:, 0:n], AF.Copy, scale=sc), reads=[B_bank[bs[cc]]], writes=[Bs])
                                else:
                                    P.op("dve", f_ts(stg[:, ch, 0:n], bank(bs[cc])[:, 0:n], sc, None, ALU.mult), reads=[B_bank[bs[cc]]], writes=[Bs])
                    else:
                        hb = (w - 8) * 8
                        for tt in range(nt):
                            b2 = next_bank()
                            items = [(bank(b2)[:, :], hT[par][:, kc, tt * 128:(tt + 1) * 128], wt[:, kc, :], kc == 0, kc == 15) for kc in range(16)]
                            P.op("pe", f_mm(items), reads=[Bw, B_hT[par]], writes=[B_bank[b2]])
                            src = bank(b2)[:, :].rearrange("p (h d) -> p h d", d=64)
                            dst = vstage[:, tt, hb:hb + 8, 0:64]
                            if tt % 2 == 0:
                                P.op("act", f_act(dst, src, AF.Copy), reads=[B_bank[b2]], writes=[B_vs])
                            else:
                                P.op("dve", f_copy(dst, src), reads=[B_bank[b2]], writes=[B_vs])
                    if w == 3:
                        P.op("pool", f_dma(uT[:, :, 15 + t0:15 + t0 + n].rearrange("c p t -> p c t"), ustage[:, :, 0:n]),
                             reads=[B_us], writes=gr(G_u, t0, t0 + n), dsem=S_us)
                    if w == 5:
                        P.op("pool", f_dma(qT[:, :, t0:t0 + n].rearrange("c p t -> p c t"), qstage[:, :, 0:n]),
                             reads=[B_qs], writes=gr(G_q, t0, t0 + n), dsem=S_qs)
                    if w == 7:
                        P.op("pool", f_dma(kT[:, :, t0:t0 + n].rearrange("c p t -> p c t"), kstage[:, :, 0:n]),
                             reads=[B_ks], writes=gr(G_k, t0, t0 + n), dsem=S_ks)
                    if w == 9:
                        P.op("pool", f_dma(vS[t0 // 128:t0 // 128 + nt].rearrange("t p f -> p t f"),
                                           vstage[:, 0:nt].rearrange("p t h d -> p t (h d)")),
                             reads=[B_vs], writes=gr(G_v, t0, t0 + n), dsem=S_vs)

        def conv_gen(l, t0, n):
            P.op("sync", f_dma(ublk[:, :, 0:n + 30], uT[:, :, t0:t0 + n + 30].rearrange("c p t -> p c t")),
                 reads=gr(G_u, t0, t0 + n + 30), writes=[B_u], dsem=S_u)
            yield
            for c in range(8):
                dp = c % 2
                P.op("dve", f_tt(dg[dp][:, :, :], ident[:, :].unsqueeze(1).broadcast_to([128, KCONV, 128]),
                                 cpack[:, l, c, 0:KCONV].unsqueeze(2).broadcast_to([128, KCONV, 128]), ALU.mult),
                     reads=[B_const, B_par], writes=[B_dg[dp]])
                yield
                items = [(bank(4)[:, 0:n], dg[dp][:, j, :], ublk[:, c, j:j + n], j == 0, j == KCONV - 1) for j in range(KCONV)]
                P.op("pe", f_mm(items), reads=[B_dg[dp], B_u], writes=[B_bank[4]])
                P.op("act", f_act(acc[:, c, 0:n], bank(4)[:, 0:n], AF.Identity, bias=cpack[:, l, c, 31:32]),
                     reads=[B_bank[4], B_par], writes=[B_acc[c]])
                yield
            for g in range(2):
                cb = sqb[:, 0, :, 0:n]
                P.op("act", f_act(cb, acc[:, 4 * g:4 * g + 4, 0:n], AF.Copy), reads=B_acc[4 * g:4 * g + 4], writes=[B_sq[0]])
                items = [(bank(5)[:, 0:n], ones[:], sqb[:, 0, j, 0:n], (g == 0 and j == 0), (g == 1 and j == 3)) for j in range(4)]
                P.op("pe", f_mm(items), reads=[B_sq[0], B_const], writes=[B_bank[5]])
                sq = sqb[:, 1, :, 0:n]
                P.op("act", f_act(sq, acc[:, 4 * g:4 * g + 4, 0:n], AF.Square), reads=B_acc[4 * g:4 * g + 4], writes=[B_sq[1]])
                items = [(bank(6)[:, 0:n], ones[:], sqb[:, 1, j, 0:n], (g == 0 and j == 0), (g == 1 and j == 3)) for j in range(4)]
                P.op("pe", f_mm(items), reads=[B_sq[1], B_const], writes=[B_bank[6]])
                yield
            mean = rs_t[:, 1, 0:n]
            msq = rs_t[:, 2, 0:n]
            rst = rs_t[:, 3, 0:n]
            P.op("dve", f_ts(mean, bank(5)[:, 0:n], 1.0 / CW, None, ALU.mult), reads=[B_bank[5]], writes=[B_rs[1]])
            P.op("dve", f_tt(msq, mean, mean, ALU.mult), reads=[B_rs[1]], writes=[B_rs[2]])
            P.op("dve", f_stt(rst, bank(6)[:, 0:n], 1.0 / CW, msq, ALU.mult, ALU.subtract), reads=[B_bank[6], B_rs[2]], writes=[B_rs[3]])
            P.op("dve", f_ts(rst, rst, LN_EPS, None, ALU.add), reads=[B_rs[3]], writes=[B_rs[3]])
            P.op("act", f_act(rst, rst, AF.Sqrt), reads=[B_rs[3]], writes=[B_rs[3]])
            P.op("dve", f_recip(rst, rst), reads=[B_rs[3]], writes=[B_rs[3]])
            yield
            for c in range(8):
                tb = c % 2
                P.op("dve", f_tt(tmpc[:, tb, 0:n], acc[:, c, 0:n], mean, ALU.subtract), reads=[B_acc[c], B_rs[1]], writes=[B_tmpc[tb]])
                if c >= 1:
                    cp = c - 1
                    P.op("dve", f_tt(acc[:, cp, 0:n], tmpc[:, cp % 2, 0:n], rst, ALU.mult), reads=[B_tmpc[cp % 2], B_rs[3]], writes=[B_acc[cp]])
                    P.op("act", f_act(ustage[:, cp, 0:n], acc[:, cp, 0:n], AF.Silu, scale=cpack[:, l, cp, 32:33], bias=cpack[:, l, cp, 33:34]),
                         reads=[B_acc[cp], B_par], writes=[B_us])
                yield
            cp = 7
            P.op("dve", f_tt(acc[:, cp, 0:n], tmpc[:, cp % 2, 0:n], rst, ALU.mult), reads=[B_tmpc[cp % 2], B_rs[3]], writes=[B_acc[cp]])
            P.op("act", f_act(ustage[:, cp, 0:n], acc[:, cp, 0:n], AF.Silu, scale=cpack[:, l, cp, 32:33], bias=cpack[:, l, cp, 33:34]),
                 reads=[B_acc[cp], B_par], writes=[B_us])
            P.op("pool", f_dma(mixT[0:8, :, t0:t0 + n].rearrange("c p t -> p c t"), ustage[:, :, 0:n]),
                 reads=[B_us], writes=gr(G_myc, t0, t0 + n), dsem=S_us)
            yield

        def phase2(l):
            for i in range(2):
                P.op("dve", f_memset(qA[i][64:128, :], 0.0), writes=[B_qA[i]])
                P.op("dve", f_memset(qB[i][0:64, :], 0.0), writes=[B_qB[i]])
            it = 0
            for (t0, n) in blocks_of(TOUT[l]):
                cg = conv_gen(l, t0, n)
                nq = n // 128
                i0 = t0 // 128
                jlo = max(0, i0 - 2)
                jhi = max(i0 + nq - 1 + 2, 3)
                nk = jhi - jlo + 1
                edge = (i0 == 0)
                tlo, ntl = (0, NBT) if edge else (8, 5)
                total_conv_steps = 40
                per_step = (total_conv_steps + 8 * nq - 1) // (8 * nq) + 1
                for hp in range(8):
                    par = hp % 2
                    P.op("sync", f_dma(kblk[par][:, 0:nk * 128], kT[hp, :, jlo * 128:(jlo + nk) * 128]),
                         reads=gr(G_k, jlo * 128, (jlo + nk) * 128), writes=[B_k[par]], dsem=S_k[par])
                    P.op("sync", f_dma(vblk[par][:, 0:nk].rearrange("p j h d -> p j (h d)"),
                                       vS[jlo:jlo + nk, :, 2 * hp * 65:(2 * hp + 2) * 65].rearrange("j p f -> p j f")),
                         reads=gr(G_v, jlo * 128, (jlo + nk) * 128), writes=[B_v[par]], dsem=S_v[par])
                    P.op("sync", f_dma(qA[par][0:64, 0:n], qT[hp, 0:64, t0:t0 + n]), reads=gr(G_q, t0, t0 + n), writes=[B_qA[par]], dsem=S_qA[par])
                    P.op("sync", f_dma(qB[par][64:128, 0:n], qT[hp, 64:128, t0:t0 + n]), reads=gr(G_q, t0, t0 + n), writes=[B_qB[par]], dsem=S_qB[par])
                    P.op("pool", f_dma(biasb[par][:, :, 0:ntl, :], bias_d[l, 2 * hp:2 * hp + 2, :, tlo:tlo + ntl, :].rearrange("h q t k -> q h t k")),
                         writes=[B_bias[par]], dsem=S_bias[par])
                    pump_conv(1)
                    for qi in range(nq):
                        i = i0 + qi
                        if i < 2:
                            js = [0, 1, 2, 3]
                            tb = 0 if i == 0 else 4
                        else:
                            js = list(range(i - 2, i + 3))
                            tb = 8
                        tb -= tlo
                        onp = it % 2
                        for head in range(2):
                            sp = head
                            sb0 = 2 * sp
                            qsel, Bq = (qA[par], B_qA[par]) if head == 0 else (qB[par], B_qB[par])
                            items = []
                            for jj, j in enumerate(js):
                                o_ap = psF[:, sb0 * 512 + jj * 128: sb0 * 512 + (jj + 1) * 128]
                                items.append((o_ap, kblk[par][:, (j - jlo) * 128:(j - jlo + 1) * 128], qsel[:, qi * 128:(qi + 1) * 128], True, False))
                                items.append((o_ap, biasb[par][:, head, tb + jj, :], ident[:], False, True))
                            P.op("pe", f_mm(items), reads=[B_k[par], Bq, B_bias[par], B_const], writes=[B_bank[sb0], B_bank[sb0 + 1]])
                            nkk = len(js) * 128
                            P.op("act", f_act(pT[sp][:, 0:min(nkk, 512)], psF[:, sb0 * 512:sb0 * 512 + min(nkk, 512)], AF.Exp),
                                 reads=[B_bank[sb0]], writes=[B_pT[sp]])
                            if nkk > 512:
                                P.op("act", f_act(pT[sp][:, 512:nkk], psF[:, sb0 * 512 + 512:sb0 * 512 + nkk], AF.Exp),
                                     reads=[B_bank[sb0 + 1]], writes=[B_pT[sp]])
                            items = []
                            for jj, j in enumerate(js):
                                items.append((Obank[:, head * 65:(head + 1) * 65], pT[sp][:, jj * 128:(jj + 1) * 128], vblk[par][:, j - jlo, head, :],
                                              jj == 0, jj == len(js) - 1))
                            P.op("pe", f_mm(items), reads=[B_pT[sp], B_v[par]], writes=[B_bT])
                        rc = rs_t[:, 0, 0:2]
                        P.op("dve", f_recip(rc, Obank[:, 0:130].rearrange("p (h d) -> p h d", h=2)[:, :, 64]), reads=[B_bT], writes=[B_rs[0]])
                        for head in range(2):
                            P.op("dve", f_ts(onb[onp][:, head * 64:(head + 1) * 64], Obank[:, head * 65:head * 65 + 64], rs_t[:, 0, head:head + 1], None, ALU.mult),
                                 reads=[B_bT, B_rs[0]], writes=[B_on[onp]])
                        P.op("pe", f_tr(psB[:, 0:128], onb[onp][:, :], ident[:]), reads=[B_on[onp], B_const], writes=[B_bT])
                        P.op("act", f_act(qstage[:, hp, qi * 128:(qi + 1) * 128], psB[:, 0:128], AF.Copy), reads=[B_bT], writes=[B_qs])
                        it += 1
                        for _ in range(per_step):
                            next(cg, None)
                for _ in cg:
                    pass
                P.op("pool", f_dma(mixT[8:16, :, t0:t0 + n].rearrange("c p t -> p c t"), qstage[:, :, 0:n]),
                     reads=[B_qs], writes=gr(G_mya, t0, t0 + n), dsem=S_qs)

        def phase3(l):
            xsrc, Gsrc = (xin, G_x["xin"]) if l == 0 else (xs[(l - 1) % 2], G_x[(l - 1) % 2])
            if l == NL - 1:
                xdst, Gdst = yout, G_x["yout"]
            else:
                xdst, Gdst = xs[l % 2], G_x[l % 2]
            blks = blocks_of(TOUT[l])

            def load_mix(bi):
                t0, n = blks[bi]
                P.op("sync", f_dma(mixb[:, :, 0:n], mixT[:, :, t0:t0 + n].rearrange("c p t -> p c t")),
                     reads=gr(G_myc, t0, t0 + n) + gr(G_mya, t0, t0 + n), writes=[B_mix], dsem=S_mix)

            load_mix(0)
            for bi, (t0, n) in enumerate(blks):
                P.op("sync", f_dma(xblk[:, :, 0:n], xsrc[:, :, t0:t0 + n].rearrange("c p t -> p c t")),
                     reads=gr(Gsrc, t0, t0 + n), writes=B_xg, dsem=S_x)
                ssb = next_bank()
                for w in range(4):
                    wt, Bw = w_next((l, 10 + w))
                    for cc in range(4):
                        ch = w * 4 + cc
                        b = next_bank()
                        if b == ssb:
                            b = next_bank()
                        items = [(bank(b)[:, 0:n], wt[:, kc, cc * 128:(cc + 1) * 128], mixb[:, kc, 0:n], kc == 0, kc == 15) for kc in range(16)]
                        P.op("pe", f_mm(items), reads=[Bw, B_mix], writes=[B_bank[b]])
                        P.op("act", f_act(mf[:, ch, 0:n], bank(b)[:, 0:n], AF.Copy, scale=gcol(l, 1, ch)), reads=[B_bank[b], B_par], writes=[B_mfg[ch // 4]])
                        P.op("act", f_act(sqb[:, ch % 2, 0, 0:n], bank(b)[:, 0:n], AF.Square), reads=[B_bank[b]], writes=[B_sq[ch % 2]])
                        if ch >= 1:
                            cp = ch - 1
                            P.op("pe", f_mm([(bank(ssb)[:, 0:n], ones[:], sqb[:, cp % 2, 0, 0:n], cp == 0, cp == 15)]),
                                 reads=[B_sq[cp % 2], B_const], writes=[B_bank[ssb]])
                P.op("pe", f_mm([(bank(ssb)[:, 0:n], ones[:], sqb[:, 1, 0, 0:n], False, True)]),
                     reads=[B_sq[1], B_const], writes=[B_bank[ssb]])
                if bi + 1 < len(blks):
                    load_mix(bi + 1)
                rstd_from(ssb, n, 0, 1.0 / DM, RMS_EPS)
                for g in range(4):
                    P.op("dve", f_tt(mf[:, 4 * g:4 * g + 4, 0:n], mf[:, 4 * g:4 * g + 4, 0:n], rs_t[:, 0:1, 0:n].broadcast_to([128, 4, n]), ALU.mult),
                         reads=[B_mfg[g], B_rs[0]], writes=[B_mfg[g]])
                for g in range(4):
                    P.op("dve", f_tt(xblk[:, 4 * g:4 * g + 4, 0:n], xblk[:, 4 * g:4 * g + 4, 0:n], mf[:, 4 * g:4 * g + 4, 0:n], ALU.add),
                         reads=[B_xg[g], B_mfg[g]], writes=[B_xg[g]])
                for c in range(NCH):
                    P.op("dve", f_ts(h2[:, c, 0:n], xblk[:, c, 0:n], gcol(l, 2, c), None, ALU.mult),
                         reads=[B_xg[c // 4], B_par], writes=[B_h2])
                b = sumsq_of(xblk, B_xg, n, NCH)
                rstd_from(b, n, 1, 1.0 / DM, RMS_EPS)
                r2 = rs_t[:, 1, 0:n]
                r4 = rs_t[:, 3, 0:n]
                P.op("dve", f_tt(r2, r2, r2, ALU.mult), reads=[B_rs[1]], writes=[B_rs[1]])
                P.op("dve", f_tt(r4, r2, r2, ALU.mult), reads=[B_rs[1]], writes=[B_rs[3]])
                ssb = None
                for qd in range(4):
                    for r in range(4):
                        wt, Bw = w_next((l, 14 + 8 * qd + r))
                        for cc in range(4):
                            hc = r * 4 + cc
                            b = next_bank()
                            items = [(bank(b)[:, 0:n], wt[:, kc, cc * 128:(cc + 1) * 128], h2[:, kc, 0:n], kc == 0, kc == 15) for kc in range(16)]
                            P.op("pe", f_mm(items), reads=[Bw, B_h2], writes=[B_bank[b]])
                            j = hc % 2
                            if j == 0:
                                P.op("act", f_act(sg_t[:, 0, 0:n], bank(b)[:, 0:n], AF.Relu), reads=[B_bank[b]], writes=[B_sg[0]])
                                P.op("act", f_act(hid[:, hc, 0:n], sg_t[:, 0, 0:n], AF.Square), reads=[B_sg[0]], writes=[B_hid])
                            else:
                                P.op("dve", f_ts(sg_t[:, 1, 0:n], bank(b)[:, 0:n], 0.0, None, ALU.max), reads=[B_bank[b]], writes=[B_sg[1]])
                                P.op("dve", f_tt(hid[:, hc, 0:n], sg_t[:, 1, 0:n], sg_t[:, 1, 0:n], ALU.mult), reads=[B_sg[1]], writes=[B_hid])
                    if qd == 3:
                        ssb = next_bank()
                    for r in range(4):
                        wt, Bw = w_next((l, 14 + 8 * qd + 4 + r))
                        for cc in range(4):
                            ch = r * 4 + cc
                            b = next_bank()
                            if b == ssb:
                                b = next_bank()
                            items = [(bank(b)[:, 0:n], wt[:, kc, cc * 128:(cc + 1) * 128], hid[:, kc, 0:n], kc == 0, kc == 15) for kc in range(16)]
                            P.op("pe", f_mm(items), reads=[Bw, B_hid], writes=[B_bank[b]])
                            Bg = B_mfg[ch // 4]
                            if qd == 0:
                                P.op("act", f_act(mf[:, ch, 0:n], bank(b)[:, 0:n], AF.Copy), reads=[B_bank[b]], writes=[Bg])
                            else:
                                P.op("dve", f_tt(mf[:, ch, 0:n], bank(b)[:, 0:n], mf[:, ch, 0:n], ALU.add), reads=[B_bank[b], Bg], writes=[Bg])
                            if qd == 3:
                                P.op("act", f_act(sqb[:, ch % 2, 0, 0:n], mf[:, ch, 0:n], AF.Square), reads=[Bg], writes=[B_sq[ch % 2]])
                                if ch >= 1:
                                    cp = ch - 1
                                    P.op("pe", f_mm([(bank(ssb)[:, 0:n], ones[:], sqb[:, cp % 2, 0, 0:n], cp == 0, cp == 15)]),
                                         reads=[B_sq[cp % 2], B_const], writes=[B_bank[ssb]])
                P.op("pe", f_mm([(bank(ssb)[:, 0:n], ones[:], sqb[:, 1, 0, 0:n], False, True)]),
                     reads=[B_sq[1], B_const], writes=[B_bank[ssb]])
                r3 = rs_t[:, 2, 0:n]
                P.op("dve", f_stt(r3, bank(ssb)[:, 0:n], 1.0 / DM, r4, ALU.mult, ALU.mult), reads=[B_bank[ssb], B_rs[3]], writes=[B_rs[2]])
                P.op("dve", f_ts(r3, r3, RMS_EPS, None, ALU.add), reads=[B_rs[2]], writes=[B_rs[2]])
                P.op("act", f_act(r3, r3, AF.Sqrt), reads=[B_rs[2]], writes=[B_rs[2]])
                P.op("dve", f_recip(r3, r3), reads=[B_rs[2]], writes=[B_rs[2]])
                P.op("dve", f_tt(r3, r3, r2, ALU.mult), reads=[B_rs[2], B_rs[1]], writes=[B_rs[2]])
                for g in range(4):
                    P.op("dve", f_tt(mf[:, 4 * g:4 * g + 4, 0:n], mf[:, 4 * g:4 * g + 4, 0:n], rs_t[:, 2:3, 0:n].broadcast_to([128, 4, n]), ALU.mult),
                         reads=[B_mfg[g], B_rs[2]], writes=[B_mfg[g]])
                for c in range(NCH):
                    P.op("dve", f_stt(xblk[:, c, 0:n], mf[:, c, 0:n], gcol(l, 3, c), xblk[:, c, 0:n], ALU.mult, ALU.add),
                         reads=[B_xg[c // 4], B_mfg[c // 4], B_par], writes=[B_xg[c // 4]])
                P.op("pool", f_dma(xdst[:, :, t0:t0 + n].rearrange("c p t -> p c t"), xblk[:, :, 0:n]),
                     reads=B_xg, writes=gr(Gdst, t0, t0 + n), dsem=S_xst)

        queue_conversions(0)
        pump_conv(20)
        for l in range(NL):
            if stop == "conv":
                break
            phase1(l)
            if l + 1 < NL:
                queue_conversions(l + 1)
            if stop == "p1":
                break
            phase2(l)
            if stop == "p2":
                break
            phase3(l)
        pump_conv(10 ** 6)
        final = []
        for g in G_x["yout"]:
            final.extend(g.writers.items())
        for S in [S_us, S_qs, S_ks, S_vs, S_xst, S_misc] + conv_sems:
            final.append((S, S.count))
        P.wait_only("pool", final)
        P.wait_only("sync", final)

        with nc.Block() as block:
            @block.tensor
            def _(e):
                P.replay("pe", e)

            @block.scalar
            def _(e):
                P.replay("act", e)

            @block.vector
            def _(e):
                P.replay("dve", e)

            @block.gpsimd
            def _(e):
                P.replay("pool", e)

            @block.sync
            def _(e):
                P.replay("sync", e)
    info = dict(RIN=RIN, ROUT=ROUT, TIN=TIN, TOUT=TOUT, nops={k: len(v["ops"]) for k, v in P.eng.items()})
    return nc, info


def _bias_table(rpb, flipped):
    rows = SEQ // GW

    def true_rc(lr, lc):
        if flipped:
            return rows - 1 - lr, GW - 1 - lc
        return lr, lc
    tiles = [(0, 2 * j) for j in range(4)] + [(2, 2 * j) for j in range(4)] + [(20, 2 * j) for j in range(8, 13)]
    qi = np.arange(128)
    out = np.full((DEPTH, NHEAD, 128, NBT, 128), NEG, np.float32)
    for t, (qr0, kr0) in enumerate(tiles):
        qlr = qr0 + qi // GW
        qlc = qi % GW
        klr = kr0 + qi // GW
        klc = qi % GW
        qr, qc = true_rc(qlr, qlc)
        kr, kc = true_rc(klr, klc)
        rs = np.clip(qr - 4, 0, rows - 8)
        cs = np.clip(qc - 8, 0, GW - 16)
        valid = (kr[None, :] >= rs[:, None]) & (kr[None, :] < rs[:, None] + 8) & \
                (kc[None, :] >= cs[:, None]) & (kc[None, :] < cs[:, None] + 16)
        dr = np.clip(kr[None, :] - qr[:, None] + 7, 0, 14)
        dc = np.clip(kc[None, :] - qc[:, None], -15, 15) + 15
        g = rpb[:, :, dr, dc]
        out[:, :, :, t, :] = np.where(valid[None, None], g, np.float32(NEG))
    return out


_CACHE = {}


def kernel(x, w_in, w_dw, b_dw, conv_ln_g, conv_ln_b, rpb, w_out, w_up, w_down,
           pre_mix_g, post_mix_g, pre_mlp_g, post_mlp_g, _NL=DEPTH, _dump=(), _trace=False, _stop=None, _cores=8):
    NL = _NL
    f32 = np.float32
    x = np.asarray(x, f32)
    key = (NL, tuple(_dump), _stop)
    if key not in _CACHE:
        _CACHE[key] = build_program(NL, _dump, _stop)
    nc, info = _CACHE[key]
    T0 = info["TIN"][0]
    R0 = info["RIN"][0]
    w_in = np.ascontiguousarray(w_in, f32)
    w_out = np.ascontiguousarray(w_out, f32)
    w_up = np.ascontiguousarray(w_up, f32)
    w_down = np.ascontiguousarray(w_down, f32)
    g4 = np.stack([np.asarray(a, f32) for a in (pre_mix_g, post_mix_g, pre_mlp_g, post_mlp_g)], axis=1)
    gpack = np.ascontiguousarray(g4.reshape(DEPTH, 4, NCH, 128).transpose(3, 0, 1, 2)).reshape(128, -1)
    rpb = np.asarray(rpb, f32)
    packs = {}
    for flipped in (False, True):
        wd = np.asarray(w_dw, f32)
        if flipped:
            wd = wd[:, ::-1, :]
        cp = np.concatenate([wd.transpose(0, 2, 1),
                             np.asarray(b_dw, f32)[:, :, None],
                             np.asarray(conv_ln_g, f32)[:, :, None],
                             np.asarray(conv_ln_b, f32)[:, :, None]], axis=2)
        cp = cp.reshape(DEPTH, 8, 128, 34).transpose(2, 0, 1, 3)
        packs[flipped] = (np.ascontiguousarray(cp).reshape(128, -1), _bias_table(rpb, flipped))
    in_maps = []
    for c in range(_cores):
        b, half = divmod(c, 2)
        if half == 0:
            xt = x[b, 0:T0, :]
        else:
            xt = x[b, SEQ - T0:SEQ, :][::-1]
        xT = np.ascontiguousarray(xt.T).reshape(NCH, 128, T0)
        cpk, bt = packs[half == 1]
        in_maps.append({"xin": xT, "w_in": w_in, "w_out": w_out, "w_up": w_up, "w_down": w_down,
                        "gpack": gpack, "cpack": cpk, "biasg": bt})
    if _trace:
        res = run_bass_kernel_spmd(nc, in_maps, core_ids=list(range(_cores)), trace=True)
    else:
        res = run_bass_kernel_spmd(nc, in_maps, core_ids=list(range(_cores)))
    _CACHE["last"] = res
    out = np.zeros((4, SEQ, DM), f32)
    for c in range(_cores):
        b, half = divmod(c, 2)
        y = np.asarray(res.results[c]["yout"], f32).reshape(DM, 2048).T
        if half == 0:
            out[b, 0:2048] = y
        else:
            out[b, 2048:SEQ] = y[::-1]
    return out
```

```python
import numpy as np
from contextlib import ExitStack
import concourse.bass as bass
import concourse.mybir as mybir
from concourse.bass_utils import run_bass_kernel_spmd

F32 = mybir.dt.float32
BF16 = mybir.dt.bfloat16
I32 = mybir.dt.int32
AF = mybir.ActivationFunctionType
ALU = mybir.AluOpType

DEPTH = 4
DM = 2048
NCH = 16
SEQ = 4096
GW = 64
CW = 1024
NHEAD = 16
HD = 64
KCONV = 31
DFF = 8192
NBT = 13
RMS_EPS = 1e-6
LN_EPS = 1e-5
NEG = -30000.0
NS = 3
NTILE = 46


class Sem:
    def __init__(self, h):
        self.h = h
        self.count = 0


class Buf:
    def __init__(self, name):
        self.name = name
        self.writers = {}
        self.readers = {}
        self.war = {}
        self.aliases = []


def alias(*bufs):
    for a in bufs:
        for b in bufs:
            if a is not b and b not in a.aliases:
                a.aliases.append(b)


class Prog:
    def __init__(self, nc, sems):
        self.nc = nc
        self.eng = {}
        for name in ("pe", "act", "dve", "pool", "sync"):
            self.eng[name] = dict(ops=[], waited={}, sem=sems[name])

    def op(self, eng, fn, reads=(), writes=(), deps=(), dsem=None):
        E = self.eng[eng]
        rset = set(id(b) for b in reads)
        wset = set(id(b) for b in writes)
        hs = []
        for b in reads:
            hs.extend(b.writers.items())
        for b in writes:
            if b.readers or id(b) in rset:
                b.war = dict(b.readers)
                for k, v in b.writers.items():
                    b.war[k] = max(b.war.get(k, 0), v)
                b.readers = {}
                b.writers = {}
            for bb in b.aliases:
                for d in (bb.readers, bb.writers):
                    for k, v in d.items():
                        b.war[k] = max(b.war.get(k, 0), v)
            hs.extend(b.war.items())
        hs.extend(deps)
        need = {}
        for sem, val in hs:
            if sem is E["sem"] and eng == "pe":
                continue
            need[sem] = max(need.get(sem, 0), val)
        waits = []
        em = E.setdefault("emitted", {})
        for sem, val in need.items():
            if em.get(sem, 0) < val:
                em[sem] = val
                waits.append((sem, val))
        if dsem is not None:
            dsem.count += 16
            h = (dsem, dsem.count)
            inc = (dsem, 16)
        else:
            E["sem"].count += 1
            h = (E["sem"], E["sem"].count)
            inc = (E["sem"], 1)
        E["ops"].append((waits, fn, inc))
        for b in reads:
            if id(b) not in wset:
                b.readers[h[0]] = max(b.readers.get(h[0], 0), h[1])
        for b in writes:
            b.writers[h[0]] = max(b.writers.get(h[0], 0), h[1])
        return h

    def wait_only(self, eng, deps):
        E = self.eng[eng]
        waits = []
        for sem, val in deps:
            if E.setdefault("emitted", {}).get(sem, 0) < val:
                E["emitted"][sem] = val
                waits.append((sem, val))
        E["ops"].append((waits, None, None))

    def replay(self, name, e):
        for waits, fn, inc in self.eng[name]["ops"]:
            for sem, val in waits:
                e.wait_ge(sem.h, val)
            if fn is not None:
                ins = fn(e)
                ins.then_inc(inc[0].h, inc[1])


def f_dma(out, in_):
    return lambda e: e.dma_start(out=out, in_=in_)


def f_act(out, in_, func, **kw):
    return lambda e: e.activation(out=out, in_=in_, func=func, **kw)


def f_ts(out, in0, s1, s2, op0, op1=None):
    if op1 is None:
        return lambda e: e.tensor_scalar(out=out, in0=in0, scalar1=s1, scalar2=None, op0=op0)
    return lambda e: e.tensor_scalar(out=out, in0=in0, scalar1=s1, scalar2=s2, op0=op0, op1=op1)


def f_tt(out, in0, in1, op):
    return lambda e: e.tensor_tensor(out=out, in0=in0, in1=in1, op=op)


def f_stt(out, in0, scalar, in1, op0, op1):
    return lambda e: e.scalar_tensor_tensor(out=out, in0=in0, scalar=scalar, in1=in1, op0=op0, op1=op1)


def f_copy(out, in_):
    return lambda e: e.tensor_copy(out=out, in_=in_)


def f_recip(out, in_):
    return lambda e: e.reciprocal(out=out, in_=in_)


def f_memset(ap, val):
    return lambda e: e.memset(ap, val)


def f_mm(items):
    def fn(e):
        ins = None
        for (out, lhsT, rhs, start, stop) in items:
            ins = e.matmul(out, lhsT=lhsT, rhs=rhs, start=start, stop=stop)
        return ins
    return fn


def f_tr(out, in_, ident):
    return lambda e: e.transpose(out, in_, ident)


def blocks_of(T):
    res = []
    t = 0
    while t < T:
        n = 512 if T - t >= 512 else T - t
        res.append((t, n))
        t += n
    return res


def build_program(NL=DEPTH, dump=(), stop=None):
    nc = bass.Bass("TRN2", target_bir_lowering=False)
    RIN = [32 + 4 * (NL - l) for l in range(NL)]
    ROUT = [32 + 4 * (NL - 1 - l) for l in range(NL)]
    TIN = [r * GW for r in RIN]
    TOUT = [r * GW for r in ROUT]
    T0 = TIN[0]
    T1 = TOUT[0]

    def dram(name, shape, dt, kind="Internal"):
        if name in dump:
            kind = "ExternalOutput"
        return nc.dram_tensor(name, shape, dt, kind=kind).ap()

    xin = dram("xin", [NCH, 128, T0], F32, "ExternalInput")
    w_in = dram("w_in", [DEPTH, DM, 5120], F32, "ExternalInput")
    w_out = dram("w_out", [DEPTH, DM, DM], F32, "ExternalInput")
    w_up = dram("w_up", [DEPTH, DM, DFF], F32, "ExternalInput")
    w_down = dram("w_down", [DEPTH, DFF, DM], F32, "ExternalInput")
    gpack_d = dram("gpack", [128, DEPTH * 4 * NCH], F32, "ExternalInput")
    cpack_d = dram("cpack", [128, DEPTH * 8 * 34], F32, "ExternalInput")
    bias_d = dram("biasg", [DEPTH, NHEAD, 128, NBT, 128], F32, "ExternalInput")
    yout = dram("yout", [NCH, 128, TOUT[NL - 1]], F32, "ExternalOutput")
    xs = [dram("xs0", [NCH, 128, T1], F32), dram("xs1", [NCH, 128, T1], F32)]
    wsc = dram("wsc", [2, NTILE, 128, 8192], BF16)
    uT = dram("uT", [8, 128, T0 + 32], BF16)
    qT = dram("qT", [8, 128, T0], BF16)
    kT = dram("kT", [8, 128, T0], BF16)
    vS = dram("vS", [T0 // 128, 128, NHEAD * 65], BF16)
    mixT = dram("mixT", [NCH, 128, T1], BF16)

    es = ExitStack()
    with es:
        def sb(name, shape, dt):
            return es.enter_context(nc.sbuf_tensor(name, shape, dt))

        def newsem(name):
            return Sem(es.enter_context(nc.semaphore(name)))

        RW = sb("RW", [128, NS * 4096], F32)
        RX = sb("RX", [128, 8192], F32)
        RH = sb("RH", [128, 9216], F32)
        RS = sb("RS", [128, 8320], F32)
        RD = sb("RD", [128, 4096], F32)
        sqb = sb("sqb", [128, 2, 4, 512], BF16)
        rs_t = sb("rs_t", [128, 4, 512], F32)
        sg_t = sb("sg_t", [128, 2, 512], F32)
        ident = sb("ident", [128, 128], BF16)
        ones = sb("ones", [128, 128], BF16)
        iot = sb("iot", [128, 128], I32)
        zer = sb("zer", [128, 128], BF16)
        gpack = sb("gpack_s", [128, DEPTH, 4, NCH], F32)
        cpack = sb("cpack_s", [128, DEPTH, 8, 34], F32)
        psF = es.enter_context(nc.psum_tensor("psF", [128, 8 * 512], F32))
        onesAB = sb("onesAB", [128, 2, 128], BF16)

        def bfview(reg, off_w, n_w):
            return reg[:, off_w:off_w + n_w].bitcast(BF16)

        wslot = [bfview(RW, s * 4096, 4096) for s in range(NS)]
        B_w = [Buf(f"w{s}") for s in range(NS)]
        S_w = [newsem(f"sw{s}") for s in range(NS)]
        xblk = RX[:, :].rearrange("p (c n) -> p c n", c=NCH)
        B_xg = [Buf(f"x{g}") for g in range(4)]
        acc = RX[:, 0:4096].rearrange("p (c n) -> p c n", c=8)
        B_acc = [Buf(f"acc{c}") for c in range(8)]
        ublk = bfview(RX, 4096, 2176)[:, 0:8 * 542].rearrange("p (c n) -> p c n", c=8)
        B_u = Buf("ublk")
        S_u = newsem("su")
        tmpc = RX[:, 6272:6272 + 1024].rearrange("p (c n) -> p c n", c=2)
        B_tmpc = [Buf("tmpc0"), Buf("tmpc1")]
        for a in B_acc + B_tmpc + [B_u]:
            a.aliases = list(B_xg)
        for a in B_xg:
            a.aliases = B_acc + B_tmpc + [B_u]
        S_x = newsem("sx")
        hT = [bfview(RH, i * 4096, 4096).rearrange("p (c n) -> p c n", c=NCH) for i in range(2)]
        B_hT = [Buf("hT0"), Buf("hT1")]
        mixb = hT[0]
        B_mix = B_hT[0]
        S_mix = newsem("smix")
        h2 = hT[1]
        B_h2 = B_hT[1]
        RHb = RH[:, :].bitcast(BF16)
        o = 0
        kblk = []
        for i in range(2):
            kblk.append(RHb[:, o:o + 1024]); o += 1024
        vblk = []
        for i in range(2):
            vblk.append(RHb[:, o:o + 2048].rearrange("p (j h d) -> p j h d", j=8, h=2)); o += 2048
        qA = []
        qB = []
        for i in range(2):
            qA.append(RHb[:, o:o + 512]); o += 512
            qB.append(RHb[:, o:o + 512]); o += 512
        biasb = []
        for i in range(2):
            biasb.append(RHb[:, o:o + 2 * NBT * 128].rearrange("p (h t k) -> p h t k", h=2, t=NBT)); o += 2 * NBT * 128
        pT = []
        for i in range(4):
            pT.append(RHb[:, o:o + 640]); o += 640
        assert o <= 18432, o
        B_k = [Buf("k0"), Buf("k1")]
        B_v = [Buf("v0"), Buf("v1")]
        B_qA = [Buf("qA0"), Buf("qA1")]
        B_qB = [Buf("qB0"), Buf("qB1")]
        B_bias = [Buf("bias0"), Buf("bias1")]
        B_pT = [Buf(f"pT{i}") for i in range(4)]
        S_k = [newsem("sk0"), newsem("sk1")]
        S_v = [newsem("sv0"), newsem("sv1")]
        S_qA = [newsem("sqa0"), newsem("sqa1")]
        S_qB = [newsem("sqb0"), newsem("sqb1")]
        S_bias = [newsem("sbi0"), newsem("sbi1")]
        attn_bufs = B_k + B_v + B_qA + B_qB + B_bias + B_pT
        for a in attn_bufs:
            a.aliases = list(B_hT)
        for a in B_hT:
            a.aliases = list(attn_bufs)
        RSb = RS[:, :].bitcast(BF16)
        ustage = RSb[:, 0:4096].rearrange("p (c n) -> p c n", c=8)
        qstage = RSb[:, 4096:8192].rearrange("p (c n) -> p c n", c=8)
        kstage = RSb[:, 8192:12288].rearrange("p (c n) -> p c n", c=8)
        vstage = RSb[:, 12288:12288 + 4160].rearrange("p (t h d) -> p t h d", t=4, h=NHEAD)
        mf = RS[:, 0:8192].rearrange("p (c n) -> p c n", c=NCH)
        B_us, B_qs, B_ks, B_vs = Buf("ustage"), Buf("qstage"), Buf("kstage"), Buf("vstage")
        S_us, S_qs, S_ks, S_vs = newsem("sus"), newsem("sqs"), newsem("sks"), newsem("svs")
        B_mfg = [Buf(f"mf{g}") for g in range(4)]
        for a in (B_us, B_qs, B_ks, B_vs):
            a.aliases = list(B_mfg)
        for a in B_mfg:
            a.aliases = [B_us, B_qs, B_ks, B_vs]
        S_xst = newsem("sxst")
        hid = RD[:, :].bitcast(BF16).rearrange("p (c n) -> p c n", c=16)
        B_hid = Buf("hid")
        RDb = RD[:, :].bitcast(BF16)
        dg = [RDb[:, i * 3968:(i + 1) * 3968].rearrange("p (j k) -> p j k", j=KCONV) for i in range(2)]
        B_dg = [Buf("dg0"), Buf("dg1")]
        for a in B_dg:
            a.aliases = [B_hid]
        B_hid.aliases = list(B_dg)
        B_sq = [Buf("sq0"), Buf("sq1")]
        B_rs = [Buf(f"rs{i}") for i in range(4)]
        B_sg = [Buf("sg0"), Buf("sg1")]
        B_const = Buf("const")
        B_par = Buf("params")
        S_par = newsem("spar")
        S_misc = newsem("smisc")
        def bank(b):
            return psF[:, b * 512:(b + 1) * 512]
        B_bank = [Buf(f"bank{b}") for b in range(8)]
        rot = [0]

        def next_bank():
            b = rot[0]
            rot[0] = (b + 1) % 8
            return b

        sems = {n: newsem("e_" + n) for n in ("pe", "act", "dve", "pool", "sync")}
        P = Prog(nc, sems)
        conv_sems = [newsem(f"cv{i}") for i in range(8)]
        conv_hist = []

        def gran(name, n):
            return [Buf(f"{name}_g{i}") for i in range(n)]
        NG = T0 // 256
        G_x = {"xin": gran("xin", NG), 0: gran("xs0", NG), 1: gran("xs1", NG), "yout": gran("yout", NG)}
        G_u = gran("uT", NG + 1)
        G_q = gran("qT", NG)
        G_k = gran("kT", NG)
        G_v = gran("vS", NG)
        G_myc = gran("myc", NG)
        G_mya = gran("mya", NG)
        B_wsc = [[[Buf(f"wsc{p}_{t}_{h}") for h in range(2)] for t in range(NTILE)] for p in range(2)]

        def gr(G, lo, hi):
            return G[lo // 256:(hi + 255) // 256]

        P.op("pool", lambda e: e.iota(iot[:], pattern=[[1, 128]], base=0, channel_multiplier=-1), writes=[B_const])
        P.op("dve", f_ts(ident[:], iot[:], 0.0, None, ALU.is_equal), reads=[B_const], writes=[B_const])
        P.op("dve", f_memset(ones[:], 1.0), writes=[B_const])
        P.op("dve", f_memset(zer[:], 0.0), writes=[B_const])
        B_oab = Buf("onesAB")
        P.op("dve", f_memset(onesAB[:], 0.0), writes=[B_oab])
        P.op("dve", f_memset(onesAB[0:128, 0, 0:64], 1.0), reads=[B_oab], writes=[B_oab])
        P.op("dve", f_memset(onesAB[0:128, 1, 64:128], 1.0), reads=[B_oab], writes=[B_oab])
        P.op("sync", f_dma(gpack[:].rearrange("p a b c -> p (a b c)"), gpack_d), writes=[B_par], dsem=S_par)
        P.op("sync", f_dma(cpack[:].rearrange("p a b c -> p (a b c)"), cpack_d), writes=[B_par], dsem=S_par)
        for c in range(8):
            P.op("pool", f_dma(uT[c, :, 0:15], zer[:, 0:15]), reads=[B_const], writes=[G_u[0]], dsem=S_misc)

        def tile_src(l, tid, half):
            par = l % 2
            dst = wsc[par, tid].rearrange("p (k n) -> p k n", k=16)
            k0, k1 = half * 8, half * 8 + 8
            res = []
            if tid < 4:
                for (cb, d0) in ((2 * tid * 128, 0), (1024 + 2 * tid * 128, 256)):
                    src = w_in[l, k0 * 128:k1 * 128, cb:cb + 256].rearrange("(k p) n -> p k n", p=128)
                    res.append((dst[:, k0:k1, d0:d0 + 256], src))
            elif tid < 10:
                cb = 2048 + (tid - 4) * 512
                src = w_in[l, k0 * 128:k1 * 128, cb:cb + 512].rearrange("(k p) n -> p k n", p=128)
                res.append((dst[:, k0:k1, :], src))
            elif tid < 14:
                cb = (tid - 10) * 512
                src = w_out[l, k0 * 128:k1 * 128, cb:cb + 512].rearrange("(k p) n -> p k n", p=128)
                res.append((dst[:, k0:k1, :], src))
            else:
                qd, r = divmod(tid - 14, 8)
                if r < 4:
                    cb = qd * 2048 + r * 512
                    src = w_up[l, k0 * 128:k1 * 128, cb:cb + 512].rearrange("(k p) n -> p k n", p=128)
                else:
                    cb = (r - 4) * 512
                    rb = qd * 2048
                    src = w_down[l, rb + k0 * 128:rb + k1 * 128, cb:cb + 512].rearrange("(k p) n -> p k n", p=128)
                res.append((dst[:, k0:k1, :], src))
            return res

        pending_conv = []

        def queue_conversions(l):
            for tid in range(NTILE):
                for half in range(2):
                    pending_conv.append((l, tid, half))

        def pump_conv(k):
            for _ in range(k):
                if not pending_conv:
                    return
                l, tid, half = pending_conv.pop(0)
                for (dst, src) in tile_src(l, tid, half):
                    i = len(conv_hist)
                    cs = conv_sems[i % 8]
                    deps = [conv_hist[i - 8]] if i >= 8 else []
                    h = P.op("pool", f_dma(dst, src), writes=[B_wsc[l % 2][tid][half]], deps=deps, dsem=cs)
                    conv_hist.append(h)

        wtiles = []
        for l in range(NL):
            for _ in blocks_of(TIN[l]):
                wtiles.extend((l, t) for t in range(10))
            for _ in blocks_of(TOUT[l]):
                wtiles.extend((l, t) for t in range(10, NTILE))
        wstate = dict(next_load=0, next_use=0)

        def w_load_one():
            i = wstate["next_load"]
            if i >= len(wtiles):
                return
            l, tid = wtiles[i]
            while any(pc[0] == l and pc[1] == tid for pc in pending_conv):
                pump_conv(1)
            s = i % NS
            P.op("sync", f_dma(wslot[s], wsc[l % 2, tid]), reads=B_wsc[l % 2][tid], writes=[B_w[s]], dsem=S_w[s])
            wstate["next_load"] = i + 1

        def w_next(expect):
            i = wstate["next_use"]
            assert wtiles[i] == expect, (wtiles[i], expect)
            while wstate["next_load"] < min(i + NS, len(wtiles)):
                if wstate["next_load"] >= i + NS - 1 and i >= 1:
                    pass
                w_load_one()
            wstate["next_use"] = i + 1
            pump_conv(1)
            s = i % NS
            return wslot[s].rearrange("p (k n) -> p k n", k=16), B_w[s]

        def rstd_from(bank_b, n, ridx, scale, eps):
            r = rs_t[:, ridx, 0:n]
            P.op("dve", f_ts(r, bank(bank_b)[:, 0:n], scale, eps, ALU.mult, ALU.add), reads=[B_bank[bank_b]], writes=[B_rs[ridx]])
            P.op("act", f_act(r, r, AF.Sqrt), reads=[B_rs[ridx]], writes=[B_rs[ridx]])
            P.op("dve", f_recip(r, r), reads=[B_rs[ridx]], writes=[B_rs[ridx]])

        def sumsq_of(src3, B_src, n, nchunks):
            b = next_bank()
            ng = nchunks // 4
            for g in range(ng):
                sq = sqb[:, g % 2, :, 0:n]
                P.op("act", f_act(sq, src3[:, 4 * g:4 * g + 4, 0:n], AF.Square), reads=[B_src[g]], writes=[B_sq[g % 2]])
                items = [(bank(b)[:, 0:n], ones[:], sqb[:, g % 2, j, 0:n], (g == 0 and j == 0), (g == ng - 1 and j == 3)) for j in range(4)]
                P.op("pe", f_mm(items), reads=[B_sq[g % 2], B_const], writes=[B_bank[b]])
            return b

        def gcol(l, k, c):
            return gpack[:, l, k, c:c + 1]

        def phase1(l):
            xsrc, Gsrc = (xin, G_x["xin"]) if l == 0 else (xs[(l - 1) % 2], G_x[(l - 1) % 2])
            P.op("dve", f_memset(vstage[:, :, :, 64:65], 1.0), writes=[B_vs])
            blks = blocks_of(TIN[l])

            def load_block(bi):
                t0, n = blks[bi]
                P.op("sync", f_dma(xblk[:, :, 0:n], xsrc[:, :, t0:t0 + n].rearrange("c p t -> p c t")),
                     reads=gr(Gsrc, t0, t0 + n), writes=B_xg, dsem=S_x)

            def norm_block(bi):
                t0, n = blks[bi]
                par = bi % 2
                b = sumsq_of(xblk, B_xg, n, NCH)
                rstd_from(b, n, 0, 1.0 / DM, RMS_EPS)
                for c in range(NCH):
                    P.op("dve", f_stt(hT[par][:, c, 0:n], xblk[:, c, 0:n], gcol(l, 0, c), rs_t[:, 0, 0:n], ALU.mult, ALU.mult),
                         reads=[B_xg[c // 4], B_rs[0], B_par], writes=[B_hT[par]])
                if bi + 1 < len(blks):
                    load_block(bi + 1)

            load_block(0)
            norm_block(0)
            for bi, (t0, n) in enumerate(blks):
                par = bi % 2
                nt = n // 128
                for w in range(10):
                    wt, Bw = w_next((l, w))
                    if w == 5 and bi + 1 < len(blks):
                        norm_block(bi + 1)
                    if w < 8:
                        bs = [next_bank() for _ in range(4)]
                        for cc in range(4):
                            items = [(bank(bs[cc])[:, 0:n], wt[:, kc, cc * 128:(cc + 1) * 128], hT[par][:, kc, 0:n], kc == 0, kc == 15) for kc in range(16)]
                            P.op("pe", f_mm(items), reads=[Bw, B_hT[par]], writes=[B_bank[bs[cc]]])
                        if w < 4:
                            for j in range(2):
                                P.op("act", f_act(sg_t[:, j, 0:n], bank(bs[2 + j])[:, 0:n], AF.Sigmoid), reads=[B_bank[bs[2 + j]]], writes=[B_sg[j]])
                                P.op("dve", f_tt(ustage[:, 2 * w + j, 0:n], bank(bs[j])[:, 0:n], sg_t[:, j, 0:n], ALU.mult),
                                     reads=[B_bank[bs[j]], B_sg[j]], writes=[B_us])
                        else:
                            stg, Bs = (qstage, B_qs) if w < 6 else (kstage, B_ks)
                            sc = 0.125 if w < 6 else 1.0
                            for cc in range(4):
                                ch = (w % 2) * 4 + cc
                                if cc % 2 == 0:
                                    P.op("act", f_act(stg[:, ch, 0:n], bank(bs[cc])[:, 0:n], AF.Copy, scale=sc), reads=[B_bank[bs[cc]]], writes=[Bs])
                                else:
                                    P.op("dve", f_ts(stg[:, ch, 0:n], bank(bs[cc])[:, 0:n], sc, None, ALU.mult), reads=[B_bank[bs[cc]]], writes=[Bs])
                    else:
                        hb = (w - 8) * 8
                        for tt in range(nt):
                            b2 = next_bank()
                            items = [(bank(b2)[:, :], hT[par][:, kc, tt * 128:(tt + 1) * 128], wt[:, kc, :], kc == 0, kc == 15) for kc in range(16)]
                            P.op("pe", f_mm(items), reads=[Bw, B_hT[par]], writes=[B_bank[b2]])
                            src = bank(b2)[:, :].rearrange("p (h d) -> p h d", d=64)
                            dst = vstage[:, tt, hb:hb + 8, 0:64]
                            if tt % 2 == 0:
                                P.op("act", f_act(dst, src, AF.Copy), reads=[B_bank[b2]], writes=[B_vs])
                            else:
                                P.op("dve", f_copy(dst, src), reads=[B_bank[b2]], writes=[B_vs])
                    if w == 3:
                        P.op("pool", f_dma(uT[:, :, 15 + t0:15 + t0 + n].rearrange("c p t -> p c t"), ustage[:, :, 0:n]),
                             reads=[B_us], writes=gr(G_u, t0, t0 + n), dsem=S_us)
                    if w == 5:
                        P.op("pool", f_dma(qT[:, :, t0:t0 + n].rearrange("c p t -> p c t"), qstage[:, :, 0:n]),
                             reads=[B_qs], writes=gr(G_q, t0, t0 + n), dsem=S_qs)
                    if w == 7:
                        P.op("pool", f_dma(kT[:, :, t0:t0 + n].rearrange("c p t -> p c t"), kstage[:, :, 0:n]),
                             reads=[B_ks], writes=gr(G_k, t0, t0 + n), dsem=S_ks)
                    if w == 9:
                        P.op("pool", f_dma(vS[t0 // 128:t0 // 128 + nt].rearrange("t p f -> p t f"),
                                           vstage[:, 0:nt].rearrange("p t h d -> p t (h d)")),
                             reads=[B_vs], writes=gr(G_v, t0, t0 + n), dsem=S_vs)

        def conv_gen(l, t0, n):
            P.op("sync", f_dma(ublk[:, :, 0:n + 30], uT[:, :, t0:t0 + n + 30].rearrange("c p t -> p c t")),
                 reads=gr(G_u, t0, t0 + n + 30), writes=[B_u], dsem=S_u)
            yield
            for c in range(8):
                dp = c % 2
                P.op("dve", f_tt(dg[dp][:, :, :], ident[:, :].unsqueeze(1).broadcast_to([128, KCONV, 128]),
                                 cpack[:, l, c, 0:KCONV].unsqueeze(2).broadcast_to([128, KCONV, 128]), ALU.mult),
                     reads=[B_const, B_par], writes=[B_dg[dp]])
                yield
                cbk = 5 + dp
                items = [(bank(cbk)[:, 0:n], dg[dp][:, j, :], ublk[:, c, j:j + n], j == 0, j == KCONV - 1) for j in range(KCONV)]
                P.op("pe", f_mm(items), reads=[B_dg[dp], B_u], writes=[B_bank[cbk]])
                P.op("act", f_act(acc[:, c, 0:n], bank(cbk)[:, 0:n], AF.Identity, bias=cpack[:, l, c, 31:32]),
                     reads=[B_bank[cbk], B_par], writes=[B_acc[c]])
                yield
            for g in range(2):
                cb = sqb[:, 0, :, 0:n]
                P.op("act", f_act(cb, acc[:, 4 * g:4 * g + 4, 0:n], AF.Copy), reads=B_acc[4 * g:4 * g + 4], writes=[B_sq[0]])
                items = [(bank(5)[:, 0:n], ones[:], sqb[:, 0, j, 0:n], (g == 0 and j == 0), (g == 1 and j == 3)) for j in range(4)]
                P.op("pe", f_mm(items), reads=[B_sq[0], B_const], writes=[B_bank[5]])
                sq = sqb[:, 1, :, 0:n]
                P.op("act", f_act(sq, acc[:, 4 * g:4 * g + 4, 0:n], AF.Square), reads=B_acc[4 * g:4 * g + 4], writes=[B_sq[1]])
                items = [(bank(6)[:, 0:n], ones[:], sqb[:, 1, j, 0:n], (g == 0 and j == 0), (g == 1 and j == 3)) for j in range(4)]
                P.op("pe", f_mm(items), reads=[B_sq[1], B_const], writes=[B_bank[6]])
                yield
            mean = rs_t[:, 1, 0:n]
            msq = rs_t[:, 2, 0:n]
            rst = rs_t[:, 3, 0:n]
            P.op("dve", f_ts(mean, bank(5)[:, 0:n], 1.0 / CW, None, ALU.mult), reads=[B_bank[5]], writes=[B_rs[1]])
            P.op("dve", f_tt(msq, mean, mean, ALU.mult), reads=[B_rs[1]], writes=[B_rs[2]])
            P.op("dve", f_stt(rst, bank(6)[:, 0:n], 1.0 / CW, msq, ALU.mult, ALU.subtract), reads=[B_bank[6], B_rs[2]], writes=[B_rs[3]])
            P.op("dve", f_ts(rst, rst, LN_EPS, None, ALU.add), reads=[B_rs[3]], writes=[B_rs[3]])
            P.op("act", f_act(rst, rst, AF.Sqrt), reads=[B_rs[3]], writes=[B_rs[3]])
            P.op("dve", f_recip(rst, rst), reads=[B_rs[3]], writes=[B_rs[3]])
            yield
            for c in range(8):
                tb = c % 2
                P.op("dve", f_tt(tmpc[:, tb, 0:n], acc[:, c, 0:n], mean, ALU.subtract), reads=[B_acc[c], B_rs[1]], writes=[B_tmpc[tb]])
                if c >= 1:
                    cp = c - 1
                    P.op("dve", f_tt(acc[:, cp, 0:n], tmpc[:, cp % 2, 0:n], rst, ALU.mult), reads=[B_tmpc[cp % 2], B_rs[3]], writes=[B_acc[cp]])
                    P.op("act", f_act(ustage[:, cp, 0:n], acc[:, cp, 0:n], AF.Silu, scale=cpack[:, l, cp, 32:33], bias=cpack[:, l, cp, 33:34]),
                         reads=[B_acc[cp], B_par], writes=[B_us])
                yield
            cp = 7
            P.op("dve", f_tt(acc[:, cp, 0:n], tmpc[:, cp % 2, 0:n], rst, ALU.mult), reads=[B_tmpc[cp % 2], B_rs[3]], writes=[B_acc[cp]])
            P.op("act", f_act(ustage[:, cp, 0:n], acc[:, cp, 0:n], AF.Silu, scale=cpack[:, l, cp, 32:33], bias=cpack[:, l, cp, 33:34]),
                 reads=[B_acc[cp], B_par], writes=[B_us])
            P.op("pool", f_dma(mixT[0:8, :, t0:t0 + n].rearrange("c p t -> p c t"), ustage[:, :, 0:n]),
                 reads=[B_us], writes=gr(G_myc, t0, t0 + n), dsem=S_us)
            yield

        def phase2(l):
            for i in range(2):
                P.op("dve", f_memset(qA[i][64:128, :], 0.0), writes=[B_qA[i]])
                P.op("dve", f_memset(qB[i][0:64, :], 0.0), writes=[B_qB[i]])
                P.op("dve", f_memset(vblk[i][:, :, 0, 64:128], 0.0), writes=[B_v[i]])
                P.op("dve", f_memset(vblk[i][:, :, 1, 0:64], 0.0), writes=[B_v[i]])
            step = [0]
            for (t0, n) in blocks_of(TOUT[l]):
                cg = conv_gen(l, t0, n)
                nq = n // 128
                i0 = t0 // 128
                jlo = max(0, i0 - 2)
                jhi = max(i0 + nq - 1 + 2, 3)
                nk = jhi - jlo + 1
                edge = (i0 == 0)
                tlo, ntl = (0, NBT) if edge else (8, 5)
                per_step = 40 // (8 * nq) + 2
                pend = []

                def flush_one():
                    (hp, qi, par, js, pp, odb) = pend.pop(0)
                    items = []
                    nj = len(js)
                    for head in range(2):
                        for jj, j in enumerate(js):
                            items.append((bank(odb)[:, 0:128], vblk[par][:, j - jlo, head, :], pT[pp + head][:, jj * 128:(jj + 1) * 128],
                                          head == 0 and jj == 0, head == 1 and jj == nj - 1))
                    for head in range(2):
                        for jj, j in enumerate(js):
                            items.append((bank(odb)[:, 128:256], onesAB[:, head, :], pT[pp + head][:, jj * 128:(jj + 1) * 128],
                                          head == 0 and jj == 0, head == 1 and jj == nj - 1))
                    P.op("pe", f_mm(items), reads=[B_pT[pp], B_pT[pp + 1], B_v[par], B_oab], writes=[B_bank[odb]])
                    rc = rs_t[:, 0, 0:128]
                    P.op("dve", f_recip(rc, bank(odb)[:, 128:256]), reads=[B_bank[odb]], writes=[B_rs[0]])
                    P.op("dve", f_tt(qstage[:, hp, qi * 128:(qi + 1) * 128], bank(odb)[:, 0:128], rc, ALU.mult),
                         reads=[B_bank[odb], B_rs[0]], writes=[B_qs])

                for hp in range(8):
                    par = hp % 2
                    P.op("sync", f_dma(kblk[par][:, 0:nk * 128], kT[hp, :, jlo * 128:(jlo + nk) * 128]),
                         reads=gr(G_k, jlo * 128, (jlo + nk) * 128), writes=[B_k[par]], dsem=S_k[par])
                    for head in range(2):
                        hh = 2 * hp + head
                        P.op("sync", f_dma(vblk[par][:, 0:nk, head, head * 64:(head + 1) * 64],
                                           vS[jlo:jlo + nk, :, hh * 65:hh * 65 + 64].rearrange("j p d -> p j d")),
                             reads=gr(G_v, jlo * 128, (jlo + nk) * 128), writes=[B_v[par]], dsem=S_v[par])
                    P.op("sync", f_dma(qA[par][0:64, 0:n], qT[hp, 0:64, t0:t0 + n]), reads=gr(G_q, t0, t0 + n), writes=[B_qA[par]], dsem=S_qA[par])
                    P.op("sync", f_dma(qB[par][64:128, 0:n], qT[hp, 64:128, t0:t0 + n]), reads=gr(G_q, t0, t0 + n), writes=[B_qB[par]], dsem=S_qB[par])
                    P.op("pool", f_dma(biasb[par][:, :, 0:ntl, :], bias_d[l, 2 * hp:2 * hp + 2, :, tlo:tlo + ntl, :].rearrange("h q t k -> q h t k")),
                         writes=[B_bias[par]], dsem=S_bias[par])
                    pump_conv(1)
                    for qi in range(nq):
                        i = i0 + qi
                        if i < 2:
                            js = [0, 1, 2, 3]
                            tb = 0 if i == 0 else 4
                        else:
                            js = list(range(i - 2, i + 3))
                            tb = 8
                        tb -= tlo
                        sidx = step[0]
                        step[0] += 1
                        pp = 2 * (sidx % 2)
                        odb = 4 if sidx % 2 == 0 else 7
                        nkk = len(js) * 128
                        for head in range(2):
                            sb0 = 2 * head
                            qsel, Bq = (qA[par], B_qA[par]) if head == 0 else (qB[par], B_qB[par])
                            items = []
                            for jj, j in enumerate(js):
                                o_ap = psF[:, sb0 * 512 + jj * 128: sb0 * 512 + (jj + 1) * 128]
                                items.append((o_ap, kblk[par][:, (j - jlo) * 128:(j - jlo + 1) * 128], qsel[:, qi * 128:(qi + 1) * 128], True, False))
                                items.append((o_ap, biasb[par][:, head, tb + jj, :], ident[:], False, True))
                            P.op("pe", f_mm(items), reads=[B_k[par], Bq, B_bias[par], B_const], writes=[B_bank[sb0], B_bank[sb0 + 1]])
                            P.op("act", f_act(pT[pp + head][:, 0:min(nkk, 512)], psF[:, sb0 * 512:sb0 * 512 + min(nkk, 512)], AF.Exp),
                                 reads=[B_bank[sb0]], writes=[B_pT[pp + head]])
                            if nkk > 512:
                                P.op("act", f_act(pT[pp + head][:, 512:nkk], psF[:, sb0 * 512 + 512:sb0 * 512 + nkk], AF.Exp),
                                     reads=[B_bank[sb0 + 1]], writes=[B_pT[pp + head]])
                        if pend:
                            flush_one()
                        pend.append((hp, qi, par, js, pp, odb))
                        for _ in range(per_step):
                            next(cg, None)
                while pend:
                    flush_one()
                for _ in cg:
                    pass
                P.op("pool", f_dma(mixT[8:16, :, t0:t0 + n].rearrange("c p t -> p c t"), qstage[:, :, 0:n]),
                     reads=[B_qs], writes=gr(G_mya, t0, t0 + n), dsem=S_qs)

        def phase3(l):
            xsrc, Gsrc = (xin, G_x["xin"]) if l == 0 else (xs[(l - 1) % 2], G_x[(l - 1) % 2])
            if l == NL - 1:
                xdst, Gdst = yout, G_x["yout"]
            else:
                xdst, Gdst = xs[l % 2], G_x[l % 2]
            blks = blocks_of(TOUT[l])

            def load_mix(bi):
                t0, n = blks[bi]
                P.op("sync", f_dma(mixb[:, :, 0:n], mixT[:, :, t0:t0 + n].rearrange("c p t -> p c t")),
                     reads=gr(G_myc, t0, t0 + n) + gr(G_mya, t0, t0 + n), writes=[B_mix], dsem=S_mix)

            load_mix(0)
            for bi, (t0, n) in enumerate(blks):
                P.op("sync", f_dma(xblk[:, :, 0:n], xsrc[:, :, t0:t0 + n].rearrange("c p t -> p c t")),
                     reads=gr(Gsrc, t0, t0 + n), writes=B_xg, dsem=S_x)
                ssb = next_bank()
                for w in range(4):
                    wt, Bw = w_next((l, 10 + w))
                    for cc in range(4):
                        ch = w * 4 + cc
                        b = next_bank()
                        if b == ssb:
                            b = next_bank()
                        items = [(bank(b)[:, 0:n], wt[:, kc, cc * 128:(cc + 1) * 128], mixb[:, kc, 0:n], kc == 0, kc == 15) for kc in range(16)]
                        P.op("pe", f_mm(items), reads=[Bw, B_mix], writes=[B_bank[b]])
                        P.op("act", f_act(mf[:, ch, 0:n], bank(b)[:, 0:n], AF.Copy, scale=gcol(l, 1, ch)), reads=[B_bank[b], B_par], writes=[B_mfg[ch // 4]])
                        P.op("act", f_act(sqb[:, ch % 2, 0, 0:n], bank(b)[:, 0:n], AF.Square), reads=[B_bank[b]], writes=[B_sq[ch % 2]])
                        if ch >= 1:
                            cp = ch - 1
                            P.op("pe", f_mm([(bank(ssb)[:, 0:n], ones[:], sqb[:, cp % 2, 0, 0:n], cp == 0, cp == 15)]),
                                 reads=[B_sq[cp % 2], B_const], writes=[B_bank[ssb]])
                P.op("pe", f_mm([(bank(ssb)[:, 0:n], ones[:], sqb[:, 1, 0, 0:n], False, True)]),
                     reads=[B_sq[1], B_const], writes=[B_bank[ssb]])
                if bi + 1 < len(blks):
                    load_mix(bi + 1)
                rstd_from(ssb, n, 0, 1.0 / DM, RMS_EPS)
                for g in range(4):
                    P.op("dve", f_tt(mf[:, 4 * g:4 * g + 4, 0:n], mf[:, 4 * g:4 * g + 4, 0:n], rs_t[:, 0:1, 0:n].broadcast_to([128, 4, n]), ALU.mult),
                         reads=[B_mfg[g], B_rs[0]], writes=[B_mfg[g]])
                for g in range(4):
                    P.op("dve", f_tt(xblk[:, 4 * g:4 * g + 4, 0:n], xblk[:, 4 * g:4 * g + 4, 0:n], mf[:, 4 * g:4 * g + 4, 0:n], ALU.add),
                         reads=[B_xg[g], B_mfg[g]], writes=[B_xg[g]])
                for c in range(NCH):
                    P.op("dve", f_ts(h2[:, c, 0:n], xblk[:, c, 0:n], gcol(l, 2, c), None, ALU.mult),
                         reads=[B_xg[c // 4], B_par], writes=[B_h2])
                b = sumsq_of(xblk, B_xg, n, NCH)
                rstd_from(b, n, 1, 1.0 / DM, RMS_EPS)
                r2 = rs_t[:, 1, 0:n]
                r4 = rs_t[:, 3, 0:n]
                P.op("dve", f_tt(r2, r2, r2, ALU.mult), reads=[B_rs[1]], writes=[B_rs[1]])
                P.op("dve", f_tt(r4, r2, r2, ALU.mult), reads=[B_rs[1]], writes=[B_rs[3]])
                ssb = None
                for qd in range(4):
                    for r in range(4):
                        wt, Bw = w_next((l, 14 + 8 * qd + r))
                        for cc in range(4):
                            hc = r * 4 + cc
                            b = next_bank()
                            items = [(bank(b)[:, 0:n], wt[:, kc, cc * 128:(cc + 1) * 128], h2[:, kc, 0:n], kc == 0, kc == 15) for kc in range(16)]
                            P.op("pe", f_mm(items), reads=[Bw, B_h2], writes=[B_bank[b]])
                            j = hc % 2
                            if j == 0:
                                P.op("act", f_act(sg_t[:, 0, 0:n], bank(b)[:, 0:n], AF.Relu), reads=[B_bank[b]], writes=[B_sg[0]])
                                P.op("act", f_act(hid[:, hc, 0:n], sg_t[:, 0, 0:n], AF.Square), reads=[B_sg[0]], writes=[B_hid])
                            else:
                                P.op("dve", f_ts(sg_t[:, 1, 0:n], bank(b)[:, 0:n], 0.0, None, ALU.max), reads=[B_bank[b]], writes=[B_sg[1]])
                                P.op("dve", f_tt(hid[:, hc, 0:n], sg_t[:, 1, 0:n], sg_t[:, 1, 0:n], ALU.mult), reads=[B_sg[1]], writes=[B_hid])
                    if qd == 3:
                        ssb = next_bank()
                    for r in range(4):
                        wt, Bw = w_next((l, 14 + 8 * qd + 4 + r))
                        for cc in range(4):
                            ch = r * 4 + cc
                            b = next_bank()
                            if b == ssb:
                                b = next_bank()
                            items = [(bank(b)[:, 0:n], wt[:, kc, cc * 128:(cc + 1) * 128], hid[:, kc, 0:n], kc == 0, kc == 15) for kc in range(16)]
                            P.op("pe", f_mm(items), reads=[Bw, B_hid], writes=[B_bank[b]])
                            Bg = B_mfg[ch // 4]
                            if qd == 0:
                                P.op("act", f_act(mf[:, ch, 0:n], bank(b)[:, 0:n], AF.Copy), reads=[B_bank[b]], writes=[Bg])
                            else:
                                P.op("dve", f_tt(mf[:, ch, 0:n], bank(b)[:, 0:n], mf[:, ch, 0:n], ALU.add), reads=[B_bank[b], Bg], writes=[Bg])
                            if qd == 3:
                                P.op("act", f_act(sqb[:, ch % 2, 0, 0:n], mf[:, ch, 0:n], AF.Square), reads=[Bg], writes=[B_sq[ch % 2]])
                                if ch >= 1:
                                    cp = ch - 1
                                    P.op("pe", f_mm([(bank(ssb)[:, 0:n], ones[:], sqb[:, cp % 2, 0, 0:n], cp == 0, cp == 15)]),
                                         reads=[B_sq[cp % 2], B_const], writes=[B_bank[ssb]])
                P.op("pe", f_mm([(bank(ssb)[:, 0:n], ones[:], sqb[:, 1, 0, 0:n], False, True)]),
                     reads=[B_sq[1], B_const], writes=[B_bank[ssb]])
                r3 = rs_t[:, 2, 0:n]
                P.op("dve", f_stt(r3, bank(ssb)[:, 0:n], 1.0 / DM, r4, ALU.mult, ALU.mult), reads=[B_bank[ssb], B_rs[3]], writes=[B_rs[2]])
                P.op("dve", f_ts(r3, r3, RMS_EPS, None, ALU.add), reads=[B_rs[2]], writes=[B_rs[2]])
                P.op("act", f_act(r3, r3, AF.Sqrt), reads=[B_rs[2]], writes=[B_rs[2]])
                P.op("dve", f_recip(r3, r3), reads=[B_rs[2]], writes=[B_rs[2]])
                P.op("dve", f_tt(r3, r3, r2, ALU.mult), reads=[B_rs[2], B_rs[1]], writes=[B_rs[2]])
                for g in range(4):
                    P.op("dve", f_tt(mf[:, 4 * g:4 * g + 4, 0:n], mf[:, 4 * g:4 * g + 4, 0:n], rs_t[:, 2:3, 0:n].broadcast_to([128, 4, n]), ALU.mult),
                         reads=[B_mfg[g], B_rs[2]], writes=[B_mfg[g]])
                for c in range(NCH):
                    P.op("dve", f_stt(xblk[:, c, 0:n], mf[:, c, 0:n], gcol(l, 3, c), xblk[:, c, 0:n], ALU.mult, ALU.add),
                         reads=[B_xg[c // 4], B_mfg[c // 4], B_par], writes=[B_xg[c // 4]])
                P.op("pool", f_dma(xdst[:, :, t0:t0 + n].rearrange("c p t -> p c t"), xblk[:, :, 0:n]),
                     reads=B_xg, writes=gr(Gdst, t0, t0 + n), dsem=S_xst)

        queue_conversions(0)
        pump_conv(20)
        for l in range(NL):
            if stop == "conv":
                break
            phase1(l)
            if l + 1 < NL:
                queue_conversions(l + 1)
            if stop == "p1":
                break
            phase2(l)
            if stop == "p2":
                break
            phase3(l)
        pump_conv(10 ** 6)
        final = []
        for g in G_x["yout"]:
            final.extend(g.writers.items())
        for S in [S_us, S_qs, S_ks, S_vs, S_xst, S_misc] + conv_sems:
            final.append((S, S.count))
        P.wait_only("pool", final)
        P.wait_only("sync", final)

        with nc.Block() as block:
            @block.tensor
            def _(e):
                P.replay("pe", e)

            @block.scalar
            def _(e):
                P.replay("act", e)

            @block.vector
            def _(e):
                P.replay("dve", e)

            @block.gpsimd
            def _(e):
                P.replay("pool", e)

            @block.sync
            def _(e):
                P.replay("sync", e)
    info = dict(RIN=RIN, ROUT=ROUT, TIN=TIN, TOUT=TOUT, nops={k: len(v["ops"]) for k, v in P.eng.items()})
    return nc, info


def _bias_table(rpb, flipped):
    rows = SEQ // GW

    def true_rc(lr, lc):
        if flipped:
            return rows - 1 - lr, GW - 1 - lc
        return lr, lc
    tiles = [(0, 2 * j) for j in range(4)] + [(2, 2 * j) for j in range(4)] + [(20, 2 * j) for j in range(8, 13)]
    qi = np.arange(128)
    out = np.full((DEPTH, NHEAD, 128, NBT, 128), NEG, np.float32)
    for t, (qr0, kr0) in enumerate(tiles):
        qlr = qr0 + qi // GW
        qlc = qi % GW
        klr = kr0 + qi // GW
        klc = qi % GW
        qr, qc = true_rc(qlr, qlc)
        kr, kc = true_rc(klr, klc)
        rs = np.clip(qr - 4, 0, rows - 8)
        cs = np.clip(qc - 8, 0, GW - 16)
        valid = (kr[None, :] >= rs[:, None]) & (kr[None, :] < rs[:, None] + 8) & \
                (kc[None, :] >= cs[:, None]) & (kc[None, :] < cs[:, None] + 16)
        dr = np.clip(kr[None, :] - qr[:, None] + 7, 0, 14)
        dc = np.clip(kc[None, :] - qc[:, None], -15, 15) + 15
        g = rpb[:, :, dr, dc]
        out[:, :, :, t, :] = np.where(valid[None, None], g, np.float32(NEG))
    return out


_CACHE = {}


def kernel(x, w_in, w_dw, b_dw, conv_ln_g, conv_ln_b, rpb, w_out, w_up, w_down,
           pre_mix_g, post_mix_g, pre_mlp_g, post_mlp_g, _NL=DEPTH, _dump=(), _trace=False, _stop=None, _cores=8):
    NL = _NL
    f32 = np.float32
    x = np.asarray(x, f32)
    key = (NL, tuple(_dump), _stop)
    if key not in _CACHE:
        _CACHE[key] = build_program(NL, _dump, _stop)
    nc, info = _CACHE[key]
    T0 = info["TIN"][0]
    R0 = info["RIN"][0]
    w_in = np.ascontiguousarray(w_in, f32)
    w_out = np.ascontiguousarray(w_out, f32)
    w_up = np.ascontiguousarray(w_up, f32)
    w_down = np.ascontiguousarray(w_down, f32)
    g4 = np.stack([np.asarray(a, f32) for a in (pre_mix_g, post_mix_g, pre_mlp_g, post_mlp_g)], axis=1)
    gpack = np.ascontiguousarray(g4.reshape(DEPTH, 4, NCH, 128).transpose(3, 0, 1, 2)).reshape(128, -1)
    rpb = np.asarray(rpb, f32)
    packs = {}
    for flipped in (False, True):
        wd = np.asarray(w_dw, f32)
        if flipped:
            wd = wd[:, ::-1, :]
        cp = np.concatenate([wd.transpose(0, 2, 1),
                             np.asarray(b_dw, f32)[:, :, None],
                             np.asarray(conv_ln_g, f32)[:, :, None],
                             np.asarray(conv_ln_b, f32)[:, :, None]], axis=2)
        cp = cp.reshape(DEPTH, 8, 128, 34).transpose(2, 0, 1, 3)
        packs[flipped] = (np.ascontiguousarray(cp).reshape(128, -1), _bias_table(rpb, flipped))
    in_maps = []
    for c in range(_cores):
        b, half = divmod(c, 2)
        if half == 0:
            xt = x[b, 0:T0, :]
        else:
            xt = x[b, SEQ - T0:SEQ, :][::-1]
        xT = np.ascontiguousarray(xt.T).reshape(NCH, 128, T0)
        cpk, bt = packs[half == 1]
        in_maps.append({"xin": xT, "w_in": w_in, "w_out": w_out, "w_up": w_up, "w_down": w_down,
                        "gpack": gpack, "cpack": cpk, "biasg": bt})
    if _trace:
        res = run_bass_kernel_spmd(nc, in_maps, core_ids=list(range(_cores)), trace=True)
    else:
        res = run_bass_kernel_spmd(nc, in_maps, core_ids=list(range(_cores)))
    _CACHE["last"] = res
    out = np.zeros((4, SEQ, DM), f32)
    for c in range(_cores):
        b, half = divmod(c, 2)
        y = np.asarray(res.results[c]["yout"], f32).reshape(DM, 2048).T
        if half == 0:
            out[b, 0:2048] = y
        else:
            out[b, 2048:SEQ] = y[::-1]
    return out
```

```python
import numpy as np
from contextlib import ExitStack
import concourse.bass as bass
import concourse.mybir as mybir
from concourse.bass_utils import run_bass_kernel_spmd

F32 = mybir.dt.float32
BF16 = mybir.dt.bfloat16
I32 = mybir.dt.int32
AF = mybir.ActivationFunctionType
ALU = mybir.AluOpType

DEPTH = 4
DM = 2048
NCH = 16
SEQ = 4096
GW = 64
CW = 1024
NHEAD = 16
HD = 64
KCONV = 31
DFF = 8192
NBT = 13
RMS_EPS = 1e-6
LN_EPS = 1e-5
NEG = -30000.0
NS = 3
NTILE = 46


class Sem:
    def __init__(self, h):
        self.h = h
        self.count = 0


class Buf:
    def __init__(self, name):
        self.name = name
        self.writers = {}
        self.readers = {}
        self.war = {}
        self.aliases = []


def alias(*bufs):
    for a in bufs:
        for b in bufs:
            if a is not b and b not in a.aliases:
                a.aliases.append(b)


class Prog:
    def __init__(self, nc, sems):
        self.nc = nc
        self.eng = {}
        for name in ("pe", "act", "dve", "pool", "sync"):
            self.eng[name] = dict(ops=[], waited={}, sem=sems[name])

    def op(self, eng, fn, reads=(), writes=(), deps=(), dsem=None):
        E = self.eng[eng]
        rset = set(id(b) for b in reads)
        wset = set(id(b) for b in writes)
        hs = []
        for b in reads:
            hs.extend(b.writers.items())
        for b in writes:
            if b.readers or id(b) in rset:
                b.war = dict(b.readers)
                for k, v in b.writers.items():
                    b.war[k] = max(b.war.get(k, 0), v)
                b.readers = {}
                b.writers = {}
            for bb in b.aliases:
                for d in (bb.readers, bb.writers):
                    for k, v in d.items():
                        b.war[k] = max(b.war.get(k, 0), v)
            hs.extend(b.war.items())
        hs.extend(deps)
        need = {}
        for sem, val in hs:
            if sem is E["sem"] and eng == "pe":
                continue
            need[sem] = max(need.get(sem, 0), val)
        waits = []
        em = E.setdefault("emitted", {})
        for sem, val in need.items():
            if em.get(sem, 0) < val:
                em[sem] = val
                waits.append((sem, val))
        if dsem is not None:
            dsem.count += 16
            h = (dsem, dsem.count)
            inc = (dsem, 16)
        else:
            E["sem"].count += 1
            h = (E["sem"], E["sem"].count)
            inc = (E["sem"], 1)
        E["ops"].append((waits, fn, inc))
        for b in reads:
            if id(b) not in wset:
                b.readers[h[0]] = max(b.readers.get(h[0], 0), h[1])
        for b in writes:
            b.writers[h[0]] = max(b.writers.get(h[0], 0), h[1])
        return h

    def wait_only(self, eng, deps):
        E = self.eng[eng]
        waits = []
        for sem, val in deps:
            if E.setdefault("emitted", {}).get(sem, 0) < val:
                E["emitted"][sem] = val
                waits.append((sem, val))
        E["ops"].append((waits, None, None))

    def replay(self, name, e):
        for waits, fn, inc in self.eng[name]["ops"]:
            for sem, val in waits:
                e.wait_ge(sem.h, val)
            if fn is not None:
                ins = fn(e)
                ins.then_inc(inc[0].h, inc[1])


def f_dma(out, in_):
    return lambda e: e.dma_start(out=out, in_=in_)


def f_act(out, in_, func, **kw):
    return lambda e: e.activation(out=out, in_=in_, func=func, **kw)


def f_ts(out, in0, s1, s2, op0, op1=None):
    if op1 is None:
        return lambda e: e.tensor_scalar(out=out, in0=in0, scalar1=s1, scalar2=None, op0=op0)
    return lambda e: e.tensor_scalar(out=out, in0=in0, scalar1=s1, scalar2=s2, op0=op0, op1=op1)


def f_tt(out, in0, in1, op):
    return lambda e: e.tensor_tensor(out=out, in0=in0, in1=in1, op=op)


def f_stt(out, in0, scalar, in1, op0, op1):
    return lambda e: e.scalar_tensor_tensor(out=out, in0=in0, scalar=scalar, in1=in1, op0=op0, op1=op1)


def f_copy(out, in_):
    return lambda e: e.tensor_copy(out=out, in_=in_)


def f_recip(out, in_):
    return lambda e: e.reciprocal(out=out, in_=in_)


def f_memset(ap, val):
    return lambda e: e.memset(ap, val)


def f_mm(items):
    def fn(e):
        ins = None
        for (out, lhsT, rhs, start, stop) in items:
            ins = e.matmul(out, lhsT=lhsT, rhs=rhs, start=start, stop=stop)
        return ins
    return fn


def f_tr(out, in_, ident):
    return lambda e: e.transpose(out, in_, ident)


def blocks_of(T):
    res = []
    t = 0
    while t < T:
        n = 512 if T - t >= 512 else T - t
        res.append((t, n))
        t += n
    return res


def build_program(NL=DEPTH, dump=(), stop=None):
    nc = bass.Bass("TRN2", target_bir_lowering=False)
    RIN = [32 + 4 * (NL - l) for l in range(NL)]
    ROUT = [32 + 4 * (NL - 1 - l) for l in range(NL)]
    TIN = [r * GW for r in RIN]
    TOUT = [r * GW for r in ROUT]
    T0 = TIN[0]
    T1 = TOUT[0]

    def dram(name, shape, dt, kind="Internal"):
        if name in dump:
            kind = "ExternalOutput"
        return nc.dram_tensor(name, shape, dt, kind=kind).ap()

    xin = dram("xin", [NCH, 128, T0], F32, "ExternalInput")
    w_in = dram("w_in", [DEPTH, DM, 5120], F32, "ExternalInput")
    w_out = dram("w_out", [DEPTH, DM, DM], F32, "ExternalInput")
    w_up = dram("w_up", [DEPTH, DM, DFF], F32, "ExternalInput")
    w_down = dram("w_down", [DEPTH, DFF, DM], F32, "ExternalInput")
    gpack_d = dram("gpack", [128, DEPTH * 4 * NCH], F32, "ExternalInput")
    cpack_d = dram("cpack", [128, DEPTH * 8 * 34], F32, "ExternalInput")
    bias_d = dram("biasg", [DEPTH, NHEAD, 128, NBT, 128], F32, "ExternalInput")
    yout = dram("yout", [NCH, 128, TOUT[NL - 1]], F32, "ExternalOutput")
    xs = [dram("xs0", [NCH, 128, T1], F32), dram("xs1", [NCH, 128, T1], F32)]
    wsc = dram("wsc", [2, NTILE, 128, 8192], BF16)
    uT = dram("uT", [8, 128, T0 + 32], BF16)
    qT = dram("qT", [8, 128, T0], BF16)
    kT = dram("kT", [8, 128, T0], BF16)
    vS = dram("vS", [T0 // 128, 128, NHEAD * 128], BF16)
    biasbf = dram("biasbf", [2, 8, 128, 2 * NBT * 128], BF16)
    mixT = dram("mixT", [NCH, 128, T1], BF16)

    es = ExitStack()
    with es:
        def sb(name, shape, dt):
            return es.enter_context(nc.sbuf_tensor(name, shape, dt))

        def newsem(name):
            return Sem(es.enter_context(nc.semaphore(name)))

        RW = sb("RW", [128, NS * 4096], F32)
        RX = sb("RX", [128, 8192], F32)
        RH = sb("RH", [128, 9216], F32)
        RS = sb("RS", [128, 10240], F32)
        RD = sb("RD", [128, 4096], F32)
        sqb = sb("sqb", [128, 2, 4, 512], BF16)
        rs_t = sb("rs_t", [128, 4, 512], F32)
        sg_t = sb("sg_t", [128, 2, 512], F32)
        ident = sb("ident", [128, 128], BF16)
        ones = sb("ones", [128, 128], BF16)
        iot = sb("iot", [128, 128], I32)
        zer = sb("zer", [128, 128], BF16)
        gpack = sb("gpack_s", [128, DEPTH, 4, NCH], F32)
        cpack = sb("cpack_s", [128, DEPTH, 8, 34], F32)
        psF = es.enter_context(nc.psum_tensor("psF", [128, 8 * 512], F32))
        onesAB = sb("onesAB", [128, 2, 128], BF16)

        def bfview(reg, off_w, n_w):
            return reg[:, off_w:off_w + n_w].bitcast(BF16)

        wslot = [bfview(RW, s * 4096, 4096) for s in range(NS)]
        B_w = [Buf(f"w{s}") for s in range(NS)]
        S_w = [newsem(f"sw{s}") for s in range(NS)]
        xblk = RX[:, :].rearrange("p (c n) -> p c n", c=NCH)
        B_xg = [Buf(f"x{g}") for g in range(4)]
        acc = RX[:, 0:4096].rearrange("p (c n) -> p c n", c=8)
        B_acc = [Buf(f"acc{c}") for c in range(8)]
        ublk = bfview(RX, 4096, 2176)[:, 0:8 * 542].rearrange("p (c n) -> p c n", c=8)
        B_u = Buf("ublk")
        S_u = newsem("su")
        tmpc = RX[:, 6272:6272 + 1024].rearrange("p (c n) -> p c n", c=2)
        B_tmpc = [Buf("tmpc0"), Buf("tmpc1")]
        for a in B_acc + B_tmpc + [B_u]:
            a.aliases = list(B_xg)
        for a in B_xg:
            a.aliases = B_acc + B_tmpc + [B_u]
        S_x = newsem("sx")
        hT = [bfview(RH, i * 4096, 4096).rearrange("p (c n) -> p c n", c=NCH) for i in range(2)]
        B_hT = [Buf("hT0"), Buf("hT1")]
        mixb = hT[0]
        B_mix = B_hT[0]
        S_mix = newsem("smix")
        h2 = hT[1]
        B_h2 = B_hT[1]
        RHb = RH[:, :].bitcast(BF16)
        o = 0
        kblk = []
        for i in range(2):
            kblk.append(RHb[:, o:o + 1024]); o += 1024
        vblk = []
        for i in range(2):
            vblk.append(RHb[:, o:o + 2048].rearrange("p (j h d) -> p j h d", j=8, h=2)); o += 2048
        qA = []
        qB = []
        for i in range(2):
            qA.append(RHb[:, o:o + 512]); o += 512
            qB.append(RHb[:, o:o + 512]); o += 512
        biasb = []
        for i in range(2):
            biasb.append(RHb[:, o:o + 2 * NBT * 128].rearrange("p (h t k) -> p h t k", h=2, t=NBT)); o += 2 * NBT * 128
        pT = []
        for i in range(4):
            pT.append(RHb[:, o:o + 640]); o += 640
        assert o <= 18432, o
        B_k = [Buf("k0"), Buf("k1")]
        B_v = [Buf("v0"), Buf("v1")]
        B_qA = [Buf("qA0"), Buf("qA1")]
        B_qB = [Buf("qB0"), Buf("qB1")]
        B_bias = [Buf("bias0"), Buf("bias1")]
        B_pT = [Buf(f"pT{i}") for i in range(4)]
        S_k = [newsem("sk0"), newsem("sk1")]
        S_v = [newsem("sv0"), newsem("sv1")]
        S_qA = [newsem("sqa0"), newsem("sqa1")]
        S_qB = [newsem("sqb0"), newsem("sqb1")]
        S_bias = [newsem("sbi0"), newsem("sbi1")]
        attn_bufs = B_k + B_v + B_qA + B_qB + B_bias + B_pT
        for a in attn_bufs:
            a.aliases = list(B_hT)
        for a in B_hT:
            a.aliases = list(attn_bufs)
        RSb = RS[:, :].bitcast(BF16)
        ustage = RSb[:, 0:4096].rearrange("p (c n) -> p c n", c=8)
        qstage = RSb[:, 4096:8192].rearrange("p (c n) -> p c n", c=8)
        kstage = RSb[:, 8192:12288].rearrange("p (c n) -> p c n", c=8)
        vstage = RSb[:, 12288:12288 + 8192].rearrange("p (t hp two d) -> p t hp two d", t=4, hp=8, two=2)
        mf = RS[:, 0:8192].rearrange("p (c n) -> p c n", c=NCH)
        B_us, B_qs, B_ks, B_vs = Buf("ustage"), Buf("qstage"), Buf("kstage"), Buf("vstage")
        S_us, S_qs, S_ks, S_vs = newsem("sus"), newsem("sqs"), newsem("sks"), newsem("svs")
        B_mfg = [Buf(f"mf{g}") for g in range(4)]
        for a in (B_us, B_qs, B_ks, B_vs):
            a.aliases = list(B_mfg)
        for a in B_mfg:
            a.aliases = [B_us, B_qs, B_ks, B_vs]
        S_xst = newsem("sxst")
        hid = RD[:, :].bitcast(BF16).rearrange("p (c n) -> p c n", c=16)
        B_hid = Buf("hid")
        RDb = RD[:, :].bitcast(BF16)
        dg = [RDb[:, i * 3968:(i + 1) * 3968].rearrange("p (j k) -> p j k", j=KCONV) for i in range(2)]
        B_dg = [Buf("dg0"), Buf("dg1")]
        for a in B_dg:
            a.aliases = [B_hid]
        B_hid.aliases = list(B_dg)
        B_sq = [Buf("sq0"), Buf("sq1")]
        B_rs = [Buf(f"rs{i}") for i in range(4)]
        B_sg = [Buf("sg0"), Buf("sg1")]
        B_const = Buf("const")
        B_par = Buf("params")
        S_par = newsem("spar")
        S_misc = newsem("smisc")
        def bank(b):
            return psF[:, b * 512:(b + 1) * 512]
        B_bank = [Buf(f"bank{b}") for b in range(8)]
        rot = [0]

        def next_bank():
            b = rot[0]
            rot[0] = (b + 1) % 8
            return b

        sems = {n: newsem("e_" + n) for n in ("pe", "act", "dve", "pool", "sync")}
        P = Prog(nc, sems)
        conv_sems = [newsem(f"cv{i}") for i in range(8)]
        conv_hist = []

        def rot_dma(dst, src, reads=(), writes=()):
            i = len(conv_hist)
            cs = conv_sems[i % 8]
            deps = [conv_hist[i - 8]] if i >= 8 else []
            h = P.op("pool", f_dma(dst, src), reads=reads, writes=writes, deps=deps, dsem=cs)
            conv_hist.append(h)
            return h

        def gran(name, n):
            return [Buf(f"{name}_g{i}") for i in range(n)]
        NG = T0 // 256
        G_x = {"xin": gran("xin", NG), 0: gran("xs0", NG), 1: gran("xs1", NG), "yout": gran("yout", NG)}
        G_u = gran("uT", NG + 1)
        G_q = gran("qT", NG)
        G_k = gran("kT", NG)
        G_v = gran("vS", NG)
        G_myc = gran("myc", NG)
        G_mya = gran("mya", NG)
        B_bbf = [[Buf(f"bbf{p}_{h}") for h in range(8)] for p in range(2)]
        B_wsc = [[[Buf(f"wsc{p}_{t}_{h}") for h in range(2)] for t in range(NTILE)] for p in range(2)]

        def gr(G, lo, hi):
            return G[lo // 256:(hi + 255) // 256]

        P.op("pool", lambda e: e.iota(iot[:], pattern=[[1, 128]], base=0, channel_multiplier=-1), writes=[B_const])
        P.op("dve", f_ts(ident[:], iot[:], 0.0, None, ALU.is_equal), reads=[B_const], writes=[B_const])
        P.op("dve", f_memset(ones[:], 1.0), writes=[B_const])
        P.op("dve", f_memset(zer[:], 0.0), writes=[B_const])
        B_oab = Buf("onesAB")
        P.op("dve", f_memset(onesAB[:], 0.0), writes=[B_oab])
        P.op("dve", f_memset(onesAB[0:128, 0, 0:64], 1.0), reads=[B_oab], writes=[B_oab])
        P.op("dve", f_memset(onesAB[0:128, 1, 64:128], 1.0), reads=[B_oab], writes=[B_oab])
        P.op("sync", f_dma(gpack[:].rearrange("p a b c -> p (a b c)"), gpack_d), writes=[B_par], dsem=S_par)
        P.op("sync", f_dma(cpack[:].rearrange("p a b c -> p (a b c)"), cpack_d), writes=[B_par], dsem=S_par)
        for c in range(8):
            rot_dma(uT[c, :, 0:15], zer[:, 0:15], reads=[B_const], writes=[G_u[0]])

        def tile_src(l, tid, half):
            par = l % 2
            dst = wsc[par, tid].rearrange("p (k n) -> p k n", k=16)
            k0, k1 = half * 8, half * 8 + 8
            res = []
            if tid < 4:
                for (cb, d0) in ((2 * tid * 128, 0), (1024 + 2 * tid * 128, 256)):
                    src = w_in[l, k0 * 128:k1 * 128, cb:cb + 256].rearrange("(k p) n -> p k n", p=128)
                    res.append((dst[:, k0:k1, d0:d0 + 256], src))
            elif tid < 10:
                cb = 2048 + (tid - 4) * 512
                src = w_in[l, k0 * 128:k1 * 128, cb:cb + 512].rearrange("(k p) n -> p k n", p=128)
                res.append((dst[:, k0:k1, :], src))
            elif tid < 14:
                cb = (tid - 10) * 512
                src = w_out[l, k0 * 128:k1 * 128, cb:cb + 512].rearrange("(k p) n -> p k n", p=128)
                res.append((dst[:, k0:k1, :], src))
            else:
                qd, r = divmod(tid - 14, 8)
                if r < 4:
                    cb = qd * 2048 + r * 512
                    src = w_up[l, k0 * 128:k1 * 128, cb:cb + 512].rearrange("(k p) n -> p k n", p=128)
                else:
                    cb = (r - 4) * 512
                    rb = qd * 2048
                    src = w_down[l, rb + k0 * 128:rb + k1 * 128, cb:cb + 512].rearrange("(k p) n -> p k n", p=128)
                res.append((dst[:, k0:k1, :], src))
            return res

        pending_conv = []

        def queue_conversions(l):
            for tid in range(NTILE):
                for half in range(2):
                    pending_conv.append((l, tid, half))

        def pump_conv(k):
            for _ in range(k):
                if not pending_conv:
                    return
                l, tid, half = pending_conv.pop(0)
                for (dst, src) in tile_src(l, tid, half):
                    rot_dma(dst, src, writes=[B_wsc[l % 2][tid][half]])

        wtiles = []
        for l in range(NL):
            for _ in blocks_of(TIN[l]):
                wtiles.extend((l, t) for t in range(10))
            for _ in blocks_of(TOUT[l]):
                wtiles.extend((l, t) for t in range(10, NTILE))
        wstate = dict(next_load=0, next_use=0)

        def w_load_one():
            i = wstate["next_load"]
            if i >= len(wtiles):
                return
            l, tid = wtiles[i]
            while any(pc[0] == l and pc[1] == tid for pc in pending_conv):
                pump_conv(1)
            s = i % NS
            P.op("sync", f_dma(wslot[s], wsc[l % 2, tid]), reads=B_wsc[l % 2][tid], writes=[B_w[s]], dsem=S_w[s])
            wstate["next_load"] = i + 1

        def w_next(expect):
            i = wstate["next_use"]
            assert wtiles[i] == expect, (wtiles[i], expect)
            while wstate["next_load"] < min(i + NS, len(wtiles)):
                if wstate["next_load"] >= i + NS - 1 and i >= 1:
                    pass
                w_load_one()
            wstate["next_use"] = i + 1
            pump_conv(1)
            s = i % NS
            return wslot[s].rearrange("p (k n) -> p k n", k=16), B_w[s]

        def rstd_from(bank_b, n, ridx, scale, eps):
            r = rs_t[:, ridx, 0:n]
            P.op("dve", f_ts(r, bank(bank_b)[:, 0:n], scale, eps, ALU.mult, ALU.add), reads=[B_bank[bank_b]], writes=[B_rs[ridx]])
            P.op("act", f_act(r, r, AF.Sqrt), reads=[B_rs[ridx]], writes=[B_rs[ridx]])
            P.op("dve", f_recip(r, r), reads=[B_rs[ridx]], writes=[B_rs[ridx]])

        def sumsq_of(src3, B_src, n, nchunks):
            b = next_bank()
            ng = nchunks // 4
            for g in range(ng):
                sq = sqb[:, g % 2, :, 0:n]
                P.op("act", f_act(sq, src3[:, 4 * g:4 * g + 4, 0:n], AF.Square), reads=[B_src[g]], writes=[B_sq[g % 2]])
                items = [(bank(b)[:, 0:n], ones[:], sqb[:, g % 2, j, 0:n], (g == 0 and j == 0), (g == ng - 1 and j == 3)) for j in range(4)]
                P.op("pe", f_mm(items), reads=[B_sq[g % 2], B_const], writes=[B_bank[b]])
            return b

        def gcol(l, k, c):
            return gpack[:, l, k, c:c + 1]

        def phase1(l):
            xsrc, Gsrc = (xin, G_x["xin"]) if l == 0 else (xs[(l - 1) % 2], G_x[(l - 1) % 2])
            P.op("dve", f_memset(RSb[:, 12288:12288 + 8192], 0.0), writes=[B_vs])
            for hp in range(8):
                rot_dma(biasbf[l % 2, hp].rearrange("q (h t k) -> q h t k", h=2, t=NBT),
                        bias_d[l, 2 * hp:2 * hp + 2].rearrange("h q t k -> q h t k"), writes=[B_bbf[l % 2][hp]])
            blks = blocks_of(TIN[l])

            def load_block(bi):
                t0, n = blks[bi]
                P.op("sync", f_dma(xblk[:, :, 0:n], xsrc[:, :, t0:t0 + n].rearrange("c p t -> p c t")),
                     reads=gr(Gsrc, t0, t0 + n), writes=B_xg, dsem=S_x)

            def norm_block(bi):
                t0, n = blks[bi]
                par = bi % 2
                b = sumsq_of(xblk, B_xg, n, NCH)
                rstd_from(b, n, 0, 1.0 / DM, RMS_EPS)
                for c in range(NCH):
                    P.op("dve", f_stt(hT[par][:, c, 0:n], xblk[:, c, 0:n], gcol(l, 0, c), rs_t[:, 0, 0:n], ALU.mult, ALU.mult),
                         reads=[B_xg[c // 4], B_rs[0], B_par], writes=[B_hT[par]])
                if bi + 1 < len(blks):
                    load_block(bi + 1)

            load_block(0)
            norm_block(0)
            for bi, (t0, n) in enumerate(blks):
                par = bi % 2
                nt = n // 128
                for w in range(10):
                    wt, Bw = w_next((l, w))
                    if w == 5 and bi + 1 < len(blks):
                        norm_block(bi + 1)
                    if w < 8:
                        bs = [next_bank() for _ in range(4)]
                        for cc in range(4):
                            items = [(bank(bs[cc])[:, 0:n], wt[:, kc, cc * 128:(cc + 1) * 128], hT[par][:, kc, 0:n], kc == 0, kc == 15) for kc in range(16)]
                            P.op("pe", f_mm(items), reads=[Bw, B_hT[par]], writes=[B_bank[bs[cc]]])
                        if w < 4:
                            for j in range(2):
                                P.op("act", f_act(sg_t[:, j, 0:n], bank(bs[2 + j])[:, 0:n], AF.Sigmoid), reads=[B_bank[bs[2 + j]]], writes=[B_sg[j]])
                                P.op("dve", f_tt(ustage[:, 2 * w + j, 0:n], bank(bs[j])[:, 0:n], sg_t[:, j, 0:n], ALU.mult),
                                     reads=[B_bank[bs[j]], B_sg[j]], writes=[B_us])
                        else:
                            stg, Bs = (qstage, B_qs) if w < 6 else (kstage, B_ks)
                            sc = 0.125 if w < 6 else 1.0
                            for cc in range(4):
                                ch = (w % 2) * 4 + cc
                                if cc % 2 == 0:
                                    P.op("act", f_act(stg[:, ch, 0:n], bank(bs[cc])[:, 0:n], AF.Copy, scale=sc), reads=[B_bank[bs[cc]]], writes=[Bs])
                                else:
                                    P.op("dve", f_ts(stg[:, ch, 0:n], bank(bs[cc])[:, 0:n], sc, None, ALU.mult), reads=[B_bank[bs[cc]]], writes=[Bs])
                    else:
                        hb = (w - 8) * 8
                        for tt in range(nt):
                            b2 = next_bank()
                            items = [(bank(b2)[:, :], hT[par][:, kc, tt * 128:(tt + 1) * 128], wt[:, kc, :], kc == 0, kc == 15) for kc in range(16)]
                            P.op("pe", f_mm(items), reads=[Bw, B_hT[par]], writes=[B_bank[b2]])
                            src = bank(b2)[:, :].rearrange("p (hp two d) -> p hp two d", two=2, d=64)
                            hp0 = hb // 2
                            if tt % 2 == 0:
                                P.op("act", f_act(vstage[:, tt, hp0:hp0 + 4, 0, 0:64], src[:, :, 0, :], AF.Copy), reads=[B_bank[b2]], writes=[B_vs])
                                P.op("act", f_act(vstage[:, tt, hp0:hp0 + 4, 1, 64:128], src[:, :, 1, :], AF.Copy), reads=[B_bank[b2]], writes=[B_vs])
                            else:
                                P.op("dve", f_copy(vstage[:, tt, hp0:hp0 + 4, 0, 0:64], src[:, :, 0, :]), reads=[B_bank[b2]], writes=[B_vs])
                                P.op("dve", f_copy(vstage[:, tt, hp0:hp0 + 4, 1, 64:128], src[:, :, 1, :]), reads=[B_bank[b2]], writes=[B_vs])
                    if w == 3:
                        P.op("pool", f_dma(uT[:, :, 15 + t0:15 + t0 + n].rearrange("c p t -> p c t"), ustage[:, :, 0:n]),
                             reads=[B_us], writes=gr(G_u, t0, t0 + n), dsem=S_us)
                    if w == 5:
                        P.op("pool", f_dma(qT[:, :, t0:t0 + n].rearrange("c p t -> p c t"), qstage[:, :, 0:n]),
                             reads=[B_qs], writes=gr(G_q, t0, t0 + n), dsem=S_qs)
                    if w == 7:
                        P.op("pool", f_dma(kT[:, :, t0:t0 + n].rearrange("c p t -> p c t"), kstage[:, :, 0:n]),
                             reads=[B_ks], writes=gr(G_k, t0, t0 + n), dsem=S_ks)
                    if w == 9:
                        P.op("pool", f_dma(vS[t0 // 128:t0 // 128 + nt].rearrange("t p f -> p t f"),
                                           vstage[:, 0:nt].rearrange("p t hp two d -> p t (hp two d)")),
                             reads=[B_vs], writes=gr(G_v, t0, t0 + n), dsem=S_vs)

        def conv_gen(l, t0, n):
            P.op("sync", f_dma(ublk[:, :, 0:n + 30], uT[:, :, t0:t0 + n + 30].rearrange("c p t -> p c t")),
                 reads=gr(G_u, t0, t0 + n + 30), writes=[B_u], dsem=S_u)
            yield
            for c in range(8):
                dp = c % 2
                P.op("dve", f_tt(dg[dp][:, :, :], ident[:, :].unsqueeze(1).broadcast_to([128, KCONV, 128]),
                                 cpack[:, l, c, 0:KCONV].unsqueeze(2).broadcast_to([128, KCONV, 128]), ALU.mult),
                     reads=[B_const, B_par], writes=[B_dg[dp]])
                yield
                cbk = 5 + dp
                items = [(bank(cbk)[:, 0:n], dg[dp][:, j, :], ublk[:, c, j:j + n], j == 0, j == KCONV - 1) for j in range(KCONV)]
                P.op("pe", f_mm(items), reads=[B_dg[dp], B_u], writes=[B_bank[cbk]])
                P.op("act", f_act(acc[:, c, 0:n], bank(cbk)[:, 0:n], AF.Identity, bias=cpack[:, l, c, 31:32]),
                     reads=[B_bank[cbk], B_par], writes=[B_acc[c]])
                yield
            for g in range(2):
                cb = sqb[:, 0, :, 0:n]
                P.op("act", f_act(cb, acc[:, 4 * g:4 * g + 4, 0:n], AF.Copy), reads=B_acc[4 * g:4 * g + 4], writes=[B_sq[0]])
                items = [(bank(5)[:, 0:n], ones[:], sqb[:, 0, j, 0:n], (g == 0 and j == 0), (g == 1 and j == 3)) for j in range(4)]
                P.op("pe", f_mm(items), reads=[B_sq[0], B_const], writes=[B_bank[5]])
                sq = sqb[:, 1, :, 0:n]
                P.op("act", f_act(sq, acc[:, 4 * g:4 * g + 4, 0:n], AF.Square), reads=B_acc[4 * g:4 * g + 4], writes=[B_sq[1]])
                items = [(bank(6)[:, 0:n], ones[:], sqb[:, 1, j, 0:n], (g == 0 and j == 0), (g == 1 and j == 3)) for j in range(4)]
                P.op("pe", f_mm(items), reads=[B_sq[1], B_const], writes=[B_bank[6]])
                yield
            mean = rs_t[:, 1, 0:n]
            msq = rs_t[:, 2, 0:n]
            rst = rs_t[:, 3, 0:n]
            P.op("dve", f_ts(mean, bank(5)[:, 0:n], 1.0 / CW, None, ALU.mult), reads=[B_bank[5]], writes=[B_rs[1]])
            P.op("dve", f_tt(msq, mean, mean, ALU.mult), reads=[B_rs[1]], writes=[B_rs[2]])
            P.op("dve", f_stt(rst, bank(6)[:, 0:n], 1.0 / CW, msq, ALU.mult, ALU.subtract), reads=[B_bank[6], B_rs[2]], writes=[B_rs[3]])
            P.op("dve", f_ts(rst, rst, LN_EPS, None, ALU.add), reads=[B_rs[3]], writes=[B_rs[3]])
            P.op("act", f_act(rst, rst, AF.Sqrt), reads=[B_rs[3]], writes=[B_rs[3]])
            P.op("dve", f_recip(rst, rst), reads=[B_rs[3]], writes=[B_rs[3]])
            yield
            for c in range(8):
                tb = c % 2
                P.op("dve", f_tt(tmpc[:, tb, 0:n], acc[:, c, 0:n], mean, ALU.subtract), reads=[B_acc[c], B_rs[1]], writes=[B_tmpc[tb]])
                if c >= 1:
                    cp = c - 1
                    P.op("dve", f_tt(acc[:, cp, 0:n], tmpc[:, cp % 2, 0:n], rst, ALU.mult), reads=[B_tmpc[cp % 2], B_rs[3]], writes=[B_acc[cp]])
                    P.op("act", f_act(ustage[:, cp, 0:n], acc[:, cp, 0:n], AF.Silu, scale=cpack[:, l, cp, 32:33], bias=cpack[:, l, cp, 33:34]),
                         reads=[B_acc[cp], B_par], writes=[B_us])
                yield
            cp = 7
            P.op("dve", f_tt(acc[:, cp, 0:n], tmpc[:, cp % 2, 0:n], rst, ALU.mult), reads=[B_tmpc[cp % 2], B_rs[3]], writes=[B_acc[cp]])
            P.op("act", f_act(ustage[:, cp, 0:n], acc[:, cp, 0:n], AF.Silu, scale=cpack[:, l, cp, 32:33], bias=cpack[:, l, cp, 33:34]),
                 reads=[B_acc[cp], B_par], writes=[B_us])
            P.op("pool", f_dma(mixT[0:8, :, t0:t0 + n].rearrange("c p t -> p c t"), ustage[:, :, 0:n]),
                 reads=[B_us], writes=gr(G_myc, t0, t0 + n), dsem=S_us)
            yield

        def phase2(l):
            for i in range(2):
                P.op("dve", f_memset(qA[i][64:128, :], 0.0), writes=[B_qA[i]])
                P.op("dve", f_memset(qB[i][0:64, :], 0.0), writes=[B_qB[i]])
            step = [0]
            for (t0, n) in blocks_of(TOUT[l]):
                cg = conv_gen(l, t0, n)
                nq = n // 128
                i0 = t0 // 128
                jlo = max(0, i0 - 2)
                jhi = max(i0 + nq - 1 + 2, 3)
                nk = jhi - jlo + 1
                edge = (i0 == 0)
                tlo, ntl = (0, NBT) if edge else (8, 5)
                per_step = 40 // (8 * nq) + 2
                pend = []

                def flush_one():
                    (hp, qi, par, js, pp, odb) = pend.pop(0)
                    items = []
                    nj = len(js)
                    for head in range(2):
                        for jj, j in enumerate(js):
                            items.append((bank(odb)[:, 0:128], vblk[par][:, j - jlo, head, :], pT[pp + head][:, jj * 128:(jj + 1) * 128],
                                          head == 0 and jj == 0, head == 1 and jj == nj - 1))
                    for head in range(2):
                        for jj, j in enumerate(js):
                            items.append((bank(odb)[:, 128:256], onesAB[:, head, :], pT[pp + head][:, jj * 128:(jj + 1) * 128],
                                          head == 0 and jj == 0, head == 1 and jj == nj - 1))
                    P.op("pe", f_mm(items), reads=[B_pT[pp], B_pT[pp + 1], B_v[par], B_oab], writes=[B_bank[odb]])
                    rc = rs_t[:, 0, 0:128]
                    P.op("dve", f_recip(rc, bank(odb)[:, 128:256]), reads=[B_bank[odb]], writes=[B_rs[0]])
                    P.op("dve", f_tt(qstage[:, hp, qi * 128:(qi + 1) * 128], bank(odb)[:, 0:128], rc, ALU.mult),
                         reads=[B_bank[odb], B_rs[0]], writes=[B_qs])

                for hp in range(8):
                    par = hp % 2
                    P.op("sync", f_dma(kblk[par][:, 0:nk * 128], kT[hp, :, jlo * 128:(jlo + nk) * 128]),
                         reads=gr(G_k, jlo * 128, (jlo + nk) * 128), writes=[B_k[par]], dsem=S_k[par])
                    P.op("sync", f_dma(vblk[par][:, 0:nk].rearrange("p j h d -> p j (h d)"),
                                       vS[jlo:jlo + nk, :, hp * 256:(hp + 1) * 256].rearrange("j p f -> p j f")),
                         reads=gr(G_v, jlo * 128, (jlo + nk) * 128), writes=[B_v[par]], dsem=S_v[par])
                    P.op("sync", f_dma(qA[par][0:64, 0:n], qT[hp, 0:64, t0:t0 + n]), reads=gr(G_q, t0, t0 + n), writes=[B_qA[par]], dsem=S_qA[par])
                    P.op("sync", f_dma(qB[par][64:128, 0:n], qT[hp, 64:128, t0:t0 + n]), reads=gr(G_q, t0, t0 + n), writes=[B_qB[par]], dsem=S_qB[par])
                    P.op("sync", f_dma(biasb[par][:, :, 0:ntl, :],
                                       biasbf[l % 2, hp].rearrange("q (h t k) -> q h t k", h=2, t=NBT)[:, :, tlo:tlo + ntl, :]),
                         reads=[B_bbf[l % 2][hp]], writes=[B_bias[par]], dsem=S_bias[par])
                    pump_conv(1)
                    for qi in range(nq):
                        i = i0 + qi
                        if i < 2:
                            js = [0, 1, 2, 3]
                            tb = 0 if i == 0 else 4
                        else:
                            js = list(range(i - 2, i + 3))
                            tb = 8
                        tb -= tlo
                        sidx = step[0]
                        step[0] += 1
                        pp = 2 * (sidx % 2)
                        odb = 4 if sidx % 2 == 0 else 7
                        nkk = len(js) * 128
                        for head in range(2):
                            sb0 = 2 * head
                            qsel, Bq = (qA[par], B_qA[par]) if head == 0 else (qB[par], B_qB[par])
                            items = []
                            for jj, j in enumerate(js):
                                o_ap = psF[:, sb0 * 512 + jj * 128: sb0 * 512 + (jj + 1) * 128]
                                items.append((o_ap, kblk[par][:, (j - jlo) * 128:(j - jlo + 1) * 128], qsel[:, qi * 128:(qi + 1) * 128], True, False))
                                items.append((o_ap, biasb[par][:, head, tb + jj, :], ident[:], False, True))
                            P.op("pe", f_mm(items), reads=[B_k[par], Bq, B_bias[par], B_const], writes=[B_bank[sb0], B_bank[sb0 + 1]])
                            P.op("act", f_act(pT[pp + head][:, 0:min(nkk, 512)], psF[:, sb0 * 512:sb0 * 512 + min(nkk, 512)], AF.Exp),
                                 reads=[B_bank[sb0]], writes=[B_pT[pp + head]])
                            if nkk > 512:
                                P.op("act", f_act(pT[pp + head][:, 512:nkk], psF[:, sb0 * 512 + 512:sb0 * 512 + nkk], AF.Exp),
                                     reads=[B_bank[sb0 + 1]], writes=[B_pT[pp + head]])
                        if pend:
                            flush_one()
                        pend.append((hp, qi, par, js, pp, odb))
                        for _ in range(per_step):
                            next(cg, None)
                while pend:
                    flush_one()
                for _ in cg:
                    pass
                P.op("pool", f_dma(mixT[8:16, :, t0:t0 + n].rearrange("c p t -> p c t"), qstage[:, :, 0:n]),
                     reads=[B_qs], writes=gr(G_mya, t0, t0 + n), dsem=S_qs)

        def phase3(l):
            xsrc, Gsrc = (xin, G_x["xin"]) if l == 0 else (xs[(l - 1) % 2], G_x[(l - 1) % 2])
            if l == NL - 1:
                xdst, Gdst = yout, G_x["yout"]
            else:
                xdst, Gdst = xs[l % 2], G_x[l % 2]
            blks = blocks_of(TOUT[l])

            def load_mix(bi):
                t0, n = blks[bi]
                P.op("sync", f_dma(mixb[:, :, 0:n], mixT[:, :, t0:t0 + n].rearrange("c p t -> p c t")),
                     reads=gr(G_myc, t0, t0 + n) + gr(G_mya, t0, t0 + n), writes=[B_mix], dsem=S_mix)

            load_mix(0)
            for bi, (t0, n) in enumerate(blks):
                P.op("sync", f_dma(xblk[:, :, 0:n], xsrc[:, :, t0:t0 + n].rearrange("c p t -> p c t")),
                     reads=gr(Gsrc, t0, t0 + n), writes=B_xg, dsem=S_x)
                ssb = next_bank()
                for w in range(4):
                    wt, Bw = w_next((l, 10 + w))
                    for cc in range(4):
                        ch = w * 4 + cc
                        b = next_bank()
                        if b == ssb:
                            b = next_bank()
                        items = [(bank(b)[:, 0:n], wt[:, kc, cc * 128:(cc + 1) * 128], mixb[:, kc, 0:n], kc == 0, kc == 15) for kc in range(16)]
                        P.op("pe", f_mm(items), reads=[Bw, B_mix], writes=[B_bank[b]])
                        P.op("act", f_act(mf[:, ch, 0:n], bank(b)[:, 0:n], AF.Copy, scale=gcol(l, 1, ch)), reads=[B_bank[b], B_par], writes=[B_mfg[ch // 4]])
                        P.op("act", f_act(sqb[:, ch % 2, 0, 0:n], bank(b)[:, 0:n], AF.Square), reads=[B_bank[b]], writes=[B_sq[ch % 2]])
                        if ch >= 1:
                            cp = ch - 1
                            P.op("pe", f_mm([(bank(ssb)[:, 0:n], ones[:], sqb[:, cp % 2, 0, 0:n], cp == 0, cp == 15)]),
                                 reads=[B_sq[cp % 2], B_const], writes=[B_bank[ssb]])
                P.op("pe", f_mm([(bank(ssb)[:, 0:n], ones[:], sqb[:, 1, 0, 0:n], False, True)]),
                     reads=[B_sq[1], B_const], writes=[B_bank[ssb]])
                if bi + 1 < len(blks):
                    load_mix(bi + 1)
                rstd_from(ssb, n, 0, 1.0 / DM, RMS_EPS)
                for g in range(4):
                    P.op("dve", f_tt(mf[:, 4 * g:4 * g + 4, 0:n], mf[:, 4 * g:4 * g + 4, 0:n], rs_t[:, 0:1, 0:n].broadcast_to([128, 4, n]), ALU.mult),
                         reads=[B_mfg[g], B_rs[0]], writes=[B_mfg[g]])
                for g in range(4):
                    P.op("dve", f_tt(xblk[:, 4 * g:4 * g + 4, 0:n], xblk[:, 4 * g:4 * g + 4, 0:n], mf[:, 4 * g:4 * g + 4, 0:n], ALU.add),
                         reads=[B_xg[g], B_mfg[g]], writes=[B_xg[g]])
                for c in range(NCH):
                    P.op("dve", f_ts(h2[:, c, 0:n], xblk[:, c, 0:n], gcol(l, 2, c), None, ALU.mult),
                         reads=[B_xg[c // 4], B_par], writes=[B_h2])
                b = sumsq_of(xblk, B_xg, n, NCH)
                rstd_from(b, n, 1, 1.0 / DM, RMS_EPS)
                r2 = rs_t[:, 1, 0:n]
                r4 = rs_t[:, 3, 0:n]
                P.op("dve", f_tt(r2, r2, r2, ALU.mult), reads=[B_rs[1]], writes=[B_rs[1]])
                P.op("dve", f_tt(r4, r2, r2, ALU.mult), reads=[B_rs[1]], writes=[B_rs[3]])
                ssb = None
                for qd in range(4):
                    for r in range(4):
                        wt, Bw = w_next((l, 14 + 8 * qd + r))
                        for cc in range(4):
                            hc = r * 4 + cc
                            b = next_bank()
                            items = [(bank(b)[:, 0:n], wt[:, kc, cc * 128:(cc + 1) * 128], h2[:, kc, 0:n], kc == 0, kc == 15) for kc in range(16)]
                            P.op("pe", f_mm(items), reads=[Bw, B_h2], writes=[B_bank[b]])
                            j = hc % 2
                            if j == 0:
                                P.op("act", f_act(sg_t[:, 0, 0:n], bank(b)[:, 0:n], AF.Relu), reads=[B_bank[b]], writes=[B_sg[0]])
                                P.op("act", f_act(hid[:, hc, 0:n], sg_t[:, 0, 0:n], AF.Square), reads=[B_sg[0]], writes=[B_hid])
                            else:
                                P.op("dve", f_ts(sg_t[:, 1, 0:n], bank(b)[:, 0:n], 0.0, None, ALU.max), reads=[B_bank[b]], writes=[B_sg[1]])
                                P.op("dve", f_tt(hid[:, hc, 0:n], sg_t[:, 1, 0:n], sg_t[:, 1, 0:n], ALU.mult), reads=[B_sg[1]], writes=[B_hid])
                    if qd == 3:
                        ssb = next_bank()
                    for r in range(4):
                        wt, Bw = w_next((l, 14 + 8 * qd + 4 + r))
                        for cc in range(4):
                            ch = r * 4 + cc
                            b = next_bank()
                            if b == ssb:
                                b = next_bank()
                            items = [(bank(b)[:, 0:n], wt[:, kc, cc * 128:(cc + 1) * 128], hid[:, kc, 0:n], kc == 0, kc == 15) for kc in range(16)]
                            P.op("pe", f_mm(items), reads=[Bw, B_hid], writes=[B_bank[b]])
                            Bg = B_mfg[ch // 4]
                            if qd == 0:
                                P.op("act", f_act(mf[:, ch, 0:n], bank(b)[:, 0:n], AF.Copy), reads=[B_bank[b]], writes=[Bg])
                            else:
                                P.op("dve", f_tt(mf[:, ch, 0:n], bank(b)[:, 0:n], mf[:, ch, 0:n], ALU.add), reads=[B_bank[b], Bg], writes=[Bg])
                            if qd == 3:
                                P.op("act", f_act(sqb[:, ch % 2, 0, 0:n], mf[:, ch, 0:n], AF.Square), reads=[Bg], writes=[B_sq[ch % 2]])
                                if ch >= 1:
                                    cp = ch - 1
                                    P.op("pe", f_mm([(bank(ssb)[:, 0:n], ones[:], sqb[:, cp % 2, 0, 0:n], cp == 0, cp == 15)]),
                                         reads=[B_sq[cp % 2], B_const], writes=[B_bank[ssb]])
                P.op("pe", f_mm([(bank(ssb)[:, 0:n], ones[:], sqb[:, 1, 0, 0:n], False, True)]),
                     reads=[B_sq[1], B_const], writes=[B_bank[ssb]])
                r3 = rs_t[:, 2, 0:n]
                P.op("dve", f_stt(r3, bank(ssb)[:, 0:n], 1.0 / DM, r4, ALU.mult, ALU.mult), reads=[B_bank[ssb], B_rs[3]], writes=[B_rs[2]])
                P.op("dve", f_ts(r3, r3, RMS_EPS, None, ALU.add), reads=[B_rs[2]], writes=[B_rs[2]])
                P.op("act", f_act(r3, r3, AF.Sqrt), reads=[B_rs[2]], writes=[B_rs[2]])
                P.op("dve", f_recip(r3, r3), reads=[B_rs[2]], writes=[B_rs[2]])
                P.op("dve", f_tt(r3, r3, r2, ALU.mult), reads=[B_rs[2], B_rs[1]], writes=[B_rs[2]])
                for g in range(4):
                    P.op("dve", f_tt(mf[:, 4 * g:4 * g + 4, 0:n], mf[:, 4 * g:4 * g + 4, 0:n], rs_t[:, 2:3, 0:n].broadcast_to([128, 4, n]), ALU.mult),
                         reads=[B_mfg[g], B_rs[2]], writes=[B_mfg[g]])
                for c in range(NCH):
                    P.op("dve", f_stt(xblk[:, c, 0:n], mf[:, c, 0:n], gcol(l, 3, c), xblk[:, c, 0:n], ALU.mult, ALU.add),
                         reads=[B_xg[c // 4], B_mfg[c // 4], B_par], writes=[B_xg[c // 4]])
                P.op("pool", f_dma(xdst[:, :, t0:t0 + n].rearrange("c p t -> p c t"), xblk[:, :, 0:n]),
                     reads=B_xg, writes=gr(Gdst, t0, t0 + n), dsem=S_xst)

        queue_conversions(0)
        pump_conv(20)
        for l in range(NL):
            if stop == "conv":
                break
            phase1(l)
            if l + 1 < NL:
                queue_conversions(l + 1)
            if stop == "p1":
                break
            phase2(l)
            if stop == "p2":
                break
            phase3(l)
        pump_conv(10 ** 6)
        final = []
        for g in G_x["yout"]:
            final.extend(g.writers.items())
        for S in [S_us, S_qs, S_ks, S_vs, S_xst, S_misc] + conv_sems:
            final.append((S, S.count))
        P.wait_only("pool", final)
        P.wait_only("sync", final)

        with nc.Block() as block:
            @block.tensor
            def _(e):
                P.replay("pe", e)

            @block.scalar
            def _(e):
                P.replay("act", e)

            @block.vector
            def _(e):
                P.replay("dve", e)

            @block.gpsimd
            def _(e):
                P.replay("pool", e)

            @block.sync
            def _(e):
                P.replay("sync", e)
    info = dict(RIN=RIN, ROUT=ROUT, TIN=TIN, TOUT=TOUT, nops={k: len(v["ops"]) for k, v in P.eng.items()})
    return nc, info


def _bias_table(rpb, flipped):
    rows = SEQ // GW

    def true_rc(lr, lc):
        if flipped:
            return rows - 1 - lr, GW - 1 - lc
        return lr, lc
    tiles = [(0, 2 * j) for j in range(4)] + [(2, 2 * j) for j in range(4)] + [(20, 2 * j) for j in range(8, 13)]
    qi = np.arange(128)
    out = np.full((DEPTH, NHEAD, 128, NBT, 128), NEG, np.float32)
    for t, (qr0, kr0) in enumerate(tiles):
        qlr = qr0 + qi // GW
        qlc = qi % GW
        klr = kr0 + qi // GW
        klc = qi % GW
        qr, qc = true_rc(qlr, qlc)
        kr, kc = true_rc(klr, klc)
        rs = np.clip(qr - 4, 0, rows - 8)
        cs = np.clip(qc - 8, 0, GW - 16)
        valid = (kr[None, :] >= rs[:, None]) & (kr[None, :] < rs[:, None] + 8) & \
                (kc[None, :] >= cs[:, None]) & (kc[None, :] < cs[:, None] + 16)
        dr = np.clip(kr[None, :] - qr[:, None] + 7, 0, 14)
        dc = np.clip(kc[None, :] - qc[:, None], -15, 15) + 15
        g = rpb[:, :, dr, dc]
        out[:, :, :, t, :] = np.where(valid[None, None], g, np.float32(NEG))
    return out


_CACHE = {}


def kernel(x, w_in, w_dw, b_dw, conv_ln_g, conv_ln_b, rpb, w_out, w_up, w_down,
           pre_mix_g, post_mix_g, pre_mlp_g, post_mlp_g, _NL=DEPTH, _dump=(), _trace=False, _stop=None, _cores=8):
    NL = _NL
    f32 = np.float32
    x = np.asarray(x, f32)
    key = (NL, tuple(_dump), _stop)
    if key not in _CACHE:
        _CACHE[key] = build_program(NL, _dump, _stop)
    nc, info = _CACHE[key]
    T0 = info["TIN"][0]
    R0 = info["RIN"][0]
    w_in = np.ascontiguousarray(w_in, f32)
    w_out = np.ascontiguousarray(w_out, f32)
    w_up = np.ascontiguousarray(w_up, f32)
    w_down = np.ascontiguousarray(w_down, f32)
    g4 = np.stack([np.asarray(a, f32) for a in (pre_mix_g, post_mix_g, pre_mlp_g, post_mlp_g)], axis=1)
    gpack = np.ascontiguousarray(g4.reshape(DEPTH, 4, NCH, 128).transpose(3, 0, 1, 2)).reshape(128, -1)
    rpb = np.asarray(rpb, f32)
    packs = {}
    for flipped in (False, True):
        wd = np.asarray(w_dw, f32)
        if flipped:
            wd = wd[:, ::-1, :]
        cp = np.concatenate([wd.transpose(0, 2, 1),
                             np.asarray(b_dw, f32)[:, :, None],
                             np.asarray(conv_ln_g, f32)[:, :, None],
                             np.asarray(conv_ln_b, f32)[:, :, None]], axis=2)
        cp = cp.reshape(DEPTH, 8, 128, 34).transpose(2, 0, 1, 3)
        packs[flipped] = (np.ascontiguousarray(cp).reshape(128, -1), _bias_table(rpb, flipped))
    in_maps = []
    for c in range(_cores):
        b, half = divmod(c, 2)
        if half == 0:
            xt = x[b, 0:T0, :]
        else:
            xt = x[b, SEQ - T0:SEQ, :][::-1]
        xT = np.ascontiguousarray(xt.T).reshape(NCH, 128, T0)
        cpk, bt = packs[half == 1]
        in_maps.append({"xin": xT, "w_in": w_in, "w_out": w_out, "w_up": w_up, "w_down": w_down,
                        "gpack": gpack, "cpack": cpk, "biasg": bt})
    if _trace:
        res = run_bass_kernel_spmd(nc, in_maps, core_ids=list(range(_cores)), trace=True)
    else:
        res = run_bass_kernel_spmd(nc, in_maps, core_ids=list(range(_cores)))
    _CACHE["last"] = res
    out = np.zeros((4, SEQ, DM), f32)
    for c in range(_cores):
        b, half = divmod(c, 2)
        y = np.asarray(res.results[c]["yout"], f32).reshape(DM, 2048).T
        if half == 0:
            out[b, 0:2048] = y
        else:
            out[b, 2048:SEQ] = y[::-1]
    return out
```
